# Optimizing a Trainium2 kernel written in Bass

```python
import math
import jax, jax.numpy as jnp
from jax import lax
import numpy as np

D_MODEL = 2048
BATCH = 4
SEQ = 2048
DEPTH = 4

N_MIXERS = 4
N_LAYERS_A = (DEPTH + N_MIXERS - 1) // N_MIXERS
N_LAYERS_B = (DEPTH + N_MIXERS - 2) // N_MIXERS
N_LAYERS_C = (DEPTH + N_MIXERS - 3) // N_MIXERS
N_LAYERS_D = (DEPTH + N_MIXERS - 4) // N_MIXERS

MEM_LEN = 256
D_FF = 5632
RMS_EPS = 1e-6
BLOCK = 128

MEM_HEADS = 4
MEM_HEAD_DIM = 128
MEM_WIDTH = MEM_HEADS * MEM_HEAD_DIM

CONV_WIDTH = D_MODEL
CONV_K = 3
CONV_IN = 3 * CONV_WIDTH

MLA_HEADS = 16
MLA_Q_RANK = 512
MLA_KV_RANK = 512
MLA_NOPE = 128
MLA_ROPE = 64
MLA_V = 128
MLA_QK = MLA_NOPE + MLA_ROPE
MLA_IN = MLA_Q_RANK + MLA_KV_RANK + MLA_ROPE
ROPE_THETA = 10000.0

SWA_Q_HEADS = 32
SWA_KV_HEADS = 4
SWA_HEAD_DIM = 64
WINDOW = 128
SWA_IN = (SWA_Q_HEADS + 2 * SWA_KV_HEADS) * SWA_HEAD_DIM
REL_BUCKETS = 32
REL_MAX_DIST = 128

FOX_HEADS = 32
FOX_HEAD_DIM = 64
FOX_WIDTH = FOX_HEADS * FOX_HEAD_DIM
FOX_IN = 3 * FOX_WIDTH + FOX_HEADS

kernel_name = "hybrid_interleaved_macaron_trunk"


def rms_norm(x, g):
    xf = x.astype(jnp.float32)
    y = xf * lax.rsqrt(jnp.mean(xf * xf, axis=-1, keepdims=True) + RMS_EPS)
    return (y * g.astype(jnp.float32)).astype(x.dtype)


def swiglu(h, w_gate, w_up, w_down):
    return (jax.nn.silu(h @ w_gate) * (h @ w_up)) @ w_down


def rope(t, pos):
    half = t.shape[-1] // 2
    inv = ROPE_THETA ** (-jnp.arange(half, dtype=jnp.float32) / half)
    ang = pos.astype(jnp.float32)[:, None] * inv
    cos = jnp.cos(ang)[:, None, :]
    sin = jnp.sin(ang)[:, None, :]
    t1 = t[..., :half].astype(jnp.float32)
    t2 = t[..., half:].astype(jnp.float32)
    return jnp.concatenate([t1 * cos - t2 * sin, t1 * sin + t2 * cos], axis=-1).astype(t.dtype)


def t5_causal_bucket(dist):
    exact = REL_BUCKETS // 2
    d = np.maximum(dist, 0)
    log_b = exact + (np.log(np.maximum(d, 1) / exact) / np.log(REL_MAX_DIST / exact)
                     * (REL_BUCKETS - exact)).astype(np.int32)
    log_b = np.minimum(log_b, REL_BUCKETS - 1)
    return np.where(d < exact, d, log_b).astype(np.int32)


def causal_attention_blocks(q, k, v, scale, log_f_cum=None):
    b, s, h, _ = q.shape
    nblk = s // BLOCK
    k_pos = jnp.arange(s)
    c_bhs = None if log_f_cum is None else log_f_cum.astype(jnp.float32).transpose(0, 2, 1)

    def one_block(i):
        start = i * BLOCK
        qb = lax.dynamic_slice_in_dim(q, start, BLOCK, axis=1)
        logits = jnp.einsum('bqhd,bkhd->bhqk', qb, k, preferred_element_type=jnp.float32) * scale
        if c_bhs is not None:
            cq = lax.dynamic_slice_in_dim(c_bhs, start, BLOCK, axis=2)
            logits = logits + (cq[..., :, None] - c_bhs[..., None, :])
        q_pos = start + jnp.arange(BLOCK)
        mask = k_pos[None, :] <= q_pos[:, None]
        logits = jnp.where(mask, logits, -jnp.inf)
        p = jax.nn.softmax(logits, axis=-1).astype(v.dtype)
        return jnp.einsum('bhqk,bkhd->bqhd', p, v)

    out = lax.map(one_block, jnp.arange(nblk))
    return out.swapaxes(0, 1).reshape(b, s, h, v.shape[-1])


def sliding_window_attention(q, k, v, sinks, rel_bias):
    b, s, hq, d = q.shape
    hkv = k.shape[2]
    g = hq // hkv
    nblk = s // BLOCK
    qb = q.reshape(b, nblk, BLOCK, hkv, g, d)

    def band(t):
        tb = t.reshape(b, nblk, BLOCK, hkv, d)
        prev = jnp.pad(tb, ((0, 0), (1, 0), (0, 0), (0, 0), (0, 0)))[:, :-1]
        return jnp.concatenate([prev, tb], axis=2)

    kb, vb = band(k), band(v)
    logits = jnp.einsum('bnqhgd,bnkhd->bnhgqk', qb, kb,
                        preferred_element_type=jnp.float32) * (1.0 / math.sqrt(d))
    dist = np.arange(BLOCK)[:, None] + BLOCK - np.arange(2 * BLOCK)[None, :]
    in_window = (dist >= 0) & (dist < WINDOW)
    bias = rel_bias.astype(jnp.float32)[t5_causal_bucket(dist)]
    bias = bias.reshape(BLOCK, 2 * BLOCK, hkv, g).transpose(2, 3, 0, 1)
    key_valid = (np.arange(nblk)[:, None] * BLOCK - BLOCK + np.arange(2 * BLOCK)[None, :]) >= 0
    mask = in_window[None] & key_valid[:, None, :]
    logits = jnp.where(mask[None, :, None, None], logits + bias, -jnp.inf)
    sink = sinks.astype(jnp.float32).reshape(hkv, g)[None, None, :, :, None, None]
    sink = jnp.broadcast_to(sink, logits.shape[:-1] + (1,))
    probs = jax.nn.softmax(jnp.concatenate([logits, sink], axis=-1), axis=-1)[..., :-1]
    out = jnp.einsum('bnhgqk,bnkhd->bnqhgd', probs.astype(v.dtype), vb)
    return out.reshape(b, s, hq, d)


def memory_attention(mq, mk, mv):
    logits = jnp.einsum('bshd,bmhd->bhsm', mq, mk,
                        preferred_element_type=jnp.float32) * (1.0 / math.sqrt(MEM_HEAD_DIM))
    p = jax.nn.softmax(logits, axis=-1).astype(mv.dtype)
    return jnp.einsum('bhsm,bmhd->bshd', p, mv)


def short_conv_mixer(u, conv_w):
    s = u.shape[1]
    gb, gc, xt = jnp.split(u, 3, axis=-1)
    z = gc * xt
    zp = jnp.pad(z, ((0, 0), (CONV_K - 1, 0), (0, 0)))
    conv = zp[:, 0:s] * conv_w[0]
    for tap in range(1, CONV_K):
        conv = conv + zp[:, tap:tap + s] * conv_w[tap]
    return gb * conv


def mla_mixer(u, q_a_norm, w_q_b, kv_a_norm, w_kv_b, q_norm, k_norm):
    b, s, _ = u.shape
    q_lat = u[..., :MLA_Q_RANK]
    kv_lat = u[..., MLA_Q_RANK:MLA_Q_RANK + MLA_KV_RANK]
    k_rope = u[..., MLA_Q_RANK + MLA_KV_RANK:].reshape(b, s, 1, MLA_ROPE)
    q = (rms_norm(q_lat, q_a_norm) @ w_q_b).reshape(b, s, MLA_HEADS, MLA_QK)
    kv = (rms_norm(kv_lat, kv_a_norm) @ w_kv_b).reshape(b, s, MLA_HEADS, MLA_NOPE + MLA_V)
    k_nope, v = kv[..., :MLA_NOPE], kv[..., MLA_NOPE:]
    pos = jnp.arange(s)
    q_nope = rms_norm(q[..., :MLA_NOPE], q_norm[:MLA_NOPE])
    q_rot = rope(rms_norm(q[..., MLA_NOPE:], q_norm[MLA_NOPE:]), pos)
    k_nope = rms_norm(k_nope, k_norm[:MLA_NOPE])
    k_rot = rope(rms_norm(k_rope, k_norm[MLA_NOPE:]), pos)
    k_rot = jnp.broadcast_to(k_rot, (b, s, MLA_HEADS, MLA_ROPE))
    q_full = jnp.concatenate([q_nope, q_rot], axis=-1)
    k_full = jnp.concatenate([k_nope, k_rot], axis=-1)
    out = causal_attention_blocks(q_full, k_full, v, 1.0 / math.sqrt(MLA_QK))
    return out.reshape(b, s, MLA_HEADS * MLA_V)


def swa_mixer(u, q_norm, k_norm, sinks, rel_bias):
    b, s, _ = u.shape
    nq = SWA_Q_HEADS * SWA_HEAD_DIM
    nk = SWA_KV_HEADS * SWA_HEAD_DIM
    q = rms_norm(u[..., :nq].reshape(b, s, SWA_Q_HEADS, SWA_HEAD_DIM), q_norm)
    k = rms_norm(u[..., nq:nq + nk].reshape(b, s, SWA_KV_HEADS, SWA_HEAD_DIM), k_norm)
    v = u[..., nq + nk:].reshape(b, s, SWA_KV_HEADS, SWA_HEAD_DIM)
    out = sliding_window_attention(q, k, v, sinks, rel_bias)
    return out.reshape(b, s, nq)


def fox_mixer(u, b_f, q_norm, k_norm):
    b, s, _ = u.shape
    q = rms_norm(u[..., :FOX_WIDTH].reshape(b, s, FOX_HEADS, FOX_HEAD_DIM), q_norm)
    k = rms_norm(u[..., FOX_WIDTH:2 * FOX_WIDTH].reshape(b, s, FOX_HEADS, FOX_HEAD_DIM), k_norm)
    v = u[..., 2 * FOX_WIDTH:3 * FOX_WIDTH].reshape(b, s, FOX_HEADS, FOX_HEAD_DIM)
    f_logit = u[..., 3 * FOX_WIDTH:].astype(jnp.float32) + b_f.astype(jnp.float32)
    log_f_cum = jnp.cumsum(jax.nn.log_sigmoid(f_logit), axis=1)
    out = causal_attention_blocks(q, k, v, 1.0 / math.sqrt(FOX_HEAD_DIM), log_f_cum)
    return out.reshape(b, s, FOX_WIDTH)


def setup_inputs(seed: int = 0) -> dict:
    key = jax.random.key(seed)
    keys = iter(jax.random.split(key, 64))

    def nrm(shape, fan_in):
        return jax.random.normal(next(keys), shape, jnp.float32) * (fan_in ** -0.5)

    def gain(shape):
        return 1.0 + 0.1 * jax.random.normal(next(keys), shape, jnp.float32)

    d, f = D_MODEL, D_FF
    inp = {}
    inp["x"] = jax.random.normal(next(keys), (BATCH, SEQ, d), jnp.float32)
    inp["mem"] = jax.random.normal(next(keys), (BATCH, MEM_LEN, d), jnp.float32)
    inp["norm_ffn1"] = gain((DEPTH, d))
    inp["ffn1_w_gate"] = nrm((DEPTH, d, f), d)
    inp["ffn1_w_up"] = nrm((DEPTH, d, f), d)
    inp["ffn1_w_down"] = nrm((DEPTH, f, d), f)
    inp["norm_mix"] = gain((DEPTH, d))
    inp["norm_ffn2"] = gain((DEPTH, d))
    inp["ffn2_w_gate"] = nrm((DEPTH, d, f), d)
    inp["ffn2_w_up"] = nrm((DEPTH, d, f), d)
    inp["ffn2_w_down"] = nrm((DEPTH, f, d), f)
    inp["norm_mem"] = gain((DEPTH, d))
    inp["mem_w_kv"] = nrm((DEPTH, d, 2 * MEM_WIDTH), d)
    inp["mem_q_norm"] = gain((DEPTH, MEM_HEAD_DIM))
    inp["mem_k_norm"] = gain((DEPTH, MEM_HEAD_DIM))
    inp["conv_w_in"] = nrm((N_LAYERS_A, d, CONV_IN + MEM_WIDTH), d)
    inp["conv_w"] = nrm((N_LAYERS_A, CONV_K, CONV_WIDTH), CONV_K)
    inp["conv_w_out"] = nrm((N_LAYERS_A, CONV_WIDTH + MEM_WIDTH, d), CONV_WIDTH + MEM_WIDTH)
    inp["mla_w_in"] = nrm((N_LAYERS_B, d, MLA_IN + MEM_WIDTH), d)
    inp["mla_q_a_norm"] = gain((N_LAYERS_B, MLA_Q_RANK))
    inp["mla_w_q_b"] = nrm((N_LAYERS_B, MLA_Q_RANK, MLA_HEADS * MLA_QK), MLA_Q_RANK)
    inp["mla_kv_a_norm"] = gain((N_LAYERS_B, MLA_KV_RANK))
    inp["mla_w_kv_b"] = nrm((N_LAYERS_B, MLA_KV_RANK, MLA_HEADS * (MLA_NOPE + MLA_V)), MLA_KV_RANK)
    inp["mla_q_norm"] = gain((N_LAYERS_B, MLA_QK))
    inp["mla_k_norm"] = gain((N_LAYERS_B, MLA_QK))
    inp["mla_w_out"] = nrm((N_LAYERS_B, MLA_HEADS * MLA_V + MEM_WIDTH, d), MLA_HEADS * MLA_V + MEM_WIDTH)
    inp["swa_w_in"] = nrm((N_LAYERS_C, d, SWA_IN + MEM_WIDTH), d)
    inp["swa_q_norm"] = gain((N_LAYERS_C, SWA_HEAD_DIM))
    inp["swa_k_norm"] = gain((N_LAYERS_C, SWA_HEAD_DIM))
    inp["swa_sinks"] = 0.5 * jax.random.normal(next(keys), (N_LAYERS_C, SWA_Q_HEADS), jnp.float32)
    swa_out_in = SWA_Q_HEADS * SWA_HEAD_DIM + MEM_WIDTH
    inp["swa_w_out"] = nrm((N_LAYERS_C, swa_out_in, d), swa_out_in)
    inp["rel_bias"] = 0.5 * jax.random.normal(next(keys), (REL_BUCKETS, SWA_Q_HEADS), jnp.float32)
    inp["fox_w_in"] = nrm((N_LAYERS_D, d, FOX_IN + MEM_WIDTH), d)
    inp["fox_b_f"] = jax.random.uniform(next(keys), (N_LAYERS_D, FOX_HEADS), jnp.float32, 1.0, 4.0)
    inp["fox_q_norm"] = gain((N_LAYERS_D, FOX_HEAD_DIM))
    inp["fox_k_norm"] = gain((N_LAYERS_D, FOX_HEAD_DIM))
    inp["fox_w_out"] = nrm((N_LAYERS_D, FOX_WIDTH + MEM_WIDTH, d), FOX_WIDTH + MEM_WIDTH)
    return inp


def reference(x, mem, norm_ffn1, ffn1_w_gate, ffn1_w_up, ffn1_w_down, norm_mix,
              norm_ffn2, ffn2_w_gate, ffn2_w_up, ffn2_w_down,
              norm_mem, mem_w_kv, mem_q_norm, mem_k_norm,
              conv_w_in, conv_w, conv_w_out,
              mla_w_in, mla_q_a_norm, mla_w_q_b, mla_kv_a_norm, mla_w_kv_b, mla_q_norm, mla_k_norm, mla_w_out,
              swa_w_in, swa_q_norm, swa_k_norm, swa_sinks, swa_w_out, rel_bias,
              fox_w_in, fox_b_f, fox_q_norm, fox_k_norm, fox_w_out):
    b, s, _ = x.shape
    m_len = mem.shape[1]
    for i in range(DEPTH):
        kind, occ = i % N_MIXERS, i // N_MIXERS
        x = x + 0.5 * swiglu(rms_norm(x, norm_ffn1[i]), ffn1_w_gate[i], ffn1_w_up[i], ffn1_w_down[i])
        h = rms_norm(x, norm_mix[i])
        if kind == 0:
            u = h @ conv_w_in[occ]
            mix = short_conv_mixer(u[..., :-MEM_WIDTH], conv_w[occ])
            w_out = conv_w_out[occ]
        elif kind == 1:
            u = h @ mla_w_in[occ]
            mix = mla_mixer(u[..., :-MEM_WIDTH], mla_q_a_norm[occ], mla_w_q_b[occ], mla_kv_a_norm[occ],
                            mla_w_kv_b[occ], mla_q_norm[occ], mla_k_norm[occ])
            w_out = mla_w_out[occ]
        elif kind == 2:
            u = h @ swa_w_in[occ]
            mix = swa_mixer(u[..., :-MEM_WIDTH], swa_q_norm[occ], swa_k_norm[occ], swa_sinks[occ], rel_bias)
            w_out = swa_w_out[occ]
        else:
            u = h @ fox_w_in[occ]
            mix = fox_mixer(u[..., :-MEM_WIDTH], fox_b_f[occ], fox_q_norm[occ], fox_k_norm[occ])
            w_out = fox_w_out[occ]
        mkv = (rms_norm(mem, norm_mem[i]) @ mem_w_kv[i]).reshape(b, m_len, 2, MEM_HEADS, MEM_HEAD_DIM)
        mk = rms_norm(mkv[:, :, 0], mem_k_norm[i])
        mv = mkv[:, :, 1]
        mq = rms_norm(u[..., -MEM_WIDTH:].reshape(b, s, MEM_HEADS, MEM_HEAD_DIM), mem_q_norm[i])
        mem_out = memory_attention(mq, mk, mv).reshape(b, s, MEM_WIDTH)
        x = x + jnp.concatenate([mix, mem_out], axis=-1) @ w_out
        x = x + 0.5 * swiglu(rms_norm(x, norm_ffn2[i]), ffn2_w_gate[i], ffn2_w_up[i], ffn2_w_down[i])
    return x
```

```python
import contextlib
import math
import numpy as np
import concourse.bass as bass
import concourse.mybir as mybir
from concourse.bass_utils import run_bass_kernel_spmd

F32 = mybir.dt.float32
BF16 = mybir.dt.bfloat16
AF = mybir.ActivationFunctionType
ALU = mybir.AluOpType
AX = mybir.AxisListType

COMPUTE = ("pe", "act", "dve", "pool")

D = 2048
FF = 5632
T = 1024
SEQ = 2048
KC = D // 128
NG = FF // 512
EPS = 1e-6
MEM = 256
NCORES = 8


class Op:
    __slots__ = ("eng", "fn", "deps", "needs_inc", "is_dma", "sem", "count", "idx", "prev_same_sem", "inc")

    def __init__(self, eng, fn, is_dma=False):
        self.eng = eng
        self.fn = fn
        self.deps = []
        self.needs_inc = False
        self.is_dma = is_dma
        self.sem = None
        self.count = None
        self.idx = None
        self.prev_same_sem = None
        self.inc = 16


class Tracer:
    def __init__(self, nc, n_dma_sems=16):
        self.nc = nc
        self.ops = {e: [] for e in ("pe", "act", "dve", "pool", "sp")}
        self.last_writer = {}
        self.readers = {}
        self.n_dma_sems = n_dma_sems
        self.dma_rr = {"sp": 0, "pool": 0, "act": 0}
        self.dma_last = {}
        self.dma_cnt = {}

    def op(self, eng, fn, reads=(), writes=(), is_dma=False, cc=False, raw_dist=3):
        o = Op(eng, fn, is_dma)
        if cc:
            o.inc = 1
        o.idx = len(self.ops[eng])
        deps = []
        for r in reads:
            w = self.last_writer.get(r)
            if w is not None:
                deps.append((w, 0))
        for wkey in writes:
            w = self.last_writer.get(wkey)
            if w is not None:
                deps.append((w, 1))
            for rd in self.readers.get(wkey, ()):
                deps.append((rd, 2))
        seen = set()
        for d, kind in deps:
            if d is o or id(d) in seen:
                continue
            if (not d.is_dma) and d.eng == eng and not is_dma:
                if eng == "pe":
                    continue
                if kind != 0:
                    continue
                if o.idx - d.idx > raw_dist:
                    continue
            seen.add(id(d))
            o.deps.append(d)
            d.needs_inc = True
        if is_dma:
            if cc:
                key = ("cc", 0)
            else:
                slot = self.dma_rr[eng] % self.n_dma_sems
                self.dma_rr[eng] += 1
                key = (eng, slot)
            o.prev_same_sem = self.dma_last.get(key)
            self.dma_last[key] = o
            self.dma_cnt[key] = self.dma_cnt.get(key, 0) + o.inc
            o.sem = key
            o.count = self.dma_cnt[key]
            o.needs_inc = True
        for r in reads:
            self.readers.setdefault(r, []).append(o)
        for wkey in writes:
            self.last_writer[wkey] = o
            self.readers[wkey] = []
        self.ops[eng].append(o)
        return o

    def dma(self, queue, out, in_, reads=(), writes=(), **kw):
        return self.op(queue, lambda e: e.dma_start(out=out, in_=in_, **kw), reads, writes, is_dma=True)

    def allgather(self, src, dst, reads=(), writes=()):
        groups = [[0, 1], [2, 3], [4, 5], [6, 7]]
        return self.op("pool", lambda e: e.collective_compute("AllGather", ALU.bypass, replica_groups=groups,
                                                              ins=[src.opt()], outs=[dst.opt()]),
                       reads, writes, is_dma=True, cc=True)

    def emit(self, final_waits=()):
        nc = self.nc
        with contextlib.ExitStack() as st:
            esem = {e: st.enter_context(nc.semaphore("s_" + e)) for e in COMPUTE}
            dsem = {}
            for key in self.dma_cnt:
                dsem[key] = st.enter_context(nc.semaphore("d_%s_%d" % key))
            for e in COMPUTE:
                c = 0
                for o in self.ops[e]:
                    if o.is_dma:
                        continue
                    if o.needs_inc:
                        c += 1
                        o.count = c
                        o.sem = e
            ops = self.ops
            final_waits = list(final_waits)

            def semof(d):
                return dsem[d.sem] if d.is_dma else esem[d.sem]

            def run(ename, eng):
                known = {}
                for o in ops[ename]:
                    waits = {}
                    dl = list(o.deps)
                    if o.is_dma and o.prev_same_sem is not None:
                        dl.append(o.prev_same_sem)
                    for d in dl:
                        k = d.sem
                        if waits.get(k, (None, 0))[1] < d.count:
                            waits[k] = (semof(d), d.count)
                    for k, (s, v) in waits.items():
                        if known.get(k, 0) >= v:
                            continue
                        known[k] = v
                        eng.wait_ge(s, v)
                    ins = o.fn(eng)
                    if o.needs_inc:
                        if o.is_dma:
                            ins.then_inc(dsem[o.sem], o.inc)
                        else:
                            ins.then_inc(esem[o.sem], 1)
                if ename == "sp":
                    for d in final_waits:
                        if known.get(d.sem, 0) < d.count:
                            known[d.sem] = d.count
                            eng.wait_ge(semof(d), d.count)

            with nc.Block() as block:
                @block.sync
                def _(e):
                    run("sp", e)

                @block.tensor
                def _(e):
                    run("pe", e)

                @block.scalar
                def _(e):
                    run("act", e)

                @block.vector
                def _(e):
                    run("dve", e)

                @block.gpsimd
                def _(e):
                    run("pool", e)


CSHIFT = 12.0
NEG = -30000.0


class Prog:
    def __init__(self, cc=False):
        self.cc = cc
        self.nc = bass.Bass("TRN2", target_bir_lowering=False)
        self.tr = Tracer(self.nc)
        self.st = contextlib.ExitStack()
        self.finals = []
        nc, st = self.nc, self.st
        sb = lambda n, shp, dt: st.enter_context(nc.sbuf_tensor(n, shp, dt))
        self.XT = sb("XT", [128, KC, T], F32)
        self.HT = sb("HT", [128, KC, T], BF16)
        self.AT2 = sb("AT2", [128, 8, T], BF16)
        self.NWS = 12
        self.WR = sb("WR", [128, self.NWS, 4, 512], BF16)
        self.SG = sb("SG", [128, 2, 512], F32)
        self.SCR = sb("SCR", [128, 2056], F32)
        self.MQ = sb("MQ", [128, 4, T], BF16)
        self.LAT = sb("LAT", [128, 4, T], BF16)
        self.PT = sb("PT", [128, 4, 512], BF16)
        self.RS = sb("RS", [128, 2, 512], F32)
        self.T1 = sb("T1", [128, 2, 512], F32)
        self.ONES = sb("ONES", [128, 128], F32)
        self.ONESB = sb("ONESB", [128, 128], BF16)
        self.BD64 = sb("BD64", [128, 128], F32)
        self.RMT = sb("RMT", [64, 64], F32)
        self.CM = sb("CM", [128, 4, 512], BF16)
        self.EPSB = sb("EPSB", [128, 1], F32)
        self.NEGC = sb("NEGC", [128, 1], F32)
        self.GV = sb("GV", [128, 64], F32)
        self.GV2 = sb("GV2", [128, 96], F32)
        self.DUMMY = sb("DUMMY", [128, 4], F32)
        self.ZB = sb("ZB", [128, 64], BF16)
        self.PCOL = sb("PCOL", [128, 4], F32)
        self.GB01 = sb("GB01", [128, 16, 2], F32)
        self.ZHP = sb("ZHP", [128, 16, 2], F32)
        self.DY = sb("DY", [128, 16, 2], F32)
        self.DYT = sb("DYT", [128, 16], F32)
        self.DYB = sb("DYB", [128, 16, 2], BF16)
        self.ZH = sb("ZH", [128, 16, 2], F32)
        self.ps = [st.enter_context(nc.psum_tensor("ps%d" % i, [128, 512], F32)) for i in range(8)]
        self.wslot = 0
        self.psrr = 0
        self.rr = {}
        self.AT = [self.AT2[:, 4 * b:4 * b + 4, :] for b in range(2)]
        f32v = self.AT2[:, :, :].bitcast(F32).rearrange("p a t -> p (a t)")
        self.ACC = f32v[:, 0:1024]
        self.SQ = [f32v[:, 1024:2048], f32v[:, 2048:3072]]
        self.RSTD = f32v[:, 3072:4096]
        self.ACC1 = self.T1[:, :, :].rearrange("p a t -> p (a t)")
        self.AT32 = self.AT2[:, :, :].bitcast(F32)
        tr = self.tr
        tr.op("dve", lambda e: e.memset(self.ONES[:, :], 1.0), writes=["ones"])
        tr.op("dve", lambda e: e.memset(self.ONESB[:, :], 1.0), writes=["onesb"])
        tr.op("dve", lambda e: e.memset(self.EPSB[:, :], EPS), writes=["eps"])
        tr.op("dve", lambda e: e.memset(self.NEGC[:, :], -CSHIFT), writes=["negc"])
        tr.op("dve", lambda e: e.memset(self.ZB[:, :], 0.0), writes=["zb"])

    def rot(self, name, n=2):
        i = self.rr.get(name, 0)
        self.rr[name] = i + 1
        return i % n

    def din(self, name, shape, dt=F32):
        return self.nc.dram_tensor(name, list(shape), dt, kind="ExternalInput").ap()

    def dout(self, name, shape, dt=F32):
        return self.nc.dram_tensor(name, list(shape), dt, kind="ExternalOutput").ap()

    def dscr(self, name, shape, dt=BF16):
        return self.nc.dram_tensor(name, list(shape), dt).ap()

    def atkeys(self):
        return [("cat", c, n) for c in range(8) for n in range(2)]

    def allgather_rows(self, src, dst, rc, reads, wkey):
        R = src.shape[0]
        assert R % rc == 0
        for i in range(R // rc):
            self.tr.allgather(src[i * rc:(i + 1) * rc, :], dst[2 * rc * i:2 * rc * (i + 1), :], reads=reads, writes=[(wkey, i)])

    @staticmethod
    def r0rows(dst, rc, r_lo, n):
        i, loc = r_lo // rc, r_lo % rc
        assert loc + n <= rc
        return dst[2 * rc * i + loc:2 * rc * i + loc + n, :]

    def load_pcols(self, pcols):
        self.tr.dma("sp", self.PCOL[:, 0:3], pcols, writes=["pcol"])

    def load_consts(self, cm, bd64, rmt):
        self.tr.dma("pool", self.CM[:, :, :], cm, writes=["cm"])
        self.tr.dma("sp", self.BD64[:, :], bd64, writes=["bd64"])
        self.tr.dma("sp", self.RMT[:, :], rmt, writes=["rmt"])

    def load_x(self, xT, c0):
        v = xT.rearrange("(c p) t -> p c t", p=128)
        for q in range(4):
            self.tr.dma("sp", self.XT[:, 4 * q:4 * q + 4, :], v[:, 4 * q:4 * q + 4, c0:c0 + T],
                        writes=[("xt", c, n) for c in range(4 * q, 4 * q + 4) for n in range(2)])

    def store_x(self, yT, c0):
        v = yT.rearrange("(c p) t -> p c t", p=128)
        for q in range(4):
            o = self.tr.dma("sp", v[:, 4 * q:4 * q + 4, c0:c0 + T], self.XT[:, 4 * q:4 * q + 4, :],
                            reads=[("xt", c, n) for c in range(4 * q, 4 * q + 4) for n in range(2)],
                            writes=[("yT", q, c0)])
            self.finals.append(o)

    def next_ps(self, lo=0, hi=8):
        b = lo + (self.psrr % (hi - lo))
        self.psrr += 1
        return b

    def rmsnorm_x(self, gain):
        tr = self.tr
        XT, HT, ACC, SQ, RSTD, GV = self.XT, self.HT, self.ACC, self.SQ, self.RSTD, self.GV
        scr = self.atkeys()
        tr.dma("sp", GV[:, 0:KC], gain, writes=["gv"])
        for kc in range(KC):
            sq = SQ[kc % 2]
            tr.op("act", lambda e, kc=kc, sq=sq: e.activation(out=sq, in_=XT[:, kc, :], func=AF.Square),
                  reads=[("xt", kc, 0), ("xt", kc, 1)], writes=[("sq", kc % 2)] + (scr if kc == 0 else []))
            acc = ACC if kc % 2 == 0 else self.ACC1
            ak = ["acc"] if kc % 2 == 0 else [("t1", 0), ("t1", 1)]
            if kc < 2:
                tr.op("dve", lambda e, sq=sq, acc=acc: e.tensor_copy(out=acc, in_=sq), reads=[("sq", kc % 2)], writes=ak)
            else:
                tr.op("dve", lambda e, sq=sq, acc=acc: e.tensor_tensor(out=acc, in0=acc, in1=sq, op=ALU.add),
                      reads=[("sq", kc % 2)] + ak, writes=ak, raw_dist=1)
        for n in range(2):
            b = self.next_ps()
            sl = slice(n * 512, (n + 1) * 512)
            tr.op("pe", lambda e, b=b, sl=sl: e.matmul(self.ps[b][:, :], lhsT=self.ONES[:, :], rhs=ACC[:, sl], start=True, stop=False),
                  reads=["acc", "ones"], writes=[("ps", b)])
            tr.op("pe", lambda e, b=b, sl=sl: e.matmul(self.ps[b][:, :], lhsT=self.ONES[:, :], rhs=self.ACC1[:, sl], start=False, stop=True),
                  reads=[("t1", 0), ("t1", 1), "ones"], writes=[("ps", b)])
            tr.op("act", lambda e, b=b, sl=sl: e.activation(out=RSTD[:, sl], in_=self.ps[b][:, :], func=AF.Ln,
                                                             scale=1.0 / D, bias=self.EPSB[:, 0:1]),
                  reads=[("ps", b), "eps"], writes=[("rstd", n)])
            tr.op("act", lambda e, sl=sl: e.activation(out=RSTD[:, sl], in_=RSTD[:, sl], func=AF.Exp, scale=-0.5),
                  reads=[("rstd", n)], writes=[("rstd", n)])
        for kc in range(KC):
            for n in range(2):
                sl = slice(n * 512, (n + 1) * 512)
                tr.op("dve", lambda e, kc=kc, sl=sl: e.scalar_tensor_tensor(
                    out=HT[:, kc, sl], in0=XT[:, kc, sl], scalar=GV[:, kc:kc + 1], in1=RSTD[:, sl],
                    op0=ALU.mult, op1=ALU.mult),
                    reads=[("xt", kc, n), ("rstd", n), "gv"] + scr, writes=[("ht", kc, n)])

    def wload(self, src):
        s = self.wslot % self.NWS
        self.wslot += 1
        kp, kcn, ncol = src.shape[0], src.shape[1], src.shape[2]
        self.tr.dma("pool", self.WR[0:kp, s, 0:kcn, 0:ncol], src, writes=[("w", s)])
        return s

    def ffn(self, gain, wg, wu, wd):
        tr = self.tr
        self.rmsnorm_x(gain)
        wgv = wg.rearrange("(kc p) f -> p kc f", p=128)
        wuv = wu.rearrange("(kc p) f -> p kc f", p=128)
        wdv = wd.rearrange("(fc p) d -> p fc d", p=128)
        XT, HT, WR, SG, ps = self.XT, self.HT, self.WR, self.SG, self.ps

        def gu(g):
            at = self.AT[g % 2]
            gs = [self.wload(wgv[:, 4 * s:4 * s + 4, g * 512:(g + 1) * 512]) for s in range(4)]
            us = [self.wload(wuv[:, 4 * s:4 * s + 4, g * 512:(g + 1) * 512]) for s in range(4)]
            for m in range(4):
                for n in range(2):
                    bg = self.next_ps(0, 4)
                    bu = self.next_ps(0, 4)
                    for (bb, ss) in ((bg, gs), (bu, us)):
                        for kc in range(KC):
                            tr.op("pe", lambda e, bb=bb, s=ss[kc // 4], kc=kc, m=m, n=n: e.matmul(
                                ps[bb][:, :], lhsT=WR[:, s, kc % 4, m * 128:(m + 1) * 128],
                                rhs=HT[:, kc, n * 512:(n + 1) * 512], start=(kc == 0), stop=(kc == KC - 1)),
                                reads=[("w", ss[kc // 4]), ("ht", kc, n)], writes=[("ps", bb)])
                    si = self.rot("sg")
                    tr.op("act", lambda e, bg=bg, si=si: e.activation(out=SG[:, si, :], in_=ps[bg][:, :], func=AF.Silu),
                          reads=[("ps", bg)], writes=[("sg", si)])
                    tr.op("dve", lambda e, bu=bu, si=si, at=at, m=m, n=n: e.tensor_tensor(
                        out=at[:, m, n * 512:(n + 1) * 512], in0=SG[:, si, :], in1=ps[bu][:, :], op=ALU.mult),
                        reads=[("sg", si), ("ps", bu)], writes=[("cat", 4 * (g % 2) + m, n)])

        def dn(g):
            ab = g % 2
            self.outproj_group(wdv[:, 4 * g:4 * g + 4, :], [(self.AT[ab][:, kc, :], ("cat", 4 * ab + kc)) for kc in range(4)], 0.5)

        gu(0)
        for g in range(1, NG):
            gu(g)
            dn(g - 1)
        dn(NG - 1)

    def outproj_group(self, wv, tiles, scale, small=None):
        tr = self.tr
        XT, WR, ps = self.XT, self.WR, self.ps
        nkt = len(tiles)
        kp = wv.shape[0]
        ds = [self.wload(wv[:, :, j * 512:(j + 1) * 512]) for j in range(4)]
        for mo in range(KC):
            for n in range(2 if small is None else 1):
                b = self.next_ps(4, 8)
                c_lo, c_hi = (n * 512, (n + 1) * 512) if small is None else (0, small)
                wd_ = c_hi - c_lo
                for kt in range(nkt):
                    a, key = tiles[kt]
                    rk = key + (n,) if small is None else key
                    tr.op("pe", lambda e, b=b, s=ds[mo // 4], kt=kt, mo=mo, a=a, c_lo=c_lo, c_hi=c_hi, wd_=wd_: e.matmul(
                        ps[b][:, 0:wd_], lhsT=WR[0:kp, s, kt, (mo % 4) * 128:(mo % 4 + 1) * 128],
                        rhs=a[:, c_lo:c_hi], start=(kt == 0), stop=(kt == nkt - 1)),
                        reads=[("w", ds[mo // 4]), rk], writes=[("ps", b)])
                tr.op("dve", lambda e, b=b, mo=mo, c_lo=c_lo, c_hi=c_hi, wd_=wd_: e.scalar_tensor_tensor(
                    out=XT[:, mo, c_lo:c_hi], in0=ps[b][:, 0:wd_], scalar=scale,
                    in1=XT[:, mo, c_lo:c_hi], op0=ALU.mult, op1=ALU.add),
                    reads=[("ps", b), ("xt", mo, n)], writes=[("xt", mo, n)])

    def linear_fm(self, src, nk, wv, groups, epi, kp=128):
        tr = self.tr
        WR, ps = self.WR, self.ps
        pending = []
        for (g0, gw, mcs) in groups:
            slots = [self.wload(wv[:, 4 * s:min(4 * s + 4, nk), g0:g0 + gw]) for s in range((nk + 3) // 4)]
            for (off, w) in mcs:
                for n in range(2):
                    b = self.next_ps(0, 4)
                    for kc in range(nk):
                        a, key = src(kc, n)
                        s = slots[kc // 4]
                        tr.op("pe", lambda e, b=b, s=s, kc=kc, off=off, w=w, a=a: e.matmul(
                            ps[b][0:w, :], lhsT=WR[0:kp, s, kc % 4, off:off + w], rhs=a,
                            start=(kc == 0), stop=(kc == nk - 1)),
                            reads=[("w", s), key], writes=[("ps", b)])
                    pending.append((g0 + off, w, n, b))
                    if len(pending) > 1:
                        epi(*pending.pop(0))
        while pending:
            epi(*pending.pop(0))

    def src_ht(self, kc, n):
        return self.HT[:, kc, n * 512:(n + 1) * 512], ("ht", kc, n)

    def src_lat(self, kc, n):
        return self.LAT[:, kc, n * 512:(n + 1) * 512], ("lat", kc, n)

    def colnorm(self, b, w, dsz, out_op, blk64=False):
        tr = self.tr
        ps = self.ps
        si = self.rot("sg")
        ri = self.rot("rs")
        SG, RS = self.SG, self.RS
        tr.op("act", lambda e: e.activation(out=SG[0:w, si, :], in_=ps[b][0:w, :], func=AF.Square),
              reads=[("ps", b)], writes=[("sg", si)])
        b2 = self.next_ps(4, 8)
        ones = self.BD64 if blk64 else self.ONES
        tr.op("pe", lambda e: e.matmul(ps[b2][0:w, :], lhsT=ones[0:w, 0:w], rhs=SG[0:w, si, :], start=True, stop=True),
              reads=[("sg", si), "ones", "bd64"], writes=[("ps", b2)])
        tr.op("act", lambda e: e.activation(out=RS[0:w, ri, :], in_=ps[b2][0:w, :], func=AF.Ln, scale=1.0 / dsz,
                                            bias=self.EPSB[0:w, 0:1]),
              reads=[("ps", b2), "eps"], writes=[("rs", ri)])
        tr.op("act", lambda e: e.activation(out=RS[0:w, ri, :], in_=RS[0:w, ri, :], func=AF.Exp, scale=-0.5), reads=[("rs", ri)], writes=[("rs", ri)])
        out_op(RS[0:w, ri, :], ("rs", ri))

    def norm_to(self, b, w, dsz, gcol, gkey, out_ap, out_key, blk64=False):
        def fin(rs, rskey):
            self.tr.op("dve", lambda e: e.scalar_tensor_tensor(out=out_ap, in0=self.ps[b][0:w, :], scalar=gcol, in1=rs,
                                                               op0=ALU.mult, op1=ALU.mult),
                       reads=[("ps", b), rskey, gkey], writes=[out_key])
        self.colnorm(b, w, dsz, fin, blk64)

    def rope_to(self, b, gcol, gkey, cs_lo, out_ap, out_key):
        tr = self.tr
        ti = self.rot("t1")
        T1 = self.T1
        cosv = self.SCR[0:64, 0:T][:, cs_lo:cs_lo + 512]
        sinv = self.SCR[0:64, T:2 * T][:, cs_lo:cs_lo + 512]
        self.norm_to(b, 64, 64.0, gcol, gkey, T1[0:64, ti, :], ("t1", ti))
        b3 = self.next_ps(4, 8)
        tr.op("pe", lambda e: e.matmul(self.ps[b3][0:64, :], lhsT=self.RMT[:, :], rhs=T1[0:64, ti, :], start=True, stop=True),
              reads=[("t1", ti), "rmt"], writes=[("ps", b3)])
        si = self.rot("sg")
        tr.op("dve", lambda e: e.tensor_tensor(out=self.SG[0:64, si, :], in0=self.ps[b3][0:64, :], in1=sinv, op=ALU.mult),
              reads=[("ps", b3), "cs"], writes=[("sg", si)])
        tr.op("dve", lambda e: e.tensor_tensor(out=T1[0:64, ti, :], in0=T1[0:64, ti, :], in1=cosv, op=ALU.mult),
              reads=[("t1", ti), "cs"], writes=[("t1", ti)])
        tr.op("dve", lambda e: e.tensor_tensor(out=out_ap, in0=T1[0:64, ti, :], in1=self.SG[0:64, si, :], op=ALU.add),
              reads=[("t1", ti), ("sg", si)], writes=[out_key])

    def mem_kv(self, memT, gmem, wkv, gk):
        tr = self.tr
        HT = self.HT
        h32 = HT[:, :, :].bitcast(F32).rearrange("p a t -> p (a t)")
        MEMF = h32[:, 0:4096].rearrange("p (c m) -> p c m", m=MEM)
        hb = HT[:, :, :].rearrange("p a t -> p (a t)")
        MEMN = hb[:, 8192:12288].rearrange("p (c m) -> p c m", m=MEM)
        MK = hb[:, 12288:13312].rearrange("p (h m) -> p h m", m=MEM)
        MV = hb[:, 13312:14336].rearrange("p (j c) -> p j c", c=512)
        allht = [("ht", kc, n) for kc in range(KC) for n in range(2)]
        tr.dma("sp", MEMF, memT.rearrange("(c p) m -> p c m", p=128), writes=allht)
        tr.dma("sp", self.GV2[:, 0:KC], gmem, writes=["gv2"])
        tr.dma("sp", self.GV2[:, 16:17], gk, writes=["gv2b"])
        b = self.next_ps(4, 8)
        for kc in range(KC):
            si = self.rot("sg")
            tr.op("act", lambda e, kc=kc, si=si: e.activation(out=self.SG[:, si, 0:MEM], in_=MEMF[:, kc, :], func=AF.Square),
                  reads=allht[0:1], writes=[("sg", si)])
            tr.op("pe", lambda e, kc=kc, si=si: e.matmul(self.ps[b][:, 0:MEM], lhsT=self.ONES[:, :], rhs=self.SG[:, si, 0:MEM],
                                                         start=(kc == 0), stop=(kc == KC - 1)),
                  reads=[("sg", si), "ones"], writes=[("ps", b)])
        ri = self.rot("rs")
        RS = self.RS
        tr.op("act", lambda e: e.activation(out=RS[:, ri, 0:MEM], in_=self.ps[b][:, 0:MEM], func=AF.Ln, scale=1.0 / D,
                                            bias=self.EPSB[:, 0:1]), reads=[("ps", b), "eps"], writes=[("rs", ri)])
        tr.op("act", lambda e: e.activation(out=RS[:, ri, 0:MEM], in_=RS[:, ri, 0:MEM], func=AF.Exp, scale=-0.5), reads=[("rs", ri)], writes=[("rs", ri)])
        for kc in range(KC):
            tr.op("dve", lambda e, kc=kc: e.scalar_tensor_tensor(out=MEMN[:, kc, :], in0=MEMF[:, kc, :], scalar=self.GV2[:, kc:kc + 1],
                                                                in1=RS[:, ri, 0:MEM], op0=ALU.mult, op1=ALU.mult),
                  reads=[("rs", ri), "gv2"] + allht[0:1], writes=[("memn", kc)])
        wv = wkv.rearrange("(kc p) f -> p kc f", p=128)
        slots = [self.wload(wv[:, 4 * s:4 * s + 4, 0:512]) for s in range(4)]
        for h in range(4):
            bb = self.next_ps(0, 4)
            for kc in range(KC):
                tr.op("pe", lambda e, kc=kc, h=h, bb=bb, s=slots[kc // 4]: e.matmul(
                    self.ps[bb][:, 0:MEM], lhsT=self.WR[:, s, kc % 4, h * 128:(h + 1) * 128], rhs=MEMN[:, kc, :],
                    start=(kc == 0), stop=(kc == KC - 1)), reads=[("w", slots[kc // 4]), ("memn", kc)], writes=[("ps", bb)])
            si = self.rot("sg")
            tr.op("act", lambda e, bb=bb, si=si: e.activation(out=self.SG[:, si, 0:MEM], in_=self.ps[bb][:, 0:MEM], func=AF.Square),
                  reads=[("ps", bb)], writes=[("sg", si)])
            b2 = self.next_ps(4, 8)
            tr.op("pe", lambda e, b2=b2, si=si: e.matmul(self.ps[b2][:, 0:MEM], lhsT=self.ONES[:, :], rhs=self.SG[:, si, 0:MEM],
                                                         start=True, stop=True), reads=[("sg", si), "ones"], writes=[("ps", b2)])
            r2 = self.rot("rs")
            tr.op("act", lambda e, b2=b2, r2=r2: e.activation(out=RS[:, r2, 0:MEM], in_=self.ps[b2][:, 0:MEM], func=AF.Ln,
                                                              scale=1.0 / 128, bias=self.EPSB[:, 0:1]),
                  reads=[("ps", b2), "eps"], writes=[("rs", r2)])
            tr.op("act", lambda e, r2=r2: e.activation(out=RS[:, r2, 0:MEM], in_=RS[:, r2, 0:MEM], func=AF.Exp, scale=-0.5), reads=[("rs", r2)], writes=[("rs", r2)])
            tr.op("dve", lambda e, bb=bb, r2=r2, h=h: e.scalar_tensor_tensor(out=MK[:, h, :], in0=self.ps[bb][:, 0:MEM],
                                                                            scalar=self.GV2[:, 16:17], in1=RS[:, r2, 0:MEM],
                                                                            op0=ALU.mult, op1=ALU.mult),
                  reads=[("ps", bb), ("rs", r2), "gv2b"], writes=[("mk", h)])
        slots = [self.wload(wv[:, 4 * s:4 * s + 4, 512:1024]) for s in range(4)]
        for j in range(2):
            bb = self.next_ps(0, 4)
            for kc in range(KC):
                tr.op("pe", lambda e, kc=kc, j=j, bb=bb, s=slots[kc // 4]: e.matmul(
                    self.ps[bb][:, :], lhsT=MEMN[:, kc, j * 128:(j + 1) * 128], rhs=self.WR[:, s, kc % 4, :],
                    start=(kc == 0), stop=(kc == KC - 1)), reads=[("w", slots[kc // 4]), ("memn", kc)], writes=[("ps", bb)])
            tr.op("act", lambda e, bb=bb, j=j: e.copy(out=MV[:, j, :], in_=self.ps[bb][:, :]), reads=[("ps", bb)], writes=[("mv", j)])
        return MK, MV

    def mem_attn(self, MK, MV, ab):
        tr = self.tr
        ps, PT = self.ps, self.PT
        scale = 1.0 / math.sqrt(128.0)
        for h in range(4):
            for n in range(2):
                sl = slice(n * 512, (n + 1) * 512)
                bo = self.next_ps(4, 6)
                bl = self.next_ps(6, 8)
                for j in range(2):
                    b = self.next_ps(0, 4)
                    tr.op("pe", lambda e, b=b, h=h, j=j, sl=sl: e.matmul(ps[b][:, :], lhsT=MK[:, h, j * 128:(j + 1) * 128],
                                                                        rhs=self.MQ[:, h, sl], start=True, stop=True),
                          reads=[("mk", h), ("mq", h, n)], writes=[("ps", b)])
                    pi = self.rot("pt", 4)
                    tr.op("act", lambda e, b=b, pi=pi: e.activation(out=PT[:, pi, :], in_=ps[b][:, :], func=AF.Exp, scale=scale,
                                                                    bias=self.NEGC[:, 0:1]),
                          reads=[("ps", b), "negc"], writes=[("pt", pi)])
                    tr.op("pe", lambda e, bo=bo, h=h, j=j, pi=pi: e.matmul(ps[bo][:, :], lhsT=MV[:, j, h * 128:(h + 1) * 128],
                                                                          rhs=PT[:, pi, :], start=(j == 0), stop=(j == 1)),
                          reads=[("mv", j), ("pt", pi)], writes=[("ps", bo)])
                    tr.op("pe", lambda e, bl=bl, j=j, pi=pi: e.matmul(ps[bl][:, :], lhsT=self.ONESB[:, :], rhs=PT[:, pi, :],
                                                                     start=(j == 0), stop=(j == 1)),
                          reads=["onesb", ("pt", pi)], writes=[("ps", bl)])
                self.attn_finish(bo, bl, 128, self.AT[ab][:, h, sl], ("cat", 4 * ab + h, n))

    def attn_finish(self, bo, bl, dv, out_ap, out_key, add_col=None, add_key=None):
        tr = self.tr
        ri = self.rot("rs")
        RS = self.RS
        if add_col is None:
            tr.op("act", lambda e: e.activation(out=RS[0:dv, ri, :], in_=self.ps[bl][0:dv, :], func=AF.Ln), reads=[("ps", bl)], writes=[("rs", ri)])
        else:
            tr.op("act", lambda e: e.activation(out=RS[0:dv, ri, :], in_=self.ps[bl][0:dv, :], func=AF.Ln, bias=add_col),
                  reads=[("ps", bl), add_key], writes=[("rs", ri)])
        tr.op("act", lambda e: e.activation(out=RS[0:dv, ri, :], in_=RS[0:dv, ri, :], func=AF.Exp, scale=-1.0), reads=[("rs", ri)], writes=[("rs", ri)])
        tr.op("dve", lambda e: e.tensor_tensor(out=out_ap, in0=self.ps[bo][0:dv, :], in1=RS[0:dv, ri, :], op=ALU.mult),
              reads=[("ps", bo), ("rs", ri)], writes=[out_key])

    def mq_epi(self, col0):
        def epi(c, w, n, b):
            h = (c - col0) // 128
            self.norm_to(b, 128, 128.0, self.GV2[:, 17:18], "gv2c", self.MQ[:, h, n * 512:(n + 1) * 512], ("mq", h, n))
        return epi

    def mixer_conv(self, hf, L, S=None):
        tr = self.tr
        cc = self.cc
        if cc:
            hf = 0
        w_in = L["w_in"].rearrange("(kc p) f -> p kc f", p=128)
        tr.dma("sp", self.GV2[:, 17:18], L["mem_q_norm"], writes=["gv2c"])
        tr.dma("sp", self.GV2[:, 20:36], L["conv_w0"], writes=["cw"])
        tr.dma("sp", self.GV2[:, 36:52], L["conv_w1"], writes=["cw"])
        tr.dma("sp", self.GV2[:, 52:68], L["conv_w2"], writes=["cw"])
        self.rmsnorm_x(L["norm_mix"])
        self.linear_fm(self.src_ht, KC, w_in, [(6144, 512, [(i * 128, 128) for i in range(4)])], self.mq_epi(6144))
        Z = self.SCR[:, 0:T + 2]
        CV = self.SCR[:, T + 8:2 * T + 8]
        S1 = self.SG[:, :, :].rearrange("p a t -> p (a t)")
        ps, WR, HT = self.ps, self.WR, self.HT
        wout = L["w_out"]
        for G in range(4):
            ab = G % 2
            sl_gc = [self.wload(w_in[:, 4 * s:4 * s + 4, 2048 + 512 * G:2048 + 512 * (G + 1)]) for s in range(4)]
            sl_xt = [self.wload(w_in[:, 4 * s:4 * s + 4, 4096 + 512 * G:4096 + 512 * (G + 1)]) for s in range(4)]
            sl_gb = [self.wload(w_in[:, 4 * s:4 * s + 4, 512 * G:512 * (G + 1)]) for s in range(4)]
            for m in range(4):
                c = 4 * G + m
                w0 = self.GV2[:, 20 + c:21 + c]
                w1 = self.GV2[:, 36 + c:37 + c]
                w2 = self.GV2[:, 52 + c:53 + c]
                banks = {}
                for (nm, ss) in (("xt", sl_xt), ("gc", sl_gc), ("gb", sl_gb)):
                    for n in range(2):
                        b = self.next_ps(0, 8)
                        banks[(nm, n)] = b
                        for kc in range(KC):
                            tr.op("pe", lambda e, b=b, s=ss[kc // 4], kc=kc, m=m, n=n: e.matmul(
                                ps[b][:, :], lhsT=WR[:, s, kc % 4, m * 128:(m + 1) * 128], rhs=HT[:, kc, n * 512:(n + 1) * 512],
                                start=(kc == 0), stop=(kc == KC - 1)), reads=[("w", ss[kc // 4]), ("ht", kc, n)], writes=[("ps", b)])
                        if nm == "xt":
                            tr.op("act", lambda e, b=b, n=n: e.copy(out=S1[:, n * 512:(n + 1) * 512], in_=ps[b][:, :]),
                                  reads=[("ps", b)], writes=[("sg", n)])
                        if nm == "gc":
                            tr.op("dve", lambda e, b=b, n=n: e.tensor_tensor(out=Z[:, 2 + n * 512:2 + (n + 1) * 512], in0=S1[:, n * 512:(n + 1) * 512],
                                                                            in1=ps[b][:, :], op=ALU.mult),
                                  reads=[("ps", b), ("sg", n)], writes=[("z", n)])
                if hf == 0:
                    tr.op("dve", lambda e: e.memset(Z[:, 0:2], 0.0), writes=[("z", 2)])
                else:
                    tr.op("dve", lambda e, c=c: e.tensor_copy(out=Z[:, 0:2], in_=self.ZH[:, c, :]), reads=[("zh", c)], writes=[("z", 2)])
                zk = [("z", 0), ("z", 1), ("z", 2)]
                tr.op("act", lambda e, w2=w2: e.mul(out=CV, in_=Z[:, 2:T + 2], mul=w2), reads=zk + ["cw"], writes=["cv"])
                tr.op("dve", lambda e, w1=w1: e.scalar_tensor_tensor(out=CV, in0=Z[:, 1:T + 1], scalar=w1, in1=CV, op0=ALU.mult, op1=ALU.add),
                      reads=zk + ["cv", "cw"], writes=["cv"])
                tr.op("dve", lambda e, w0=w0: e.scalar_tensor_tensor(out=CV, in0=Z[:, 0:T], scalar=w0, in1=CV, op0=ALU.mult, op1=ALU.add),
                      reads=zk + ["cv", "cw"], writes=["cv"])
                if hf == 0:
                    tr.op("act", lambda e, c=c: e.copy(out=self.ZH[:, c, :], in_=Z[:, T:T + 2]), reads=zk, writes=[("zh", c)])
                if cc:
                    b0 = banks[("gb", 0)]
                    tr.op("act", lambda e, c=c, b0=b0: e.copy(out=self.GB01[:, c, :], in_=ps[b0][:, 0:2]), reads=[("ps", b0)], writes=[("gb01", c)])
                for n in range(2):
                    b = banks[("gb", n)]
                    tr.op("dve", lambda e, b=b, n=n, m=m, ab=ab: e.tensor_tensor(out=self.AT[ab][:, m, n * 512:(n + 1) * 512], in0=CV[:, n * 512:(n + 1) * 512],
                                                                                  in1=ps[b][:, :], op=ALU.mult),
                          reads=[("ps", b), "cv"], writes=[("cat", 4 * ab + m, n)])
            self.outproj_group(wout[512 * G:512 * (G + 1), :].rearrange("(kt p) d -> p kt d", p=128),
                               [(self.AT[ab][:, m, :], ("cat", 4 * ab + m)) for m in range(4)], 1.0)
        if cc:
            zhk = [("zh", c) for c in range(16)]
            tr.dma("sp", S["ZHs"], self.ZH[:, :, :].rearrange("p c t -> p (c t)"), reads=zhk, writes=["zhs"])
            tr.allgather(S["ZHs"], S["ZHr"], reads=["zhs"], writes=["zhr"])
            tr.dma("sp", self.ZHP[:, :, :].rearrange("p c t -> p (c t)"), S["ZHr"][0:128, :], reads=["zhr"], writes=["zhp"])
            W0, W1 = self.GV2[:, 20:36], self.GV2[:, 36:52]
            DY, DYT, GB01, ZHP = self.DY, self.DYT, self.GB01, self.ZHP
            gk = [("gb01", c) for c in range(16)]
            tr.op("dve", lambda e: e.tensor_tensor(out=DY[:, :, 0], in0=W0, in1=ZHP[:, :, 0], op=ALU.mult), reads=["zhp", "cw"], writes=["dy0"])
            tr.op("dve", lambda e: e.tensor_tensor(out=DYT[:, :], in0=W1, in1=ZHP[:, :, 1], op=ALU.mult), reads=["zhp", "cw"], writes=["dyt"])
            tr.op("dve", lambda e: e.tensor_tensor(out=DY[:, :, 0], in0=DY[:, :, 0], in1=DYT[:, :], op=ALU.add), reads=["dy0", "dyt"], writes=["dy0"])
            tr.op("dve", lambda e: e.tensor_tensor(out=DY[:, :, 0], in0=DY[:, :, 0], in1=GB01[:, :, 0], op=ALU.mult), reads=["dy0"] + gk, writes=["dy0"])
            tr.op("dve", lambda e: e.tensor_tensor(out=DY[:, :, 1], in0=W0, in1=ZHP[:, :, 1], op=ALU.mult), reads=["zhp", "cw"], writes=["dy1"])
            tr.op("dve", lambda e: e.tensor_tensor(out=DY[:, :, 1], in0=DY[:, :, 1], in1=GB01[:, :, 1], op=ALU.mult), reads=["dy1"] + gk, writes=["dy1"])
            tr.op("dve", lambda e: e.tensor_scalar(out=self.DYB[:, :, :], in0=DY[:, :, :], scalar1=self.PCOL[:, 2:3], scalar2=None, op0=ALU.mult),
                  reads=["dy0", "dy1", "pcol"], writes=[("dyb",)])
            for G in range(4):
                self.outproj_group(wout[512 * G:512 * (G + 1), :].rearrange("(kt p) d -> p kt d", p=128),
                                   [(self.DYB[:, 4 * G + m, :], ("dyb",)) for m in range(4)], 1.0, small=2)
        MK, MV = self.mem_kv(L["memT"], L["norm_mem"], L["mem_w_kv"], L["mem_k_norm"])
        self.mem_attn(MK, MV, 0)
        self.barrier(self.htkeys() + self.hdkeys() + [("memn", k) for k in range(KC)] + [("mk", h) for h in range(4)] + [("mv", j) for j in range(2)])
        self.outproj_group(wout[2048:2560, :].rearrange("(kt p) d -> p kt d", p=128),
                           [(self.AT[0][:, m, :], ("cat", m)) for m in range(4)], 1.0)

    def lat_norm(self, gcol0, nsz):
        tr = self.tr
        A32 = self.AT32
        for n in range(2):
            b = self.next_ps(4, 8)
            for c in range(4):
                si = self.rot("sg")
                tr.op("act", lambda e, c=c, n=n, si=si: e.activation(out=self.SG[:, si, :], in_=A32[:, 2 * c + n, :], func=AF.Square),
                      reads=[("a32", c, n)], writes=[("sg", si)])
                tr.op("pe", lambda e, c=c, si=si, b=b: e.matmul(self.ps[b][:, :], lhsT=self.ONES[:, :], rhs=self.SG[:, si, :],
                                                                start=(c == 0), stop=(c == 3)),
                      reads=[("sg", si), "ones"], writes=[("ps", b)])
            ri = self.rot("rs")
            RS = self.RS
            tr.op("act", lambda e, b=b, ri=ri: e.activation(out=RS[:, ri, :], in_=self.ps[b][:, :], func=AF.Ln, scale=1.0 / nsz,
                                                            bias=self.EPSB[:, 0:1]), reads=[("ps", b), "eps"], writes=[("rs", ri)])
            tr.op("act", lambda e, ri=ri: e.activation(out=RS[:, ri, :], in_=RS[:, ri, :], func=AF.Exp, scale=-0.5), reads=[("rs", ri)], writes=[("rs", ri)])
            for c in range(4):
                tr.op("dve", lambda e, c=c, n=n, ri=ri: e.scalar_tensor_tensor(
                    out=self.LAT[:, c, n * 512:(n + 1) * 512], in0=A32[:, 2 * c + n, :], scalar=self.GV2[:, gcol0 + c:gcol0 + c + 1],
                    in1=RS[:, ri, :], op0=ALU.mult, op1=ALU.mult),
                    reads=[("a32", c, n), ("rs", ri), "gvm"], writes=[("lat", c, n)])

    def raw_epi(self, col0):
        def epi(c, w, n, b):
            cc = (c - col0) // 128
            self.tr.op("act", lambda e: e.copy(out=self.AT32[:, 2 * cc + n, :], in_=self.ps[b][:, :]),
                       reads=[("ps", b)], writes=[("a32", cc, n)])
        return epi

    def stage_out(self, w, make, dst, dkey):
        pi = self.rot("pt", 4)
        make(self.PT[0:w, pi, :], ("pt", pi))
        self.tr.dma("sp", dst, self.PT[0:w, pi, :], reads=[("pt", pi)], writes=[dkey])

    def run_pipeline(self, tasks, la=2):
        n = len(tasks)
        for i in range(n + la):
            if i < n:
                tasks[i][0]()
                tasks[i][1]()
            if i - la >= 0:
                tasks[i - la][2]()

    def attn_tasks(self, hf, members, dvm, scale, out_ap_fn, out_key_fn, post=None, clamp=False):
        tr = self.tr
        ps, PT = self.ps, self.PT
        tasks = []
        nm = len(members)
        dtot = members[-1]["po"] + dvm
        for n in range(2):
            sl = slice(n * 512, (n + 1) * 512)
            nj = hf * 8 + 4 * (n + 1)
            st = {}
            for mi, mem in enumerate(members):
                qparts, kparts, vt, vkey, bias_fn, pre, po = mem["q"], mem["k"], mem["v"], mem["vkey"], mem["bias_fn"], mem["pre"], mem["po"]
                np_ = len(qparts)
                for j in range(nj):
                    t = {}
                    r = j - (hf * 8 + 4 * n)

                    def s1(t=t, st=st, j=j, n=n, sl=sl, mi=mi, pre=pre, qparts=qparts, kparts=kparts, np_=np_):
                        if j == 0 and n == 0 and pre is not None:
                            pre()
                        if j == 0 and mi == 0:
                            st["bo"] = self.next_ps(4, 6)
                            st["bl"] = self.next_ps(6, 8)
                        b = t["b"] = self.next_ps(0, 4)
                        for i in range(np_):
                            qa, qk = qparts[i]
                            ka, kk = kparts[i]
                            tr.op("pe", lambda e, b=b, i=i, qa=qa, ka=ka: e.matmul(
                                ps[b][:, :], lhsT=ka[:, j * 128:(j + 1) * 128], rhs=qa[:, sl], start=(i == 0), stop=(i == np_ - 1)),
                                reads=[qk, kk], writes=[("ps", b)])

                    def s2(t=t, j=j, r=r, bias_fn=bias_fn):
                        b = t["b"]
                        pi = t["pi"] = self.rot("pt", 4)
                        if bias_fn is None:
                            ba, bk = self.NEGC[:, 0:1], "negc"
                        else:
                            ba, bk = bias_fn(j)
                        if clamp and r >= 0:
                            ti = self.rot("t1")
                            tr.op("dve", lambda e: e.tensor_scalar(out=self.T1[:, ti, :], in0=ps[b][:, :], scalar1=ba, scalar2=60.0,
                                                                   op0=ALU.add, op1=ALU.min),
                                  reads=[("ps", b), bk], writes=[("t1", ti)])
                            tr.op("act", lambda e: e.activation(out=PT[:, pi, :], in_=self.T1[:, ti, :], func=AF.Exp),
                                  reads=[("t1", ti)], writes=[("pt", pi)])
                        else:
                            tr.op("act", lambda e: e.activation(out=PT[:, pi, :], in_=ps[b][:, :], func=AF.Exp, scale=scale, bias=ba),
                                  reads=[("ps", b), bk], writes=[("pt", pi)])
                        if r >= 0:
                            tr.op("dve", lambda e: e.tensor_tensor(out=PT[:, pi, :], in0=PT[:, pi, :], in1=self.CM[:, r, :], op=ALU.mult),
                                  reads=[("pt", pi), "cm"], writes=[("pt", pi)])

                    def s3(t=t, st=st, j=j, n=n, nj=nj, mi=mi, vt=vt, vkey=vkey, po=po):
                        pi, bo, bl = t["pi"], st["bo"], st["bl"]
                        tr.op("pe", lambda e: e.matmul(ps[bo][po:po + dvm, :], lhsT=vt[:, j, :], rhs=PT[:, pi, :], start=(j == 0), stop=(j == nj - 1)),
                              reads=[vkey, ("pt", pi)], writes=[("ps", bo)])
                        tr.op("pe", lambda e: e.matmul(ps[bl][po:po + dvm, :], lhsT=self.ONESB[:, 0:dvm], rhs=PT[:, pi, :], start=(j == 0), stop=(j == nj - 1)),
                              reads=["onesb", ("pt", pi)], writes=[("ps", bl)])
                        if j == nj - 1 and mi == nm - 1:
                            self.attn_finish(bo, bl, dtot, out_ap_fn(n), out_key_fn(n))
                            if n == 1 and post is not None:
                                post()
                    tasks.append((s1, s2, s3))
        return tasks

    def mem_block(self, L, wout):
        MK, MV = self.mem_kv(L["memT"], L["norm_mem"], L["mem_w_kv"], L["mem_k_norm"])
        self.mem_attn(MK, MV, 0)
        self.barrier(self.htkeys() + self.hdkeys() + [("memn", k) for k in range(KC)] + [("mk", h) for h in range(4)] + [("mv", j) for j in range(2)])
        self.outproj_group(wout[2048:2560, :].rearrange("(kt p) d -> p kt d", p=128),
                           [(self.AT[0][:, m, :], ("cat", m)) for m in range(4)], 1.0)

    def mixer_mla(self, hf, L, S):
        tr = self.tr
        ps, WR, HT, GV2 = self.ps, self.WR, self.HT, self.GV2
        cc = self.cc
        c0 = 0 if cc else hf * T
        if cc:
            hf = 1
        w_in = L["w_in"].rearrange("(kc p) f -> p kc f", p=128)
        tr.dma("sp", GV2[:, 17:18], L["mem_q_norm"], writes=["gv2c"])
        tr.dma("sp", GV2[:, 20:24], L["q_a_norm"], writes=["gvm"])
        tr.dma("sp", GV2[:, 24:28], L["kv_a_norm"], writes=["gvm"])
        tr.dma("sp", GV2[:, 28:32], L["qk_cols"], writes=["gvm"])
        tr.dma("sp", self.SCR[0:64, 0:T], L["cos"][:, c0:c0 + T], writes=["cs"])
        tr.dma("sp", self.SCR[0:64, T:2 * T], L["sin"][:, c0:c0 + T], writes=["cs"])
        self.rmsnorm_x(L["norm_mix"])
        self.linear_fm(self.src_ht, KC, w_in, [(1088, 512, [(i * 128, 128) for i in range(4)])], self.mq_epi(1088))
        def krope_epi(c, w, n, b):
            self.stage_out(64, lambda o, k: self.rope_to(b, GV2[0:64, 31:32], "gvm", n * 512, o, k),
                           S["KR"][:, c0 + n * 512:c0 + (n + 1) * 512], ("krd", hf, n))
        self.linear_fm(self.src_ht, KC, w_in, [(1024, 64, [(0, 64)])], krope_epi)
        a32keys = [("a32", c, n) for c in range(4) for n in range(2)]
        self.barrier(self.atkeys() + a32keys)
        self.linear_fm(self.src_ht, KC, w_in, [(0, 512, [(i * 128, 128) for i in range(4)])], self.raw_epi(0))
        self.lat_norm(20, 512.0)
        wq = L["w_q_b"].rearrange("(kc p) f -> p kc f", p=128)
        def q_epi(c, w, n, b):
            h = c // 192
            if c % 192 == 0:
                self.stage_out(128, lambda o, k: self.norm_to(b, 128, 128.0, GV2[:, 28:29], "gvm", o, k),
                               S["QN"][h, :, n * 512:(n + 1) * 512], ("qnd", h, n))
            else:
                self.stage_out(64, lambda o, k: self.rope_to(b, GV2[0:64, 29:30], "gvm", n * 512, o, k),
                               S["QR"][h, :, n * 512:(n + 1) * 512], ("qrd", h, n))
        self.linear_fm(self.src_lat, 4, wq, [(384 * g, 384, [(0, 128), (128, 64), (192, 128), (320, 64)]) for g in range(8)], q_epi)
        self.linear_fm(self.src_ht, KC, w_in, [(512, 512, [(i * 128, 128) for i in range(4)])], self.raw_epi(512))
        self.lat_norm(24, 512.0)
        self.barrier(self.atkeys() + a32keys)
        wkv = L["w_kv_b"].rearrange("(kc p) f -> p kc f", p=128)
        for g in range(8):
            s = self.wload(wkv[:, 0:4, 512 * g:512 * (g + 1)])
            for hh in range(2):
                h = 2 * g + hh
                for n in range(2):
                    b = self.next_ps(0, 4)
                    for kc in range(4):
                        tr.op("pe", lambda e, b=b, kc=kc, hh=hh, n=n, s=s: e.matmul(
                            ps[b][:, :], lhsT=WR[:, s, kc, hh * 256:hh * 256 + 128], rhs=self.LAT[:, kc, n * 512:(n + 1) * 512],
                            start=(kc == 0), stop=(kc == 3)), reads=[("w", s), ("lat", kc, n)], writes=[("ps", b)])
                    self.stage_out(128, lambda o, k, b=b: self.norm_to(b, 128, 128.0, GV2[:, 30:31], "gvm", o, k),
                                   S["KN"][h, :, c0 + n * 512:c0 + (n + 1) * 512], ("knd", h, hf, n))
            for tt in range(8):
                b = self.next_ps(0, 4)
                for kc in range(4):
                    tr.op("pe", lambda e, b=b, kc=kc, tt=tt, s=s: e.matmul(
                        ps[b][:, :], lhsT=self.LAT[:, kc, tt * 128:(tt + 1) * 128], rhs=WR[:, s, kc, :],
                        start=(kc == 0), stop=(kc == 3)), reads=[("w", s), ("lat", kc, tt // 4)], writes=[("ps", b)])
                pi = self.rot("pt", 4)
                tr.op("act", lambda e, b=b, pi=pi: e.copy(
                    out=self.PT[:, pi, 0:256].rearrange("p (h d) -> p h d", d=128),
                    in_=ps[b][:, :].rearrange("p (h two d) -> p h two d", two=2, d=128)[:, :, 1, :]),
                    reads=[("ps", b)], writes=[("pt", pi)])
                tr.dma("sp", S["V"][c0 + tt * 128:c0 + (tt + 1) * 128, 256 * g:256 * (g + 1)], self.PT[:, pi, 0:256],
                       reads=[("pt", pi)], writes=[("vd", g, hf, tt)])
        wout = L["w_out"]
        if cc:
            self.allgather_rows(S["KNo"], S["KNr"], 1024, [("knd", h2, 1, n2) for h2 in range(16) for n2 in range(2)], "knr")
            tr.allgather(S["KR"], S["KRr"], reads=[("krd", 1, n2) for n2 in range(2)], writes=["krr"])
            self.allgather_rows(S["V"], S["Vr"], 512, [("vd", g2, 1, tt) for g2 in range(8) for tt in range(8)], "vr")
        self.mem_block(L, wout)
        hb = HT[:, :, :].rearrange("p a t -> p (a t)")
        Sk = (hf + 1) * T
        KRt = hb[0:64, 12288:12288 + SEQ]
        if cc:
            tr.dma("sp", KRt[:, 0:T], S["KRr"][0:64, :], reads=["krr"], writes=[("hd", "kr")])
            tr.dma("sp", KRt[:, T:2 * T], S["KR"], reads=[("krd", 1, n2) for n2 in range(2)], writes=[("hd", "kr")])
        else:
            krk = [("krd", h2, n2) for h2 in range(hf + 1) for n2 in range(2)]
            tr.dma("sp", KRt[:, 0:Sk], S["KR"][:, 0:Sk], reads=krk, writes=[("hd", "kr")])
        scale = 1.0 / math.sqrt(192.0)
        tasks = []
        for h in range(16):
            st_ = h % 2
            base = st_ * 6144
            QNt = hb[:, base:base + T]
            QRt = hb[0:64, base + T:base + 2 * T]
            KNt = hb[:, base + 2 * T:base + 2 * T + SEQ]
            Vt = hb[:, base + 4 * T:base + 4 * T + SEQ].rearrange("p (j c) -> p j c", c=128)
            ab = (h // 4) % 2
            m = h % 4

            def pre(h=h, st_=st_, QNt=QNt, QRt=QRt, KNt=KNt, Vt=Vt):
                tr.dma("sp", QNt, S["QN"][h], reads=[("qnd", h, n2) for n2 in range(2)], writes=[("hd", st_, "qn")])
                tr.dma("sp", QRt, S["QR"][h], reads=[("qrd", h, n2) for n2 in range(2)], writes=[("hd", st_, "qr")])
                if cc:
                    tr.dma("sp", KNt[:, 0:T], self.r0rows(S["KNr"], 1024, 128 * h, 128), reads=[("knr", h // 8)], writes=[("hd", st_, "kn")])
                    tr.dma("sp", KNt[:, T:2 * T], S["KN"][h], reads=[("knd", h, 1, n2) for n2 in range(2)], writes=[("hd", st_, "kn")])
                    for i2 in range(2):
                        tr.dma("sp", Vt[:, 4 * i2:4 * i2 + 4, :],
                               self.r0rows(S["Vr"], 512, 512 * i2, 512)[:, 128 * h:128 * (h + 1)].rearrange("(j p) c -> p j c", p=128),
                               reads=[("vr", i2)], writes=[("hd", st_, "v")])
                    tr.dma("sp", Vt[:, 8:16, :], S["V"][:, 128 * h:128 * (h + 1)].rearrange("(j p) c -> p j c", p=128),
                           reads=[("vd", h // 2, 1, tt) for tt in range(8)], writes=[("hd", st_, "v")])
                    return
                tr.dma("sp", KNt[:, 0:Sk], S["KN"][h, :, 0:Sk], reads=[("knd", h, h2, n2) for h2 in range(hf + 1) for n2 in range(2)],
                       writes=[("hd", st_, "kn")])
                tr.dma("sp", Vt[:, 0:Sk // 128, :], S["V"][0:Sk, 128 * h:128 * (h + 1)].rearrange("(j p) c -> p j c", p=128),
                       reads=[("vd", h // 2, h2, tt) for h2 in range(hf + 1) for tt in range(8)], writes=[("hd", st_, "v")])

            post = None
            if m == 3:
                def post(h=h, ab=ab):
                    g4 = h // 4
                    self.outproj_group(wout[512 * g4:512 * (g4 + 1), :].rearrange("(kt p) d -> p kt d", p=128),
                                       [(self.AT[ab][:, mm, :], ("cat", 4 * ab + mm)) for mm in range(4)], 1.0)
            mem_ = dict(q=[(QNt, ("hd", st_, "qn")), (QRt, ("hd", st_, "qr"))], k=[(KNt, ("hd", st_, "kn")), (KRt, ("hd", "kr"))],
                        v=Vt, vkey=("hd", st_, "v"), pre=pre, po=0,
                        bias_fn=(lambda j: (self.PCOL[:, 0:1], "pcol") if j < 8 else (self.NEGC[:, 0:1], "negc")) if cc else None)
            tasks += self.attn_tasks(hf, [mem_], 128, scale,
                                     lambda n, ab=ab, m=m: self.AT[ab][:, m, n * 512:(n + 1) * 512],
                                     lambda n, ab=ab, m=m: ("cat", 4 * ab + m, n), post=post)
        self.run_pipeline(tasks)
        self.barrier(self.htkeys() + self.hdkeys())

    def mixer_swa(self, hf, L, S):
        tr = self.tr
        ps, WR, HT, GV2, PT = self.ps, self.WR, self.HT, self.GV2, self.PT
        cc = self.cc
        c0 = 0 if cc else hf * T
        if cc:
            hf = 1
        W = 1152
        w_in = L["w_in"].rearrange("(kc p) f -> p kc f", p=128)
        tr.dma("sp", GV2[:, 17:18], L["mem_q_norm"], writes=["gv2c"])
        tr.dma("sp", GV2[:, 28:30], L["qk_cols"], writes=["gvm"])
        SK = GV2[:, 32:48]
        tr.dma("sp", SK, L["sinks_pair"], writes=["sk"])
        tr.op("act", lambda e: e.activation(out=SK, in_=SK, func=AF.Exp, bias=self.NEGC[:, 0:1]), reads=["sk", "negc"], writes=["sk"])
        if hf == 0 or cc:
            GROW = self.SCR[0:32, 0:W]
            tr.op("dve", lambda e: e.memset(GROW, NEG), writes=["grow"])
            tr.dma("sp", self.T1[0:32, 0, 0:32], L["rel_bias"], writes=[("t1", 0)])
            tr.dma("sp", self.T1[0:32, 1, 0:128], L["onehot"], writes=[("t1", 1)])
            b = self.next_ps(0, 4)
            tr.op("pe", lambda e: e.matmul(ps[b][0:32, 0:128], lhsT=self.T1[0:32, 0, 0:32], rhs=self.T1[0:32, 1, 0:128], start=True, stop=True),
                  reads=[("t1", 0), ("t1", 1)], writes=[("ps", b)])
            tr.op("act", lambda e: e.copy(out=GROW[:, 511:639], in_=ps[b][0:32, 0:128]), reads=[("ps", b), "grow"], writes=["grow"])
            for q4 in range(8):
                tr.dma("sp", S["GD"][:, 16 * q4:16 * (q4 + 1), :], GROW.unsqueeze(1).broadcast_to([32, 16, W]), reads=["grow"], writes=[("gd", q4)])
        self.rmsnorm_x(L["norm_mix"])
        self.linear_fm(self.src_ht, KC, w_in, [(2560, 512, [(i * 128, 128) for i in range(4)])], self.mq_epi(2560))
        def q_epi(c, w, n, b):
            cc = c // 128
            pi = self.rot("pt", 4)
            self.norm_to(b, 128, 64.0, GV2[:, 28:29], "gvm", PT[:, pi, :], ("pt", pi), blk64=True)
            for hh in range(2):
                tr.dma("sp", S["Q"][2 * cc + hh, :, n * 512:(n + 1) * 512], PT[64 * hh:64 * hh + 64, pi, :], reads=[("pt", pi)],
                       writes=[("qd", 2 * cc + hh, n)])
        self.linear_fm(self.src_ht, KC, w_in, [(512 * g, 512, [(i * 128, 128) for i in range(4)]) for g in range(4)], q_epi)
        def k_epi(c, w, n, b):
            cc = (c - 2048) // 128
            pi = self.rot("pt", 4)
            self.norm_to(b, 128, 64.0, GV2[:, 29:30], "gvm", PT[:, pi, :], ("pt", pi), blk64=True)
            for hh in range(2):
                tr.dma("sp", S["K"][2 * cc + hh, :, c0 + n * 512:c0 + (n + 1) * 512], PT[64 * hh:64 * hh + 64, pi, :], reads=[("pt", pi)],
                       writes=[("kd", 2 * cc + hh, hf, n)])
                if self.cc and n == 1:
                    tr.dma("sp", S["KHs"][64 * (2 * cc + hh):64 * (2 * cc + hh + 1), :], PT[64 * hh:64 * hh + 64, pi, 384:512], reads=[("pt", pi)],
                           writes=[("khs", 2 * cc + hh)])
        self.linear_fm(self.src_ht, KC, w_in, [(2048, 256, [(0, 128), (128, 128)])], k_epi)
        slots = [self.wload(w_in[:, 4 * s:4 * s + 4, 2304:2560]) for s in range(4)]
        for tt in range(8):
            b = self.next_ps(0, 4)
            for kc in range(KC):
                tr.op("pe", lambda e, b=b, kc=kc, tt=tt, s=slots[kc // 4]: e.matmul(
                    ps[b][:, 0:256], lhsT=HT[:, kc, tt * 128:(tt + 1) * 128], rhs=WR[:, s, kc % 4, 0:256],
                    start=(kc == 0), stop=(kc == KC - 1)), reads=[("w", slots[kc // 4]), ("ht", kc, tt // 4)], writes=[("ps", b)])
            pi = self.rot("pt", 4)
            tr.op("act", lambda e, b=b, pi=pi: e.copy(out=PT[:, pi, 0:256], in_=ps[b][:, 0:256]), reads=[("ps", b)], writes=[("pt", pi)])
            tr.dma("sp", S["V"][c0 + tt * 128:c0 + (tt + 1) * 128, :], PT[:, pi, 0:256], reads=[("pt", pi)], writes=[("vd", hf, tt)])
            if cc and tt == 7:
                tr.dma("sp", S["VHs"], PT[:, pi, 0:256], reads=[("pt", pi)], writes=["vhs"])
        if cc:
            tr.allgather(S["KHs"], S["KHr"], reads=[("khs", g2) for g2 in range(4)], writes=["khr"])
            tr.allgather(S["VHs"], S["VHr"], reads=["vhs"], writes=["vhr"])
        wout = L["w_out"]
        self.mem_block(L, wout)
        hb = HT[:, :, :].rearrange("p a t -> p (a t)")
        h32 = HT[:, :, :].bitcast(F32).rearrange("p a t -> p (a t)")
        Sk = (hf + 1) * T
        scale = 1.0 / 8.0
        jbase = hf * 8
        tasks = []
        for pr in range(16):
            g = (2 * pr) // 8
            pst = pr % 2
            gs_ = g % 2
            Kt = hb[0:64, 4096 + 2048 * gs_:4096 + 2048 * (gs_ + 1)]
            Vt = hb[:, 8192 + 1024 * gs_:8192 + 1024 * (gs_ + 1)].rearrange("p (j c) -> p j c", c=64)
            ab = (pr // 4) % 2
            m = pr % 4
            pres, Qts, BBs, sks = [], [], [], []
            for hh in range(2):
                h = 2 * pr + hh
                sk = 2 * pst + hh
                Qt = hb[0:64, 1024 * sk:1024 * (sk + 1)]
                BB = h32[:, 5120 + 256 * sk:5120 + 256 * (sk + 1)]

                def pre(h=h, g=g, sk=sk, gs_=gs_, Qt=Qt, Kt=Kt, Vt=Vt, BB=BB):
                    tr.dma("sp", Qt, S["Q"][h], reads=[("qd", h, n2) for n2 in range(2)], writes=[("hd", sk, "q")])
                    if h % 8 == 0 and cc:
                        tr.dma("sp", Kt[:, 0:128], S["KHr"][64 * g:64 * (g + 1), :], reads=["khr"], writes=[("hd", gs_, "k")])
                        tr.dma("sp", Kt[:, 128:128 + T], S["K"][g], reads=[("kd", g, 1, n2) for n2 in range(2)], writes=[("hd", gs_, "k")])
                        tr.dma("sp", Vt[:, 0, :], S["VHr"][0:128, 64 * g:64 * (g + 1)], reads=["vhr"], writes=[("hd", gs_, "v")])
                        tr.dma("sp", Vt[:, 1:9, :], S["V"][:, 64 * g:64 * (g + 1)].rearrange("(j p) c -> p j c", p=128),
                               reads=[("vd", 1, tt) for tt in range(8)], writes=[("hd", gs_, "v")])
                    elif h % 8 == 0:
                        tr.dma("sp", Kt[:, 0:Sk], S["K"][g, :, 0:Sk], reads=[("kd", g, h2, n2) for h2 in range(hf + 1) for n2 in range(2)],
                               writes=[("hd", gs_, "k")])
                        tr.dma("sp", Vt[:, 0:Sk // 128, :], S["V"][0:Sk, 64 * g:64 * (g + 1)].rearrange("(j p) c -> p j c", p=128),
                               reads=[("vd", h2, tt) for h2 in range(hf + 1) for tt in range(8)], writes=[("hd", gs_, "v")])
                    src = bass.AP(S["GDh"], h * 128 * W + 511, [[W - 1, 128], [1, 256]])
                    tr.dma("sp", BB, src, reads=[("gd", q4) for q4 in range(8)], writes=[("hd", "b", sk)])
                pres.append(pre); Qts.append(Qt); BBs.append(BB); sks.append(sk)

            for n in range(2):
                js = [((jbase + 4 * n + r) - (7 if cc else 0), r) for r in range(-1, 4) if jbase + 4 * n + r >= 0]
                has_prev = js[0][1] == -1
                st = {}
                for hh in range(2):
                    for idx, (j, r) in enumerate(js):
                        lo = max(0, 128 * r)
                        hi = min(512, 128 * r + 256)
                        w = hi - lo
                        boff = lo - 128 * r
                        t = {}
                        last = (idx == len(js) - 1) and hh == 1

                        def s1(t=t, st=st, idx=idx, j=j, n=n, lo=lo, hi=hi, w=w, pre=pres[hh], gs_=gs_, sk=sks[hh], Kt=Kt, Qt=Qts[hh], hh=hh):
                            if idx == 0 and n == 0:
                                pre()
                            if idx == 0 and hh == 0:
                                st["bo"] = self.next_ps(4, 6)
                                st["bl"] = self.next_ps(6, 8)
                            b = t["b"] = self.next_ps(0, 4)
                            tr.op("pe", lambda e: e.matmul(ps[b][:, 0:w], lhsT=Kt[:, j * 128:(j + 1) * 128], rhs=Qt[:, n * 512 + lo:n * 512 + hi],
                                                           start=True, stop=True),
                                  reads=[("hd", gs_, "k"), ("hd", sk, "q")], writes=[("ps", b)])

                        def s2(t=t, w=w, boff=boff, BB=BBs[hh], sk=sks[hh], jj=j):
                            b = t["b"]
                            ti = self.rot("t1")
                            tr.op("dve", lambda e: e.scalar_tensor_tensor(out=self.T1[:, ti, 0:w], in0=ps[b][:, 0:w], scalar=scale,
                                                                          in1=BB[:, boff:boff + w], op0=ALU.mult, op1=ALU.add),
                                  reads=[("ps", b), ("hd", "b", sk)], writes=[("t1", ti)])
                            pi = t["pi"] = self.rot("pt", 4)
                            ebias = self.PCOL[:, 0:1] if (cc and jj == 0) else self.NEGC[:, 0:1]
                            tr.op("act", lambda e: e.activation(out=PT[:, pi, 0:w], in_=self.T1[:, ti, 0:w], func=AF.Exp, bias=ebias),
                                  reads=[("t1", ti), "negc", "pcol"], writes=[("pt", pi)])

                        def s3(t=t, st=st, j=j, n=n, lo=lo, hi=hi, last=last, gs_=gs_, Vt=Vt, pr=pr, ab=ab, m=m, r=r, has_prev=has_prev, po=64 * hh):
                            pi, bo, bl = t["pi"], st["bo"], st["bl"]
                            for c in range(lo // 128, hi // 128):
                                first = (c == r + 1) or (c == 0 and not has_prev)
                                stp = (c == r)
                                pof = 128 * c - lo
                                tr.op("pe", lambda e, c=c, first=first, stp=stp, pof=pof: e.matmul(
                                    ps[bo][po:po + 64, 128 * c:128 * (c + 1)], lhsT=Vt[:, j, :], rhs=PT[:, pi, pof:pof + 128], start=first, stop=stp),
                                    reads=[("hd", gs_, "v"), ("pt", pi)], writes=[("ps", bo)])
                                tr.op("pe", lambda e, c=c, first=first, stp=stp, pof=pof: e.matmul(
                                    ps[bl][po:po + 64, 128 * c:128 * (c + 1)], lhsT=self.ONESB[:, 0:64], rhs=PT[:, pi, pof:pof + 128], start=first, stop=stp),
                                    reads=["onesb", ("pt", pi)], writes=[("ps", bl)])
                            if last:
                                sl = slice(n * 512, (n + 1) * 512)
                                self.attn_finish(bo, bl, 128, self.AT[ab][:, m, sl], ("cat", 4 * ab + m, n), add_col=SK[:, pr:pr + 1], add_key="sk")
                                if n == 1 and m == 3:
                                    g4 = pr // 4
                                    self.outproj_group(wout[512 * g4:512 * (g4 + 1), :].rearrange("(kt p) d -> p kt d", p=128),
                                                       [(self.AT[ab][:, mm, :], ("cat", 4 * ab + mm)) for mm in range(4)], 1.0)
                        tasks.append((s1, s2, s3))
        self.run_pipeline(tasks)
        self.barrier(self.htkeys() + self.hdkeys())

    def mixer_fox(self, hf, L, S):
        tr = self.tr
        ps, WR, HT, GV2, PT, T1 = self.ps, self.WR, self.HT, self.GV2, self.PT, self.T1
        cc = self.cc
        c0 = 0 if cc else hf * T
        if cc:
            hf = 1
        w_in = L["w_in"].rearrange("(kc p) f -> p kc f", p=128)
        tr.dma("sp", GV2[:, 17:18], L["mem_q_norm"], writes=["gv2c"])
        tr.dma("sp", GV2[:, 28:30], L["qk_cols"], writes=["gvm"])
        tr.op("dve", lambda e: e.tensor_scalar(out=GV2[:, 28:29], in0=GV2[:, 28:29], scalar1=1.0 / 8.0, scalar2=None, op0=ALU.mult),
              reads=["gvm"], writes=["gvm"])
        BF = GV2[:, 32:64]
        tr.dma("sp", BF, L["bf_bc"], writes=["bf"])
        IDN = self.SCR[:, 0:128]
        TRIU = self.SCR[:, 128:256]
        NLF = self.SCR[:, 256:512].rearrange("p (j c) -> p j c", c=32)
        CSN = self.SCR[:, 512:768].rearrange("p (j c) -> p j c", c=32)
        tr.dma("sp", IDN, L["ident"], writes=["idn"])
        tr.dma("sp", TRIU, L["triu"], writes=["triu"])
        self.rmsnorm_x(L["norm_mix"])
        self.linear_fm(self.src_ht, KC, w_in, [(6176, 512, [(i * 128, 128) for i in range(4)])], self.mq_epi(6176))
        slots = [self.wload(w_in[:, 4 * s:4 * s + 4, 6144:6176]) for s in range(4)]
        for tt in range(8):
            b = self.next_ps(0, 4)
            for kc in range(KC):
                tr.op("pe", lambda e, b=b, kc=kc, tt=tt, s=slots[kc // 4]: e.matmul(
                    ps[b][:, 0:32], lhsT=HT[:, kc, tt * 128:(tt + 1) * 128], rhs=WR[:, s, kc % 4, 0:32],
                    start=(kc == 0), stop=(kc == KC - 1)), reads=[("w", slots[kc // 4]), ("ht", kc, tt // 4)], writes=[("ps", b)])
            tr.op("dve", lambda e, b=b, tt=tt: e.tensor_tensor(out=NLF[:, tt, :], in0=ps[b][:, 0:32], in1=BF, op=ALU.add),
                  reads=[("ps", b), "bf"], writes=[("nlf", tt)])
            tr.op("act", lambda e, tt=tt: e.activation(out=NLF[:, tt, :], in_=NLF[:, tt, :], func=AF.Exp, scale=-1.0), reads=[("nlf", tt)], writes=[("nlf", tt)])
            tr.op("act", lambda e, tt=tt: e.activation(out=NLF[:, tt, :], in_=NLF[:, tt, :], func=AF.Ln, bias=self.ONES[:, 0:1]),
                  reads=[("nlf", tt), "ones"], writes=[("nlf", tt)])
        for tt in range(8):
            b = self.next_ps(0, 4)
            for ts in range(tt + 1):
                tri = TRIU if ts == tt else self.ONES[:, :]
                tr.op("pe", lambda e, b=b, ts=ts, tt=tt, tri=tri: e.matmul(ps[b][:, 0:32], lhsT=tri, rhs=NLF[:, ts, :], start=(ts == 0), stop=(ts == tt)),
                      reads=[("nlf", ts), "triu", "ones"], writes=[("ps", b)])
            tr.op("act", lambda e, b=b, tt=tt: e.copy(out=CSN[:, tt, :], in_=ps[b][:, 0:32]), reads=[("ps", b)], writes=[("csn", tt)])
            tr.dma("sp", S["CS"][c0 + tt * 128:c0 + (tt + 1) * 128, :], CSN[:, tt, :], reads=[("csn", tt)], writes=[("csd", hf, tt)])
        CQ = T1[0:32, :, :].rearrange("p a t -> p (a t)")
        for n in range(2):
            b = self.next_ps(0, 4)
            for t4 in range(4):
                tt = 4 * n + t4
                tr.op("pe", lambda e, b=b, tt=tt, t4=t4: e.matmul(ps[b][0:32, t4 * 128:(t4 + 1) * 128], lhsT=CSN[:, tt, :], rhs=IDN, start=True, stop=True),
                      reads=[("csn", tt), "idn"], writes=[("ps", b)])
            tr.op("act", lambda e, b=b, n=n: e.mul(out=CQ[:, n * 512:(n + 1) * 512], in_=ps[b][0:32, :], mul=-1.0), reads=[("ps", b)], writes=[("t1", n)])
        cqk = [("t1", 0), ("t1", 1)]
        SPL = self.LAT[0:32, 0:3, :]
        R1 = self.RS[0:32, :, :].rearrange("p a t -> p (a t)")
        tr.op("dve", lambda e: e.tensor_copy(out=SPL[:, 0, :], in_=CQ), reads=cqk, writes=[("lat", 0, 0)])
        tr.op("dve", lambda e: e.tensor_tensor(out=R1, in0=CQ, in1=SPL[:, 0, :], op=ALU.subtract), reads=cqk + [("lat", 0, 0)], writes=[("rs", 0), ("rs", 1)])
        tr.op("dve", lambda e: e.tensor_copy(out=SPL[:, 1, :], in_=R1), reads=[("rs", 0)], writes=[("lat", 1, 0)])
        tr.op("dve", lambda e: e.tensor_tensor(out=R1, in0=R1, in1=SPL[:, 1, :], op=ALU.subtract), reads=[("rs", 0), ("lat", 1, 0)], writes=[("rs", 0), ("rs", 1)])
        tr.op("dve", lambda e: e.tensor_copy(out=SPL[:, 2, :], in_=R1), reads=[("rs", 0)], writes=[("lat", 2, 0)])
        for i in range(3):
            tr.dma("sp", S["Q"][:, 64 + i, :], SPL[:, i, :], reads=[("lat", i, 0)], writes=[("qaug", i)])
        pi = self.rot("pt", 4)
        tr.op("dve", lambda e: e.memset(PT[0:32, pi, :], 1.0), writes=[("pt", pi)])
        for i in range(3):
            for n in range(2):
                tr.dma("sp", S["K"][:, 64 + i, c0 + n * 512:c0 + (n + 1) * 512], PT[0:32, pi, :], reads=[("pt", pi)], writes=[("kaug", hf, i, n)])
        def mk_epi(dst, col0, gc, keyname, cbase):
            def epi(c, w, n, b):
                cc = (c - col0) // 128
                pi = self.rot("pt", 4)
                self.norm_to(b, 128, 64.0, GV2[:, gc:gc + 1], "gvm", PT[:, pi, :], ("pt", pi), blk64=True)
                for hh in range(2):
                    tr.dma("sp", dst[2 * cc + hh, 0:64, cbase + n * 512:cbase + (n + 1) * 512], PT[64 * hh:64 * hh + 64, pi, :],
                           reads=[("pt", pi)], writes=[(keyname, 2 * cc + hh, hf if keyname == "kd" else 0, n)])
            return epi
        self.linear_fm(self.src_ht, KC, w_in, [(512 * g, 512, [(i * 128, 128) for i in range(4)]) for g in range(4)],
                       mk_epi(S["Q"], 0, 28, "qd", 0))
        self.linear_fm(self.src_ht, KC, w_in, [(2048 + 512 * g, 512, [(i * 128, 128) for i in range(4)]) for g in range(4)],
                       mk_epi(S["K"], 2048, 29, "kd", c0))
        for g in range(4):
            slots = [self.wload(w_in[:, 4 * s:4 * s + 4, 4096 + 512 * g:4096 + 512 * (g + 1)]) for s in range(4)]
            for tt in range(8):
                b = self.next_ps(0, 4)
                for kc in range(KC):
                    tr.op("pe", lambda e, b=b, kc=kc, tt=tt, s=slots[kc // 4]: e.matmul(
                        ps[b][:, :], lhsT=HT[:, kc, tt * 128:(tt + 1) * 128], rhs=WR[:, s, kc % 4, :],
                        start=(kc == 0), stop=(kc == KC - 1)), reads=[("w", slots[kc // 4]), ("ht", kc, tt // 4)], writes=[("ps", b)])
                pi = self.rot("pt", 4)
                tr.op("act", lambda e, b=b, pi=pi: e.copy(out=PT[:, pi, :], in_=ps[b][:, :]), reads=[("ps", b)], writes=[("pt", pi)])
                tr.dma("sp", S["V"][c0 + tt * 128:c0 + (tt + 1) * 128, 512 * g:512 * (g + 1)], PT[:, pi, :], reads=[("pt", pi)],
                       writes=[("vd", g, hf, tt)])
        if cc:
            own_k = [("kd", h2, 1, n2) for h2 in range(32) for n2 in range(2)] + [("kaug", 1, i, n2) for i in range(3) for n2 in range(2)]
            self.allgather_rows(S["Ko"], S["Kr"], 536, own_k, "kr")
            self.allgather_rows(S["V"], S["Vr"], 512, [("vd", g2, 1, tt) for g2 in range(4) for tt in range(8)], "vr")
            tr.allgather(S["CS"], S["CSr"], reads=[("csd", 1, tt) for tt in range(8)], writes=["csr"])
        wout = L["w_out"]
        self.mem_block(L, wout)
        hb = HT[:, :, :].rearrange("p a t -> p (a t)")
        h32 = HT[:, :, :].bitcast(F32).rearrange("p a t -> p (a t)")
        Sk = (hf + 1) * T
        l32 = self.LAT[:, :, :].bitcast(F32).rearrange("p a t -> p (a t)")
        BKt = l32[:, 0:512].rearrange("p (j c) -> p j c", c=32)
        TOT = l32[:, 512:544]
        self.barrier([("lat", c2, n2) for c2 in range(4) for n2 in range(2)] + [("hd", "bk"), ("hd", "tot")])
        csk = [("csd", h2, tt) for h2 in range(hf + 1) for tt in range(8)]
        if cc:
            tr.dma("sp", BKt[:, 0:8, :], S["CSr"][0:T, :].rearrange("(j p) c -> p j c", p=128), reads=["csr"], writes=[("hd", "bk")])
            tr.dma("sp", BKt[:, 8:16, :], S["CS"].rearrange("(j p) c -> p j c", p=128), reads=[("csd", 1, tt) for tt in range(8)], writes=[("hd", "bk")])
            tr.dma("sp", TOT, S["CSr"][T - 1:T, :].partition_broadcast(128), reads=["csr"], writes=[("hd", "tot")])
        else:
            tr.dma("sp", BKt[:, 0:Sk // 128, :], S["CS"][0:Sk, :].rearrange("(j p) c -> p j c", p=128), reads=csk, writes=[("hd", "bk")])
            if hf == 1:
                tr.dma("sp", TOT, S["CS"][T - 1:T, :].partition_broadcast(128), reads=csk, writes=[("hd", "tot")])
        if hf == 1:
            tr.op("dve", lambda e: e.tensor_tensor(out=BKt[:, 0:8, :], in0=BKt[:, 0:8, :], in1=TOT.unsqueeze(1).broadcast_to([128, 8, 32]), op=ALU.subtract),
                  reads=[("hd", "bk"), ("hd", "tot")], writes=[("hd", "bk")])
        if cc:
            tr.op("dve", lambda e: e.tensor_scalar(out=BKt[:, 0:8, :], in0=BKt[:, 0:8, :], scalar1=self.PCOL[:, 1:2], scalar2=None, op0=ALU.add),
                  reads=[("hd", "bk"), "pcol"], writes=[("hd", "bk")])
        tr.op("dve", lambda e: e.tensor_scalar(out=BKt[:, 0:Sk // 128, :], in0=BKt[:, 0:Sk // 128, :], scalar1=-CSHIFT, scalar2=None, op0=ALU.add),
              reads=[("hd", "bk")], writes=[("hd", "bk")])
        tasks = []
        for pr in range(16):
            pst = pr % 2
            members = []
            for hh in range(2):
                h = 2 * pr + hh
                base = (2 * pst + hh) * 4096
                Qt = hb[0:67, base:base + T]
                Kt = hb[0:67, base + T:base + T + SEQ]
                Vt = hb[:, base + 3 * T:base + 4 * T].rearrange("p (j c) -> p j c", c=64)
                sk = 2 * pst + hh

                def pre(h=h, sk=sk, Qt=Qt, Kt=Kt, Vt=Vt):
                    tr.dma("sp", Qt, S["Q"][h], reads=[("qd", h, 0, n2) for n2 in range(2)] + [("qaug", i) for i in range(3)], writes=[("hd", sk, "q")])
                    if cc:
                        tr.dma("sp", Kt[:, 0:T], self.r0rows(S["Kr"], 536, 67 * h, 67), reads=[("kr", h // 8)], writes=[("hd", sk, "k")])
                        tr.dma("sp", Kt[:, T:2 * T], S["K"][h], reads=[("kd", h, 1, n2) for n2 in range(2)] +
                               [("kaug", 1, i, n2) for i in range(3) for n2 in range(2)], writes=[("hd", sk, "k")])
                        for i2 in range(2):
                            tr.dma("sp", Vt[:, 4 * i2:4 * i2 + 4, :],
                                   self.r0rows(S["Vr"], 512, 512 * i2, 512)[:, 64 * h:64 * (h + 1)].rearrange("(j p) c -> p j c", p=128),
                                   reads=[("vr", i2)], writes=[("hd", sk, "v")])
                        tr.dma("sp", Vt[:, 8:16, :], S["V"][:, 64 * h:64 * (h + 1)].rearrange("(j p) c -> p j c", p=128),
                               reads=[("vd", h // 8, 1, tt) for tt in range(8)], writes=[("hd", sk, "v")])
                        return
                    tr.dma("sp", Kt[:, 0:Sk], S["K"][h, :, 0:Sk], reads=[("kd", h, h2, n2) for h2 in range(hf + 1) for n2 in range(2)] +
                           [("kaug", h2, i, n2) for h2 in range(hf + 1) for i in range(3) for n2 in range(2)], writes=[("hd", sk, "k")])
                    tr.dma("sp", Vt[:, 0:Sk // 128, :], S["V"][0:Sk, 64 * h:64 * (h + 1)].rearrange("(j p) c -> p j c", p=128),
                           reads=[("vd", h // 8, h2, tt) for h2 in range(hf + 1) for tt in range(8)], writes=[("hd", sk, "v")])

                members.append(dict(q=[(Qt, ("hd", sk, "q"))], k=[(Kt, ("hd", sk, "k"))], v=Vt, vkey=("hd", sk, "v"), pre=pre, po=64 * hh,
                                    bias_fn=lambda j, h=h: (BKt[:, j, h:h + 1], ("hd", "bk"))))
            ab = (pr // 4) % 2
            m = pr % 4
            post = None
            if m == 3:
                def post(pr=pr, ab=ab):
                    g4 = pr // 4
                    self.outproj_group(wout[512 * g4:512 * (g4 + 1), :].rearrange("(kt p) d -> p kt d", p=128),
                                       [(self.AT[ab][:, mm, :], ("cat", 4 * ab + mm)) for mm in range(4)], 1.0)
            tasks += self.attn_tasks(hf, members, 64, 1.0,
                                     lambda n, ab=ab, m=m: self.AT[ab][:, m, n * 512:(n + 1) * 512],
                                     lambda n, ab=ab, m=m: ("cat", 4 * ab + m, n), post=post, clamp=True)
        self.run_pipeline(tasks)
        self.barrier(self.htkeys() + self.hdkeys() + [("hd", "bk"), ("hd", "tot")] + [("lat", c2, n2) for c2 in range(4) for n2 in range(2)])

    def barrier(self, keys):
        self.tr.op("dve", lambda e: e.memset(self.DUMMY[:, 0:1], 0.0), writes=list(keys))

    def hdkeys(self):
        return ([("hd", s2, nm) for s2 in range(4) for nm in ("qn", "qr", "kn", "v", "q", "k")] + [("hd", "kr"), ("hd", "bk"), ("hd", "tot")]
                + [("hd", "b", r) for r in range(-1, 4)])

    def htkeys(self):
        return [("ht", kc, n) for kc in range(KC) for n in range(2)]

    def finish(self):
        self.tr.emit(final_waits=self.finals)
        self.st.close()
        return self.nc


def gain_layout(g):
    g = np.asarray(g, dtype=np.float32)
    return np.ascontiguousarray(g.reshape(-1, 128).T)


def col_layout(g, n=128):
    return np.ascontiguousarray(np.asarray(g, dtype=np.float32).reshape(n, 1))


def const_tables():
    k = np.arange(128)[:, None]
    q = np.arange(512)[None, :]
    cm = np.stack([(r * 128 + k <= q).astype(np.float32) for r in range(4)], axis=1)
    bd = np.zeros((128, 128), np.float32)
    bd[:64, :64] = 1.0
    bd[64:, 64:] = 1.0
    rmt = np.zeros((64, 64), np.float32)
    for m in range(32):
        rmt[m + 32, m] = -1.0
    for m in range(32, 64):
        rmt[m - 32, m] = 1.0
    return np.ascontiguousarray(cm), bd, rmt


def rope_tables():
    half = 32
    inv = 10000.0 ** (-np.arange(half, dtype=np.float32) / half)
    ang = np.arange(SEQ, dtype=np.float32)[None, :] * inv[:, None].astype(np.float32)
    cos = np.cos(ang).astype(np.float32)
    sin = np.sin(ang).astype(np.float32)
    return np.ascontiguousarray(np.concatenate([cos, cos], 0)), np.ascontiguousarray(np.concatenate([sin, sin], 0))


def t5_bucket_onehot():
    d = np.arange(128)
    exact = 16
    log_b = exact + (np.log(np.maximum(d, 1) / exact) / np.log(128 / exact) * (32 - exact)).astype(np.int32)
    log_b = np.minimum(log_b, 31)
    bk = np.where(d < exact, d, log_b).astype(np.int32)
    oh = np.zeros((32, 128), np.float32)
    oh[bk, d] = 1.0
    return oh


def bc128(v):
    v = np.asarray(v, dtype=np.float32).reshape(1, -1)
    return np.ascontiguousarray(np.repeat(v, 128, axis=0))


MIX_IN = {0: 6656, 1: 1600, 2: 3072, 3: 6688}


def build_full(n_layers=4, halves=(0, 1), use_ffn=True, cc=False):
    p = Prog(cc=cc)
    if cc:
        halves = (0,)
    xT = p.din("xT", [D, T if cc else SEQ])
    yT = p.dout("yT", [D, T if cc else SEQ])
    if cc:
        pcols = p.din("pcols", [128, 3])
    cm = p.din("cm", [128, 4, 512])
    bd = p.din("bd64", [128, 128])
    rmt = p.din("rmt", [64, 64])
    memT = p.din("memT", [D, MEM])
    Ls = []
    for i in range(n_layers):
        L = {"memT": memT}
        def di(nm, shp, i=i, L=L):
            L[nm] = p.din("l%d_%s" % (i, nm), shp)
        for nm in ("norm_ffn1", "norm_mix", "norm_ffn2", "norm_mem"):
            di(nm, [128, KC])
        for nm in ("w1g", "w1u", "w2g", "w2u"):
            di(nm, [D, FF])
        for nm in ("w1d", "w2d"):
            di(nm, [FF, D])
        di("w_in", [D, MIX_IN[i % 4]])
        di("w_out", [2560, D])
        di("mem_w_kv", [D, 1024])
        di("mem_q_norm", [128, 1])
        di("mem_k_norm", [128, 1])
        k = i % 4
        if k == 0:
            for nm in ("conv_w0", "conv_w1", "conv_w2"):
                di(nm, [128, 16])
            if cc:
                L["S"] = {"ZHs": p.dscr("c_ZHs", [128, 32], F32), "ZHr": p.dscr("c_ZHr", [256, 32], F32)}
        elif k == 1 and cc:
            di("q_a_norm", [128, 4]); di("kv_a_norm", [128, 4]); di("qk_cols", [128, 4])
            di("w_q_b", [512, 3072]); di("w_kv_b", [512, 4096]); di("cos", [64, T]); di("sin", [64, T])
            kno = p.dscr("m_KNo", [2048, T])
            L["S"] = {"QN": p.dscr("m_QN", [16, 128, T]), "QR": p.dscr("m_QR", [16, 64, T]),
                      "KNo": kno, "KN": kno.rearrange("(h p) t -> h p t", p=128), "KNr": p.dscr("m_KNr", [4096, T]),
                      "KR": p.dscr("m_KR", [64, T]), "KRr": p.dscr("m_KRr", [128, T]),
                      "V": p.dscr("m_V", [T, 2048]), "Vr": p.dscr("m_Vr", [2 * T, 2048])}
        elif k == 2 and cc:
            di("qk_cols", [128, 2]); di("sinks_pair", [128, 16]); di("rel_bias", [32, 32]); di("onehot", [32, 128])
            gdh = p.nc.dram_tensor("s_GD", [32, 128, 1152], F32)
            ko = p.dscr("s_Ko", [256, T])
            L["S"] = {"Q": p.dscr("s_Q", [32, 64, T]), "Ko": ko, "K": ko.rearrange("(h p) t -> h p t", p=64), "V": p.dscr("s_V", [T, 256]),
                      "KHs": p.dscr("s_KHs", [256, 128]), "KHr": p.dscr("s_KHr", [512, 128]),
                      "VHs": p.dscr("s_VHs", [128, 256]), "VHr": p.dscr("s_VHr", [256, 256]),
                      "GD": gdh.ap(), "GDh": gdh}
        elif k == 3 and cc:
            di("qk_cols", [128, 2]); di("bf_bc", [128, 32]); di("ident", [128, 128]); di("triu", [128, 128])
            ko = p.dscr("f_Ko", [32 * 67, T])
            L["S"] = {"Q": p.dscr("f_Q", [32, 67, T]), "Ko": ko, "K": ko.rearrange("(h p) t -> h p t", p=67), "Kr": p.dscr("f_Kr", [2 * 32 * 67, T]),
                      "V": p.dscr("f_V", [T, 2048]), "Vr": p.dscr("f_Vr", [2 * T, 2048]),
                      "CS": p.dscr("f_CS", [T, 32], F32), "CSr": p.dscr("f_CSr", [2 * T, 32], F32)}
        elif k == 1:
            di("q_a_norm", [128, 4]); di("kv_a_norm", [128, 4]); di("qk_cols", [128, 4])
            di("w_q_b", [512, 3072]); di("w_kv_b", [512, 4096]); di("cos", [64, SEQ]); di("sin", [64, SEQ])
            L["S"] = {"QN": p.dscr("m_QN", [16, 128, T]), "QR": p.dscr("m_QR", [16, 64, T]), "KN": p.dscr("m_KN", [16, 128, SEQ]),
                      "KR": p.dscr("m_KR", [64, SEQ]), "V": p.dscr("m_V", [SEQ, 2048])}
        elif k == 2:
            di("qk_cols", [128, 2]); di("sinks_pair", [128, 16]); di("rel_bias", [32, 32]); di("onehot", [32, 128])
            gdh = p.nc.dram_tensor("s_GD", [32, 128, 1152], F32)
            L["S"] = {"Q": p.dscr("s_Q", [32, 64, T]), "K": p.dscr("s_K", [4, 64, SEQ]), "V": p.dscr("s_V", [SEQ, 256]),
                      "GD": gdh.ap(), "GDh": gdh}
        else:
            di("qk_cols", [128, 2]); di("bf_bc", [128, 32]); di("ident", [128, 128]); di("triu", [128, 128])
            L["S"] = {"Q": p.dscr("f_Q", [32, 67, T]), "K": p.dscr("f_K", [32, 67, SEQ]), "V": p.dscr("f_V", [SEQ, 2048]),
                      "CS": p.dscr("f_CS", [SEQ, 32], F32)}
        Ls.append(L)
    p.load_consts(cm, bd, rmt)
    if cc:
        p.load_pcols(pcols)
    for hf in halves:
        p.load_x(xT, hf * T)
        for i in range(n_layers):
            L = Ls[i]
            if use_ffn:
                p.ffn(L["norm_ffn1"], L["w1g"], L["w1u"], L["w1d"])
            k = i % 4
            if k == 0:
                p.mixer_conv(hf, L, L.get("S"))
            elif k == 1:
                p.mixer_mla(hf, L, L["S"])
            elif k == 2:
                p.mixer_swa(hf, L, L["S"])
            else:
                p.mixer_fox(hf, L, L["S"])
            if use_ffn:
                p.ffn(L["norm_ffn2"], L["w2g"], L["w2u"], L["w2d"])
        p.store_x(yT, hf * T)
    return p.finish()


def host_inputs(inputs, n_layers=4, cc=False):
    f = lambda a: np.ascontiguousarray(np.asarray(a, dtype=np.float32))
    cmv, bdv, rmtv = const_tables()
    cosv, sinv = rope_tables()
    shared = {"cm": cmv, "bd64": bdv, "rmt": rmtv}
    for i in range(n_layers):
        pre = "l%d_" % i
        k, occ = i % 4, i // 4
        for nm in ("norm_ffn1", "norm_mix", "norm_ffn2", "norm_mem"):
            shared[pre + nm] = gain_layout(inputs[nm][i])
        shared[pre + "w1g"] = f(inputs["ffn1_w_gate"][i]); shared[pre + "w1u"] = f(inputs["ffn1_w_up"][i]); shared[pre + "w1d"] = f(inputs["ffn1_w_down"][i])
        shared[pre + "w2g"] = f(inputs["ffn2_w_gate"][i]); shared[pre + "w2u"] = f(inputs["ffn2_w_up"][i]); shared[pre + "w2d"] = f(inputs["ffn2_w_down"][i])
        shared[pre + "mem_w_kv"] = f(inputs["mem_w_kv"][i])
        shared[pre + "mem_q_norm"] = col_layout(inputs["mem_q_norm"][i])
        shared[pre + "mem_k_norm"] = col_layout(inputs["mem_k_norm"][i])
        if k == 0:
            shared[pre + "w_in"] = f(inputs["conv_w_in"][occ]); shared[pre + "w_out"] = f(inputs["conv_w_out"][occ])
            for t in range(3):
                shared[pre + "conv_w%d" % t] = gain_layout(np.asarray(inputs["conv_w"][occ])[t])
        elif k == 1:
            shared[pre + "w_in"] = f(inputs["mla_w_in"][occ]); shared[pre + "w_out"] = f(inputs["mla_w_out"][occ])
            shared[pre + "q_a_norm"] = gain_layout(inputs["mla_q_a_norm"][occ]); shared[pre + "kv_a_norm"] = gain_layout(inputs["mla_kv_a_norm"][occ])
            qn = np.asarray(inputs["mla_q_norm"][occ], dtype=np.float32); kn = np.asarray(inputs["mla_k_norm"][occ], dtype=np.float32)
            qk = np.zeros((128, 4), np.float32)
            qk[:, 0] = qn[:128]; qk[:64, 1] = qn[128:]; qk[:, 2] = kn[:128]; qk[:64, 3] = kn[128:]
            shared[pre + "qk_cols"] = qk
            shared[pre + "w_q_b"] = f(inputs["mla_w_q_b"][occ]); shared[pre + "w_kv_b"] = f(inputs["mla_w_kv_b"][occ])
            shared[pre + "cos"] = cosv; shared[pre + "sin"] = sinv
        elif k == 2:
            shared[pre + "w_in"] = f(inputs["swa_w_in"][occ]); shared[pre + "w_out"] = f(inputs["swa_w_out"][occ])
            qk = np.zeros((128, 2), np.float32)
            qk[:, 0] = np.tile(np.asarray(inputs["swa_q_norm"][occ], dtype=np.float32), 2)
            qk[:, 1] = np.tile(np.asarray(inputs["swa_k_norm"][occ], dtype=np.float32), 2)
            shared[pre + "qk_cols"] = qk
            sk_ = np.asarray(inputs["swa_sinks"][occ], dtype=np.float32)
            shared[pre + "sinks_pair"] = np.ascontiguousarray(np.repeat(sk_.reshape(16, 2).T, 64, axis=0))
            shared[pre + "rel_bias"] = f(inputs["rel_bias"]); shared[pre + "onehot"] = t5_bucket_onehot()
        else:
            shared[pre + "w_in"] = f(inputs["fox_w_in"][occ]); shared[pre + "w_out"] = f(inputs["fox_w_out"][occ])
            qk = np.zeros((128, 2), np.float32)
            qk[:, 0] = np.tile(np.asarray(inputs["fox_q_norm"][occ], dtype=np.float32), 2)
            qk[:, 1] = np.tile(np.asarray(inputs["fox_k_norm"][occ], dtype=np.float32), 2)
            shared[pre + "qk_cols"] = qk
            shared[pre + "bf_bc"] = bc128(inputs["fox_b_f"][occ])
            shared[pre + "ident"] = np.eye(128, dtype=np.float32); shared[pre + "triu"] = np.triu(np.ones((128, 128), np.float32))
    x = np.asarray(inputs["x"], dtype=np.float32)
    mem = np.asarray(inputs["mem"], dtype=np.float32)
    maps = []
    if not cc:
        for b in range(x.shape[0]):
            m = dict(shared)
            m["xT"] = np.ascontiguousarray(x[b].T)
            m["memT"] = np.ascontiguousarray(mem[b].T)
            maps.append(m)
        return maps
    for c in range(NCORES):
        b, hf = c // 2, c % 2
        m = dict(shared)
        m["xT"] = np.ascontiguousarray(x[b, hf * T:(hf + 1) * T].T)
        m["memT"] = np.ascontiguousarray(mem[b].T)
        pc = np.zeros((128, 3), np.float32)
        pc[:, 0] = -CSHIFT if hf == 1 else NEG
        pc[:, 1] = 0.0 if hf == 1 else NEG
        pc[:, 2] = 1.0 if hf == 1 else 0.0
        m["pcols"] = pc
        for i in range(n_layers):
            if i % 4 == 1:
                m["l%d_cos" % i] = np.ascontiguousarray(cosv[:, hf * T:(hf + 1) * T])
                m["l%d_sin" % i] = np.ascontiguousarray(sinv[:, hf * T:(hf + 1) * T])
        maps.append(m)
    return maps


def kernel(**inputs):
    nc = build_full(cc=True)
    maps = host_inputs(inputs, cc=True)
    res = run_bass_kernel_spmd(nc, maps, core_ids=list(range(NCORES)))
    out = np.zeros((4, SEQ, D), np.float32)
    for c, r in enumerate(res.results):
        out[c // 2, (c % 2) * T:(c % 2 + 1) * T] = r["yT"].T
    return out
```

```python
import contextlib
import math
import numpy as np
import concourse.bass as bass
import concourse.mybir as mybir
from concourse.bass_utils import run_bass_kernel_spmd

F32 = mybir.dt.float32
BF16 = mybir.dt.bfloat16
AF = mybir.ActivationFunctionType
ALU = mybir.AluOpType
AX = mybir.AxisListType

COMPUTE = ("pe", "act", "dve", "pool")

D = 2048
FF = 5632
T = 1024
SEQ = 2048
KC = D // 128
NG = FF // 512
EPS = 1e-6
MEM = 256
NCORES = 8


class Op:
    __slots__ = ("eng", "fn", "deps", "needs_inc", "is_dma", "sem", "count", "idx", "prev_same_sem", "inc")

    def __init__(self, eng, fn, is_dma=False):
        self.eng = eng
        self.fn = fn
        self.deps = []
        self.needs_inc = False
        self.is_dma = is_dma
        self.sem = None
        self.count = None
        self.idx = None
        self.prev_same_sem = None
        self.inc = 16


class Tracer:
    def __init__(self, nc, n_dma_sems=16):
        self.nc = nc
        self.ops = {e: [] for e in ("pe", "act", "dve", "pool", "sp")}
        self.last_writer = {}
        self.readers = {}
        self.n_dma_sems = n_dma_sems
        self.dma_rr = {"sp": 0, "pool": 0, "act": 0}
        self.dma_last = {}
        self.dma_cnt = {}

    def op(self, eng, fn, reads=(), writes=(), is_dma=False, cc=False, raw_dist=3):
        o = Op(eng, fn, is_dma)
        if cc:
            o.inc = 1
        o.idx = len(self.ops[eng])
        deps = []
        for r in reads:
            w = self.last_writer.get(r)
            if w is not None:
                deps.append((w, 0))
        for wkey in writes:
            w = self.last_writer.get(wkey)
            if w is not None:
                deps.append((w, 1))
            for rd in self.readers.get(wkey, ()):
                deps.append((rd, 2))
        seen = set()
        for d, kind in deps:
            if d is o or id(d) in seen:
                continue
            if (not d.is_dma) and d.eng == eng and not is_dma:
                if eng == "pe":
                    continue
                if kind != 0:
                    continue
                if o.idx - d.idx > raw_dist:
                    continue
            seen.add(id(d))
            o.deps.append(d)
            d.needs_inc = True
        if is_dma:
            if cc:
                key = ("cc", 0)
            else:
                slot = self.dma_rr[eng] % self.n_dma_sems
                self.dma_rr[eng] += 1
                key = (eng, slot)
            o.prev_same_sem = self.dma_last.get(key)
            self.dma_last[key] = o
            self.dma_cnt[key] = self.dma_cnt.get(key, 0) + o.inc
            o.sem = key
            o.count = self.dma_cnt[key]
            o.needs_inc = True
        for r in reads:
            self.readers.setdefault(r, []).append(o)
        for wkey in writes:
            self.last_writer[wkey] = o
            self.readers[wkey] = []
        self.ops[eng].append(o)
        return o

    def dma(self, queue, out, in_, reads=(), writes=(), **kw):
        return self.op(queue, lambda e: e.dma_start(out=out, in_=in_, **kw), reads, writes, is_dma=True)

    def allgather(self, src, dst, reads=(), writes=()):
        groups = [[0, 1], [2, 3], [4, 5], [6, 7]]
        return self.op("pool", lambda e: e.collective_compute("AllGather", ALU.bypass, replica_groups=groups,
                                                              ins=[src.opt()], outs=[dst.opt()]),
                       reads, writes, is_dma=True, cc=True)

    def emit(self, final_waits=()):
        nc = self.nc
        with contextlib.ExitStack() as st:
            esem = {e: st.enter_context(nc.semaphore("s_" + e)) for e in COMPUTE}
            dsem = {}
            for key in self.dma_cnt:
                dsem[key] = st.enter_context(nc.semaphore("d_%s_%d" % key))
            for e in COMPUTE:
                c = 0
                for o in self.ops[e]:
                    if o.is_dma:
                        continue
                    if o.needs_inc:
                        c += 1
                        o.count = c
                        o.sem = e
            ops = self.ops
            final_waits = list(final_waits)

            def semof(d):
                return dsem[d.sem] if d.is_dma else esem[d.sem]

            def run(ename, eng):
                known = {}
                for o in ops[ename]:
                    waits = {}
                    dl = list(o.deps)
                    if o.is_dma and o.prev_same_sem is not None:
                        dl.append(o.prev_same_sem)
                    for d in dl:
                        k = d.sem
                        if waits.get(k, (None, 0))[1] < d.count:
                            waits[k] = (semof(d), d.count)
                    for k, (s, v) in waits.items():
                        if known.get(k, 0) >= v:
                            continue
                        known[k] = v
                        eng.wait_ge(s, v)
                    ins = o.fn(eng)
                    if o.needs_inc:
                        if o.is_dma:
                            ins.then_inc(dsem[o.sem], o.inc)
                        else:
                            ins.then_inc(esem[o.sem], 1)
                if ename == "sp":
                    for d in final_waits:
                        if known.get(d.sem, 0) < d.count:
                            known[d.sem] = d.count
                            eng.wait_ge(semof(d), d.count)

            with nc.Block() as block:
                @block.sync
                def _(e):
                    run("sp", e)

                @block.tensor
                def _(e):
                    run("pe", e)

                @block.scalar
                def _(e):
                    run("act", e)

                @block.vector
                def _(e):
                    run("dve", e)

                @block.gpsimd
                def _(e):
                    run("pool", e)


CSHIFT = 12.0
NEG = -30000.0


class Prog:
    def __init__(self, cc=False):
        self.cc = cc
        self.nc = bass.Bass("TRN2", target_bir_lowering=False)
        self.tr = Tracer(self.nc)
        self.st = contextlib.ExitStack()
        self.finals = []
        nc, st = self.nc, self.st
        sb = lambda n, shp, dt: st.enter_context(nc.sbuf_tensor(n, shp, dt))
        self.XT = sb("XT", [128, KC, T], F32)
        self.HT = sb("HT", [128, KC, T], BF16)
        self.AT2 = sb("AT2", [128, 8, T], BF16)
        self.NWS = 12
        self.WR = sb("WR", [128, self.NWS, 4, 512], BF16)
        self.SG = sb("SG", [128, 2, 512], F32)
        self.SCR = sb("SCR", [128, 2056], F32)
        self.MQ = sb("MQ", [128, 4, T], BF16)
        self.LAT = sb("LAT", [128, 4, T], BF16)
        self.PT = sb("PT", [128, 4, 512], BF16)
        self.RS = sb("RS", [128, 2, 512], F32)
        self.T1 = sb("T1", [128, 2, 512], F32)
        self.ONES = sb("ONES", [128, 128], F32)
        self.ONESB = sb("ONESB", [128, 128], BF16)
        self.BD64 = sb("BD64", [128, 128], F32)
        self.RMT = sb("RMT", [64, 64], F32)
        self.CM = sb("CM", [128, 4, 512], BF16)
        self.EPSB = sb("EPSB", [128, 1], F32)
        self.NEGC = sb("NEGC", [128, 1], F32)
        self.GV = sb("GV", [128, 64], F32)
        self.GV2 = sb("GV2", [128, 96], F32)
        self.DUMMY = sb("DUMMY", [128, 4], F32)
        self.ZB = sb("ZB", [128, 64], BF16)
        self.PCOL = sb("PCOL", [128, 4], F32)
        self.GB01 = sb("GB01", [128, 16, 2], F32)
        self.ZHP = sb("ZHP", [128, 16, 2], F32)
        self.DY = sb("DY", [128, 16, 2], F32)
        self.DYT = sb("DYT", [128, 16], F32)
        self.DYB = sb("DYB", [128, 16, 2], BF16)
        self.ZH = sb("ZH", [128, 16, 2], F32)
        self.ps = [st.enter_context(nc.psum_tensor("ps%d" % i, [128, 512], F32)) for i in range(8)]
        self.wslot = 0
        self.psrr = {}
        self.rr = {}
        self.AT = [self.AT2[:, 4 * b:4 * b + 4, :] for b in range(2)]
        f32v = self.AT2[:, :, :].bitcast(F32).rearrange("p a t -> p (a t)")
        self.ACC = f32v[:, 0:1024]
        self.SQ = [f32v[:, 1024:2048], f32v[:, 2048:3072]]
        self.RSTD = f32v[:, 3072:4096]
        self.ACC1 = self.T1[:, :, :].rearrange("p a t -> p (a t)")
        self.AT32 = self.AT2[:, :, :].bitcast(F32)
        tr = self.tr
        tr.op("dve", lambda e: e.memset(self.ONES[:, :], 1.0), writes=["ones"])
        tr.op("dve", lambda e: e.memset(self.ONESB[:, :], 1.0), writes=["onesb"])
        tr.op("dve", lambda e: e.memset(self.EPSB[:, :], EPS), writes=["eps"])
        tr.op("dve", lambda e: e.memset(self.NEGC[:, :], -CSHIFT), writes=["negc"])
        tr.op("dve", lambda e: e.memset(self.ZB[:, :], 0.0), writes=["zb"])

    def rot(self, name, n=2):
        i = self.rr.get(name, 0)
        self.rr[name] = i + 1
        return i % n

    def din(self, name, shape, dt=F32):
        return self.nc.dram_tensor(name, list(shape), dt, kind="ExternalInput").ap()

    def dout(self, name, shape, dt=F32):
        return self.nc.dram_tensor(name, list(shape), dt, kind="ExternalOutput").ap()

    def dscr(self, name, shape, dt=BF16):
        return self.nc.dram_tensor(name, list(shape), dt).ap()

    def atkeys(self):
        return [("cat", c, n) for c in range(8) for n in range(2)]

    def allgather_rows(self, src, dst, rc, reads, wkey):
        R = src.shape[0]
        assert R % rc == 0
        for i in range(R // rc):
            self.tr.allgather(src[i * rc:(i + 1) * rc, :], dst[2 * rc * i:2 * rc * (i + 1), :], reads=reads, writes=[(wkey, i)])

    @staticmethod
    def r0rows(dst, rc, r_lo, n):
        i, loc = r_lo // rc, r_lo % rc
        assert loc + n <= rc
        return dst[2 * rc * i + loc:2 * rc * i + loc + n, :]

    def load_pcols(self, pcols):
        self.tr.dma("sp", self.PCOL[:, 0:3], pcols, writes=["pcol"])

    def load_consts(self, cm, bd64, rmt):
        self.tr.dma("pool", self.CM[:, :, :], cm, writes=["cm"])
        self.tr.dma("sp", self.BD64[:, :], bd64, writes=["bd64"])
        self.tr.dma("sp", self.RMT[:, :], rmt, writes=["rmt"])

    def load_x(self, xT, c0):
        v = xT.rearrange("(c p) t -> p c t", p=128)
        for q in range(4):
            self.tr.dma("sp", self.XT[:, 4 * q:4 * q + 4, :], v[:, 4 * q:4 * q + 4, c0:c0 + T],
                        writes=[("xt", c, n) for c in range(4 * q, 4 * q + 4) for n in range(2)])

    def store_x(self, yT, c0):
        v = yT.rearrange("(c p) t -> p c t", p=128)
        for q in range(4):
            o = self.tr.dma("sp", v[:, 4 * q:4 * q + 4, c0:c0 + T], self.XT[:, 4 * q:4 * q + 4, :],
                            reads=[("xt", c, n) for c in range(4 * q, 4 * q + 4) for n in range(2)],
                            writes=[("yT", q, c0)])
            self.finals.append(o)

    def next_ps(self, lo=0, hi=8):
        k = self.psrr.get((lo, hi), 0)
        self.psrr[(lo, hi)] = k + 1
        return lo + (k % (hi - lo))

    def rmsnorm_x(self, gain):
        tr = self.tr
        XT, HT, ACC, SQ, RSTD, GV = self.XT, self.HT, self.ACC, self.SQ, self.RSTD, self.GV
        scr = self.atkeys()
        tr.dma("sp", GV[:, 0:KC], gain, writes=["gv"])
        for kc in range(KC):
            sq = SQ[kc % 2]
            tr.op("act", lambda e, kc=kc, sq=sq: e.activation(out=sq, in_=XT[:, kc, :], func=AF.Square),
                  reads=[("xt", kc, 0), ("xt", kc, 1)], writes=[("sq", kc % 2)] + (scr if kc == 0 else []))
            acc = ACC if kc % 2 == 0 else self.ACC1
            ak = ["acc"] if kc % 2 == 0 else [("t1", 0), ("t1", 1)]
            if kc < 2:
                tr.op("dve", lambda e, sq=sq, acc=acc: e.tensor_copy(out=acc, in_=sq), reads=[("sq", kc % 2)], writes=ak)
            else:
                tr.op("dve", lambda e, sq=sq, acc=acc: e.tensor_tensor(out=acc, in0=acc, in1=sq, op=ALU.add),
                      reads=[("sq", kc % 2)] + ak, writes=ak, raw_dist=1)
        for n in range(2):
            b = self.next_ps()
            sl = slice(n * 512, (n + 1) * 512)
            tr.op("pe", lambda e, b=b, sl=sl: e.matmul(self.ps[b][:, :], lhsT=self.ONES[:, :], rhs=ACC[:, sl], start=True, stop=False),
                  reads=["acc", "ones"], writes=[("ps", b)])
            tr.op("pe", lambda e, b=b, sl=sl: e.matmul(self.ps[b][:, :], lhsT=self.ONES[:, :], rhs=self.ACC1[:, sl], start=False, stop=True),
                  reads=[("t1", 0), ("t1", 1), "ones"], writes=[("ps", b)])
            tr.op("act", lambda e, b=b, sl=sl: e.activation(out=RSTD[:, sl], in_=self.ps[b][:, :], func=AF.Ln,
                                                             scale=1.0 / D, bias=self.EPSB[:, 0:1]),
                  reads=[("ps", b), "eps"], writes=[("rstd", n)])
            tr.op("act", lambda e, sl=sl: e.activation(out=RSTD[:, sl], in_=RSTD[:, sl], func=AF.Exp, scale=-0.5),
                  reads=[("rstd", n)], writes=[("rstd", n)])
        for kc in range(KC):
            for n in range(2):
                sl = slice(n * 512, (n + 1) * 512)
                tr.op("dve", lambda e, kc=kc, sl=sl: e.scalar_tensor_tensor(
                    out=HT[:, kc, sl], in0=XT[:, kc, sl], scalar=GV[:, kc:kc + 1], in1=RSTD[:, sl],
                    op0=ALU.mult, op1=ALU.mult),
                    reads=[("xt", kc, n), ("rstd", n), "gv"] + scr, writes=[("ht", kc, n)])

    def wload(self, src):
        s = self.wslot % self.NWS
        self.wslot += 1
        kp, kcn, ncol = src.shape[0], src.shape[1], src.shape[2]
        self.tr.dma("pool", self.WR[0:kp, s, 0:kcn, 0:ncol], src, writes=[("w", s)])
        return s

    def ffn(self, gain, wg, wu, wd):
        tr = self.tr
        self.rmsnorm_x(gain)
        wgv = wg.rearrange("(kc p) f -> p kc f", p=128)
        wuv = wu.rearrange("(kc p) f -> p kc f", p=128)
        wdv = wd.rearrange("(fc p) d -> p fc d", p=128)
        XT, HT, WR, SG, ps = self.XT, self.HT, self.WR, self.SG, self.ps

        def gu(g):
            at = self.AT[g % 2]
            gs = [self.wload(wgv[:, 4 * s:4 * s + 4, g * 512:(g + 1) * 512]) for s in range(4)]
            us = [self.wload(wuv[:, 4 * s:4 * s + 4, g * 512:(g + 1) * 512]) for s in range(4)]
            for m in range(4):
                for n in range(2):
                    bg = self.next_ps(0, 4)
                    bu = self.next_ps(0, 4)
                    for (bb, ss) in ((bg, gs), (bu, us)):
                        for kc in range(KC):
                            tr.op("pe", lambda e, bb=bb, s=ss[kc // 4], kc=kc, m=m, n=n: e.matmul(
                                ps[bb][:, :], lhsT=WR[:, s, kc % 4, m * 128:(m + 1) * 128],
                                rhs=HT[:, kc, n * 512:(n + 1) * 512], start=(kc == 0), stop=(kc == KC - 1)),
                                reads=[("w", ss[kc // 4]), ("ht", kc, n)], writes=[("ps", bb)])
                    si = self.rot("sg")
                    tr.op("act", lambda e, bg=bg, si=si: e.activation(out=SG[:, si, :], in_=ps[bg][:, :], func=AF.Silu),
                          reads=[("ps", bg)], writes=[("sg", si)])
                    tr.op("dve", lambda e, bu=bu, si=si, at=at, m=m, n=n: e.tensor_tensor(
                        out=at[:, m, n * 512:(n + 1) * 512], in0=SG[:, si, :], in1=ps[bu][:, :], op=ALU.mult),
                        reads=[("sg", si), ("ps", bu)], writes=[("cat", 4 * (g % 2) + m, n)])

        def dn(g):
            ab = g % 2
            self.outproj_group(wdv[:, 4 * g:4 * g + 4, :], [(self.AT[ab][:, kc, :], ("cat", 4 * ab + kc)) for kc in range(4)], 0.5)

        gu(0)
        for g in range(1, NG):
            gu(g)
            dn(g - 1)
        dn(NG - 1)

    def outproj_group(self, wv, tiles, scale, small=None):
        tr = self.tr
        XT, WR, ps = self.XT, self.WR, self.ps
        nkt = len(tiles)
        kp = wv.shape[0]
        ds = [self.wload(wv[:, :, j * 512:(j + 1) * 512]) for j in range(4)]
        for mo in range(KC):
            for n in range(2 if small is None else 1):
                b = self.next_ps(4, 8)
                c_lo, c_hi = (n * 512, (n + 1) * 512) if small is None else (0, small)
                wd_ = c_hi - c_lo
                for kt in range(nkt):
                    a, key = tiles[kt]
                    rk = key + (n,) if small is None else key
                    tr.op("pe", lambda e, b=b, s=ds[mo // 4], kt=kt, mo=mo, a=a, c_lo=c_lo, c_hi=c_hi, wd_=wd_: e.matmul(
                        ps[b][:, 0:wd_], lhsT=WR[0:kp, s, kt, (mo % 4) * 128:(mo % 4 + 1) * 128],
                        rhs=a[:, c_lo:c_hi], start=(kt == 0), stop=(kt == nkt - 1)),
                        reads=[("w", ds[mo // 4]), rk], writes=[("ps", b)])
                tr.op("dve", lambda e, b=b, mo=mo, c_lo=c_lo, c_hi=c_hi, wd_=wd_: e.scalar_tensor_tensor(
                    out=XT[:, mo, c_lo:c_hi], in0=ps[b][:, 0:wd_], scalar=scale,
                    in1=XT[:, mo, c_lo:c_hi], op0=ALU.mult, op1=ALU.add),
                    reads=[("ps", b), ("xt", mo, n)], writes=[("xt", mo, n)])

    def linear_fm(self, src, nk, wv, groups, epi, kp=128):
        tr = self.tr
        WR, ps = self.WR, self.ps
        pending = []
        for (g0, gw, mcs) in groups:
            slots = [self.wload(wv[:, 4 * s:min(4 * s + 4, nk), g0:g0 + gw]) for s in range((nk + 3) // 4)]
            for (off, w) in mcs:
                for n in range(2):
                    b = self.next_ps(0, 4)
                    for kc in range(nk):
                        a, key = src(kc, n)
                        s = slots[kc // 4]
                        tr.op("pe", lambda e, b=b, s=s, kc=kc, off=off, w=w, a=a: e.matmul(
                            ps[b][0:w, :], lhsT=WR[0:kp, s, kc % 4, off:off + w], rhs=a,
                            start=(kc == 0), stop=(kc == nk - 1)),
                            reads=[("w", s), key], writes=[("ps", b)])
                    pending.append((g0 + off, w, n, b))
                    if len(pending) > (2 if nk <= 4 else 1):
                        epi(*pending.pop(0))
        while pending:
            epi(*pending.pop(0))

    def src_ht(self, kc, n):
        return self.HT[:, kc, n * 512:(n + 1) * 512], ("ht", kc, n)

    def src_lat(self, kc, n):
        return self.LAT[:, kc, n * 512:(n + 1) * 512], ("lat", kc, n)

    def colnorm(self, b, w, dsz, out_op, blk64=False):
        tr = self.tr
        ps = self.ps
        si = self.rot("sg")
        ri = self.rot("rs")
        SG, RS = self.SG, self.RS
        tr.op("act", lambda e: e.activation(out=SG[0:w, si, :], in_=ps[b][0:w, :], func=AF.Square),
              reads=[("ps", b)], writes=[("sg", si)])
        b2 = self.next_ps(4, 8)
        ones = self.BD64 if blk64 else self.ONES
        tr.op("pe", lambda e: e.matmul(ps[b2][0:w, :], lhsT=ones[0:w, 0:w], rhs=SG[0:w, si, :], start=True, stop=True),
              reads=[("sg", si), "ones", "bd64"], writes=[("ps", b2)])
        tr.op("act", lambda e: e.activation(out=RS[0:w, ri, :], in_=ps[b2][0:w, :], func=AF.Ln, scale=1.0 / dsz,
                                            bias=self.EPSB[0:w, 0:1]),
              reads=[("ps", b2), "eps"], writes=[("rs", ri)])
        tr.op("act", lambda e: e.activation(out=RS[0:w, ri, :], in_=RS[0:w, ri, :], func=AF.Exp, scale=-0.5), reads=[("rs", ri)], writes=[("rs", ri)])
        out_op(RS[0:w, ri, :], ("rs", ri))

    def norm_to(self, b, w, dsz, gcol, gkey, out_ap, out_key, blk64=False):
        def fin(rs, rskey):
            self.tr.op("dve", lambda e: e.scalar_tensor_tensor(out=out_ap, in0=self.ps[b][0:w, :], scalar=gcol, in1=rs,
                                                               op0=ALU.mult, op1=ALU.mult),
                       reads=[("ps", b), rskey, gkey], writes=[out_key])
        self.colnorm(b, w, dsz, fin, blk64)

    def rope_to(self, b, gcol, gkey, cs_lo, out_ap, out_key):
        tr = self.tr
        ti = self.rot("t1")
        T1 = self.T1
        cosv = self.SCR[0:64, 0:T][:, cs_lo:cs_lo + 512]
        sinv = self.SCR[0:64, T:2 * T][:, cs_lo:cs_lo + 512]
        self.norm_to(b, 64, 64.0, gcol, gkey, T1[0:64, ti, :], ("t1", ti))
        b3 = self.next_ps(4, 8)
        tr.op("pe", lambda e: e.matmul(self.ps[b3][0:64, :], lhsT=self.RMT[:, :], rhs=T1[0:64, ti, :], start=True, stop=True),
              reads=[("t1", ti), "rmt"], writes=[("ps", b3)])
        si = self.rot("sg")
        tr.op("dve", lambda e: e.tensor_tensor(out=self.SG[0:64, si, :], in0=self.ps[b3][0:64, :], in1=sinv, op=ALU.mult),
              reads=[("ps", b3), "cs"], writes=[("sg", si)])
        tr.op("dve", lambda e: e.tensor_tensor(out=T1[0:64, ti, :], in0=T1[0:64, ti, :], in1=cosv, op=ALU.mult),
              reads=[("t1", ti), "cs"], writes=[("t1", ti)])
        tr.op("dve", lambda e: e.tensor_tensor(out=out_ap, in0=T1[0:64, ti, :], in1=self.SG[0:64, si, :], op=ALU.add),
              reads=[("t1", ti), ("sg", si)], writes=[out_key])

    def mem_kv(self, memT, gmem, wkv, gk):
        tr = self.tr
        HT = self.HT
        h32 = HT[:, :, :].bitcast(F32).rearrange("p a t -> p (a t)")
        MEMF = h32[:, 0:4096].rearrange("p (c m) -> p c m", m=MEM)
        hb = HT[:, :, :].rearrange("p a t -> p (a t)")
        MEMN = hb[:, 8192:12288].rearrange("p (c m) -> p c m", m=MEM)
        MK = hb[:, 12288:13312].rearrange("p (h m) -> p h m", m=MEM)
        MV = hb[:, 13312:14336].rearrange("p (j c) -> p j c", c=512)
        allht = [("ht", kc, n) for kc in range(KC) for n in range(2)]
        tr.dma("sp", MEMF, memT.rearrange("(c p) m -> p c m", p=128), writes=allht)
        tr.dma("sp", self.GV2[:, 0:KC], gmem, writes=["gv2"])
        tr.dma("sp", self.GV2[:, 16:17], gk, writes=["gv2b"])
        b = self.next_ps(4, 8)
        for kc in range(KC):
            si = self.rot("sg")
            tr.op("act", lambda e, kc=kc, si=si: e.activation(out=self.SG[:, si, 0:MEM], in_=MEMF[:, kc, :], func=AF.Square),
                  reads=allht[0:1], writes=[("sg", si)])
            tr.op("pe", lambda e, kc=kc, si=si: e.matmul(self.ps[b][:, 0:MEM], lhsT=self.ONES[:, :], rhs=self.SG[:, si, 0:MEM],
                                                         start=(kc == 0), stop=(kc == KC - 1)),
                  reads=[("sg", si), "ones"], writes=[("ps", b)])
        ri = self.rot("rs")
        RS = self.RS
        tr.op("act", lambda e: e.activation(out=RS[:, ri, 0:MEM], in_=self.ps[b][:, 0:MEM], func=AF.Ln, scale=1.0 / D,
                                            bias=self.EPSB[:, 0:1]), reads=[("ps", b), "eps"], writes=[("rs", ri)])
        tr.op("act", lambda e: e.activation(out=RS[:, ri, 0:MEM], in_=RS[:, ri, 0:MEM], func=AF.Exp, scale=-0.5), reads=[("rs", ri)], writes=[("rs", ri)])
        for kc in range(KC):
            tr.op("dve", lambda e, kc=kc: e.scalar_tensor_tensor(out=MEMN[:, kc, :], in0=MEMF[:, kc, :], scalar=self.GV2[:, kc:kc + 1],
                                                                in1=RS[:, ri, 0:MEM], op0=ALU.mult, op1=ALU.mult),
                  reads=[("rs", ri), "gv2"] + allht[0:1], writes=[("memn", kc)])
        wv = wkv.rearrange("(kc p) f -> p kc f", p=128)
        slots = [self.wload(wv[:, 4 * s:4 * s + 4, 0:512]) for s in range(4)]
        for h in range(4):
            bb = self.next_ps(0, 4)
            for kc in range(KC):
                tr.op("pe", lambda e, kc=kc, h=h, bb=bb, s=slots[kc // 4]: e.matmul(
                    self.ps[bb][:, 0:MEM], lhsT=self.WR[:, s, kc % 4, h * 128:(h + 1) * 128], rhs=MEMN[:, kc, :],
                    start=(kc == 0), stop=(kc == KC - 1)), reads=[("w", slots[kc // 4]), ("memn", kc)], writes=[("ps", bb)])
            si = self.rot("sg")
            tr.op("act", lambda e, bb=bb, si=si: e.activation(out=self.SG[:, si, 0:MEM], in_=self.ps[bb][:, 0:MEM], func=AF.Square),
                  reads=[("ps", bb)], writes=[("sg", si)])
            b2 = self.next_ps(4, 8)
            tr.op("pe", lambda e, b2=b2, si=si: e.matmul(self.ps[b2][:, 0:MEM], lhsT=self.ONES[:, :], rhs=self.SG[:, si, 0:MEM],
                                                         start=True, stop=True), reads=[("sg", si), "ones"], writes=[("ps", b2)])
            r2 = self.rot("rs")
            tr.op("act", lambda e, b2=b2, r2=r2: e.activation(out=RS[:, r2, 0:MEM], in_=self.ps[b2][:, 0:MEM], func=AF.Ln,
                                                              scale=1.0 / 128, bias=self.EPSB[:, 0:1]),
                  reads=[("ps", b2), "eps"], writes=[("rs", r2)])
            tr.op("act", lambda e, r2=r2: e.activation(out=RS[:, r2, 0:MEM], in_=RS[:, r2, 0:MEM], func=AF.Exp, scale=-0.5), reads=[("rs", r2)], writes=[("rs", r2)])
            tr.op("dve", lambda e, bb=bb, r2=r2, h=h: e.scalar_tensor_tensor(out=MK[:, h, :], in0=self.ps[bb][:, 0:MEM],
                                                                            scalar=self.GV2[:, 16:17], in1=RS[:, r2, 0:MEM],
                                                                            op0=ALU.mult, op1=ALU.mult),
                  reads=[("ps", bb), ("rs", r2), "gv2b"], writes=[("mk", h)])
        slots = [self.wload(wv[:, 4 * s:4 * s + 4, 512:1024]) for s in range(4)]
        for j in range(2):
            bb = self.next_ps(0, 4)
            for kc in range(KC):
                tr.op("pe", lambda e, kc=kc, j=j, bb=bb, s=slots[kc // 4]: e.matmul(
                    self.ps[bb][:, :], lhsT=MEMN[:, kc, j * 128:(j + 1) * 128], rhs=self.WR[:, s, kc % 4, :],
                    start=(kc == 0), stop=(kc == KC - 1)), reads=[("w", slots[kc // 4]), ("memn", kc)], writes=[("ps", bb)])
            tr.op("act", lambda e, bb=bb, j=j: e.copy(out=MV[:, j, :], in_=self.ps[bb][:, :]), reads=[("ps", bb)], writes=[("mv", j)])
        return MK, MV

    def mem_attn(self, MK, MV, ab):
        tr = self.tr
        ps, PT = self.ps, self.PT
        scale = 1.0 / math.sqrt(128.0)
        for h in range(4):
            for n in range(2):
                sl = slice(n * 512, (n + 1) * 512)
                bo = self.next_ps(4, 6)
                bl = self.next_ps(6, 8)
                for j in range(2):
                    b = self.next_ps(0, 4)
                    tr.op("pe", lambda e, b=b, h=h, j=j, sl=sl: e.matmul(ps[b][:, :], lhsT=MK[:, h, j * 128:(j + 1) * 128],
                                                                        rhs=self.MQ[:, h, sl], start=True, stop=True),
                          reads=[("mk", h), ("mq", h, n)], writes=[("ps", b)])
                    pi = self.rot("pt", 4)
                    tr.op("act", lambda e, b=b, pi=pi: e.activation(out=PT[:, pi, :], in_=ps[b][:, :], func=AF.Exp, scale=scale,
                                                                    bias=self.NEGC[:, 0:1]),
                          reads=[("ps", b), "negc"], writes=[("pt", pi)])
                    tr.op("pe", lambda e, bo=bo, h=h, j=j, pi=pi: e.matmul(ps[bo][:, :], lhsT=MV[:, j, h * 128:(h + 1) * 128],
                                                                          rhs=PT[:, pi, :], start=(j == 0), stop=(j == 1)),
                          reads=[("mv", j), ("pt", pi)], writes=[("ps", bo)])
                    tr.op("pe", lambda e, bl=bl, j=j, pi=pi: e.matmul(ps[bl][:, :], lhsT=self.ONESB[:, :], rhs=PT[:, pi, :],
                                                                     start=(j == 0), stop=(j == 1)),
                          reads=["onesb", ("pt", pi)], writes=[("ps", bl)])
                self.attn_finish(bo, bl, 128, self.AT[ab][:, h, sl], ("cat", 4 * ab + h, n))

    def attn_finish(self, bo, bl, dv, out_ap, out_key, add_col=None, add_key=None):
        tr = self.tr
        ri = self.rot("rs")
        RS = self.RS
        if add_col is None:
            tr.op("act", lambda e: e.activation(out=RS[0:dv, ri, :], in_=self.ps[bl][0:dv, :], func=AF.Ln), reads=[("ps", bl)], writes=[("rs", ri)])
        else:
            tr.op("act", lambda e: e.activation(out=RS[0:dv, ri, :], in_=self.ps[bl][0:dv, :], func=AF.Ln, bias=add_col),
                  reads=[("ps", bl), add_key], writes=[("rs", ri)])
        tr.op("act", lambda e: e.activation(out=RS[0:dv, ri, :], in_=RS[0:dv, ri, :], func=AF.Exp, scale=-1.0), reads=[("rs", ri)], writes=[("rs", ri)])
        tr.op("dve", lambda e: e.tensor_tensor(out=out_ap, in0=self.ps[bo][0:dv, :], in1=RS[0:dv, ri, :], op=ALU.mult),
              reads=[("ps", bo), ("rs", ri)], writes=[out_key])

    def mq_epi(self, col0):
        def epi(c, w, n, b):
            h = (c - col0) // 128
            self.norm_to(b, 128, 128.0, self.GV2[:, 17:18], "gv2c", self.MQ[:, h, n * 512:(n + 1) * 512], ("mq", h, n))
        return epi

    def mixer_conv(self, hf, L, S=None):
        tr = self.tr
        cc = self.cc
        if cc:
            hf = 0
        w_in = L["w_in"].rearrange("(kc p) f -> p kc f", p=128)
        tr.dma("sp", self.GV2[:, 17:18], L["mem_q_norm"], writes=["gv2c"])
        tr.dma("sp", self.GV2[:, 20:36], L["conv_w0"], writes=["cw"])
        tr.dma("sp", self.GV2[:, 36:52], L["conv_w1"], writes=["cw"])
        tr.dma("sp", self.GV2[:, 52:68], L["conv_w2"], writes=["cw"])
        self.rmsnorm_x(L["norm_mix"])
        self.linear_fm(self.src_ht, KC, w_in, [(6144, 512, [(i * 128, 128) for i in range(4)])], self.mq_epi(6144))
        Z = self.SCR[:, 0:T + 2]
        CV = self.SCR[:, T + 8:2 * T + 8]
        S1 = self.SG[:, :, :].rearrange("p a t -> p (a t)")
        ps, WR, HT = self.ps, self.WR, self.HT
        wout = L["w_out"]
        for G in range(4):
            ab = G % 2
            sl_gc = [self.wload(w_in[:, 4 * s:4 * s + 4, 2048 + 512 * G:2048 + 512 * (G + 1)]) for s in range(4)]
            sl_xt = [self.wload(w_in[:, 4 * s:4 * s + 4, 4096 + 512 * G:4096 + 512 * (G + 1)]) for s in range(4)]
            sl_gb = [self.wload(w_in[:, 4 * s:4 * s + 4, 512 * G:512 * (G + 1)]) for s in range(4)]
            for m in range(4):
                c = 4 * G + m
                w0 = self.GV2[:, 20 + c:21 + c]
                w1 = self.GV2[:, 36 + c:37 + c]
                w2 = self.GV2[:, 52 + c:53 + c]
                banks = {}
                for (nm, ss) in (("xt", sl_xt), ("gc", sl_gc), ("gb", sl_gb)):
                    for n in range(2):
                        b = self.next_ps(0, 8)
                        banks[(nm, n)] = b
                        for kc in range(KC):
                            tr.op("pe", lambda e, b=b, s=ss[kc // 4], kc=kc, m=m, n=n: e.matmul(
                                ps[b][:, :], lhsT=WR[:, s, kc % 4, m * 128:(m + 1) * 128], rhs=HT[:, kc, n * 512:(n + 1) * 512],
                                start=(kc == 0), stop=(kc == KC - 1)), reads=[("w", ss[kc // 4]), ("ht", kc, n)], writes=[("ps", b)])
                        if nm == "xt":
                            tr.op("act", lambda e, b=b, n=n: e.copy(out=S1[:, n * 512:(n + 1) * 512], in_=ps[b][:, :]),
                                  reads=[("ps", b)], writes=[("sg", n)])
                        if nm == "gc":
                            tr.op("dve", lambda e, b=b, n=n: e.tensor_tensor(out=Z[:, 2 + n * 512:2 + (n + 1) * 512], in0=S1[:, n * 512:(n + 1) * 512],
                                                                            in1=ps[b][:, :], op=ALU.mult),
                                  reads=[("ps", b), ("sg", n)], writes=[("z", n)])
                if hf == 0:
                    tr.op("dve", lambda e: e.memset(Z[:, 0:2], 0.0), writes=[("z", 2)])
                else:
                    tr.op("dve", lambda e, c=c: e.tensor_copy(out=Z[:, 0:2], in_=self.ZH[:, c, :]), reads=[("zh", c)], writes=[("z", 2)])
                zk = [("z", 0), ("z", 1), ("z", 2)]
                tr.op("act", lambda e, w2=w2: e.mul(out=CV, in_=Z[:, 2:T + 2], mul=w2), reads=zk + ["cw"], writes=["cv"])
                tr.op("dve", lambda e, w1=w1: e.scalar_tensor_tensor(out=CV, in0=Z[:, 1:T + 1], scalar=w1, in1=CV, op0=ALU.mult, op1=ALU.add),
                      reads=zk + ["cv", "cw"], writes=["cv"])
                tr.op("dve", lambda e, w0=w0: e.scalar_tensor_tensor(out=CV, in0=Z[:, 0:T], scalar=w0, in1=CV, op0=ALU.mult, op1=ALU.add),
                      reads=zk + ["cv", "cw"], writes=["cv"])
                if hf == 0:
                    tr.op("act", lambda e, c=c: e.copy(out=self.ZH[:, c, :], in_=Z[:, T:T + 2]), reads=zk, writes=[("zh", c)])
                if cc:
                    b0 = banks[("gb", 0)]
                    tr.op("act", lambda e, c=c, b0=b0: e.copy(out=self.GB01[:, c, :], in_=ps[b0][:, 0:2]), reads=[("ps", b0)], writes=[("gb01", c)])
                for n in range(2):
                    b = banks[("gb", n)]
                    tr.op("dve", lambda e, b=b, n=n, m=m, ab=ab: e.tensor_tensor(out=self.AT[ab][:, m, n * 512:(n + 1) * 512], in0=CV[:, n * 512:(n + 1) * 512],
                                                                                  in1=ps[b][:, :], op=ALU.mult),
                          reads=[("ps", b), "cv"], writes=[("cat", 4 * ab + m, n)])
            self.outproj_group(wout[512 * G:512 * (G + 1), :].rearrange("(kt p) d -> p kt d", p=128),
                               [(self.AT[ab][:, m, :], ("cat", 4 * ab + m)) for m in range(4)], 1.0)
        if cc:
            zhk = [("zh", c) for c in range(16)]
            tr.dma("sp", S["ZHs"], self.ZH[:, :, :].rearrange("p c t -> p (c t)"), reads=zhk, writes=["zhs"])
            tr.allgather(S["ZHs"], S["ZHr"], reads=["zhs"], writes=["zhr"])
            tr.dma("sp", self.ZHP[:, :, :].rearrange("p c t -> p (c t)"), S["ZHr"][0:128, :], reads=["zhr"], writes=["zhp"])
            W0, W1 = self.GV2[:, 20:36], self.GV2[:, 36:52]
            DY, DYT, GB01, ZHP = self.DY, self.DYT, self.GB01, self.ZHP
            gk = [("gb01", c) for c in range(16)]
            tr.op("dve", lambda e: e.tensor_tensor(out=DY[:, :, 0], in0=W0, in1=ZHP[:, :, 0], op=ALU.mult), reads=["zhp", "cw"], writes=["dy0"])
            tr.op("dve", lambda e: e.tensor_tensor(out=DYT[:, :], in0=W1, in1=ZHP[:, :, 1], op=ALU.mult), reads=["zhp", "cw"], writes=["dyt"])
            tr.op("dve", lambda e: e.tensor_tensor(out=DY[:, :, 0], in0=DY[:, :, 0], in1=DYT[:, :], op=ALU.add), reads=["dy0", "dyt"], writes=["dy0"])
            tr.op("dve", lambda e: e.tensor_tensor(out=DY[:, :, 0], in0=DY[:, :, 0], in1=GB01[:, :, 0], op=ALU.mult), reads=["dy0"] + gk, writes=["dy0"])
            tr.op("dve", lambda e: e.tensor_tensor(out=DY[:, :, 1], in0=W0, in1=ZHP[:, :, 1], op=ALU.mult), reads=["zhp", "cw"], writes=["dy1"])
            tr.op("dve", lambda e: e.tensor_tensor(out=DY[:, :, 1], in0=DY[:, :, 1], in1=GB01[:, :, 1], op=ALU.mult), reads=["dy1"] + gk, writes=["dy1"])
            tr.op("dve", lambda e: e.tensor_scalar(out=self.DYB[:, :, :], in0=DY[:, :, :], scalar1=self.PCOL[:, 2:3], scalar2=None, op0=ALU.mult),
                  reads=["dy0", "dy1", "pcol"], writes=[("dyb",)])
            for G in range(4):
                self.outproj_group(wout[512 * G:512 * (G + 1), :].rearrange("(kt p) d -> p kt d", p=128),
                                   [(self.DYB[:, 4 * G + m, :], ("dyb",)) for m in range(4)], 1.0, small=2)
        MK, MV = self.mem_kv(L["memT"], L["norm_mem"], L["mem_w_kv"], L["mem_k_norm"])
        self.mem_attn(MK, MV, 0)
        self.barrier(self.htkeys() + self.hdkeys() + [("memn", k) for k in range(KC)] + [("mk", h) for h in range(4)] + [("mv", j) for j in range(2)])
        self.outproj_group(wout[2048:2560, :].rearrange("(kt p) d -> p kt d", p=128),
                           [(self.AT[0][:, m, :], ("cat", m)) for m in range(4)], 1.0)

    def lat_norm(self, gcol0, nsz):
        tr = self.tr
        A32 = self.AT32
        for n in range(2):
            b = self.next_ps(4, 8)
            for c in range(4):
                si = self.rot("sg")
                tr.op("act", lambda e, c=c, n=n, si=si: e.activation(out=self.SG[:, si, :], in_=A32[:, 2 * c + n, :], func=AF.Square),
                      reads=[("a32", c, n)], writes=[("sg", si)])
                tr.op("pe", lambda e, c=c, si=si, b=b: e.matmul(self.ps[b][:, :], lhsT=self.ONES[:, :], rhs=self.SG[:, si, :],
                                                                start=(c == 0), stop=(c == 3)),
                      reads=[("sg", si), "ones"], writes=[("ps", b)])
            ri = self.rot("rs")
            RS = self.RS
            tr.op("act", lambda e, b=b, ri=ri: e.activation(out=RS[:, ri, :], in_=self.ps[b][:, :], func=AF.Ln, scale=1.0 / nsz,
                                                            bias=self.EPSB[:, 0:1]), reads=[("ps", b), "eps"], writes=[("rs", ri)])
            tr.op("act", lambda e, ri=ri: e.activation(out=RS[:, ri, :], in_=RS[:, ri, :], func=AF.Exp, scale=-0.5), reads=[("rs", ri)], writes=[("rs", ri)])
            for c in range(4):
                tr.op("dve", lambda e, c=c, n=n, ri=ri: e.scalar_tensor_tensor(
                    out=self.LAT[:, c, n * 512:(n + 1) * 512], in0=A32[:, 2 * c + n, :], scalar=self.GV2[:, gcol0 + c:gcol0 + c + 1],
                    in1=RS[:, ri, :], op0=ALU.mult, op1=ALU.mult),
                    reads=[("a32", c, n), ("rs", ri), "gvm"], writes=[("lat", c, n)])

    def raw_epi(self, col0):
        def epi(c, w, n, b):
            cc = (c - col0) // 128
            self.tr.op("act", lambda e: e.copy(out=self.AT32[:, 2 * cc + n, :], in_=self.ps[b][:, :]),
                       reads=[("ps", b)], writes=[("a32", cc, n)])
        return epi

    def stage_out(self, w, make, dst, dkey):
        pi = self.rot("pt", 4)
        make(self.PT[0:w, pi, :], ("pt", pi))
        self.tr.dma("sp", dst, self.PT[0:w, pi, :], reads=[("pt", pi)], writes=[dkey])

    def run_pipeline(self, tasks, la=2):
        n = len(tasks)
        for i in range(n + la):
            if i < n:
                tasks[i][0]()
                tasks[i][1]()
            if i - la >= 0:
                tasks[i - la][2]()

    def attn_tasks(self, hf, members, dvm, scale, out_ap_fn, out_key_fn, post=None, clamp=False):
        tr = self.tr
        ps, PT = self.ps, self.PT
        tasks = []
        nm = len(members)
        dtot = members[-1]["po"] + dvm
        for n in range(2):
            sl = slice(n * 512, (n + 1) * 512)
            nj = hf * 8 + 4 * (n + 1)
            st = {}
            for mi, mem in enumerate(members):
                qparts, kparts, vt, vkey, bias_fn, pre, po = mem["q"], mem["k"], mem["v"], mem["vkey"], mem["bias_fn"], mem["pre"], mem["po"]
                np_ = len(qparts)
                for j in range(nj):
                    t = {}
                    r = j - (hf * 8 + 4 * n)

                    def s1(t=t, st=st, j=j, n=n, sl=sl, mi=mi, pre=pre, qparts=qparts, kparts=kparts, np_=np_):
                        if j == 0 and n == 0 and pre is not None:
                            pre()
                        if j == 0 and mi == 0:
                            st["bo"] = self.next_ps(4, 6)
                            st["bl"] = self.next_ps(6, 8)
                        b = t["b"] = self.next_ps(0, 4)
                        for i in range(np_):
                            qa, qk = qparts[i]
                            ka, kk = kparts[i]
                            tr.op("pe", lambda e, b=b, i=i, qa=qa, ka=ka: e.matmul(
                                ps[b][:, :], lhsT=ka[:, j * 128:(j + 1) * 128], rhs=qa[:, sl], start=(i == 0), stop=(i == np_ - 1)),
                                reads=[qk, kk], writes=[("ps", b)])

                    def s2(t=t, j=j, r=r, bias_fn=bias_fn):
                        b = t["b"]
                        pi = t["pi"] = self.rot("pt", 4)
                        if bias_fn is None:
                            ba, bk = self.NEGC[:, 0:1], "negc"
                        else:
                            ba, bk = bias_fn(j)
                        if clamp and r >= 0:
                            ti = self.rot("t1")
                            tr.op("dve", lambda e: e.tensor_scalar(out=self.T1[:, ti, :], in0=ps[b][:, :], scalar1=ba, scalar2=60.0,
                                                                   op0=ALU.add, op1=ALU.min),
                                  reads=[("ps", b), bk], writes=[("t1", ti)])
                            tr.op("act", lambda e: e.activation(out=PT[:, pi, :], in_=self.T1[:, ti, :], func=AF.Exp),
                                  reads=[("t1", ti)], writes=[("pt", pi)])
                        else:
                            tr.op("act", lambda e: e.activation(out=PT[:, pi, :], in_=ps[b][:, :], func=AF.Exp, scale=scale, bias=ba),
                                  reads=[("ps", b), bk], writes=[("pt", pi)])
                        if r >= 0:
                            tr.op("dve", lambda e: e.tensor_tensor(out=PT[:, pi, :], in0=PT[:, pi, :], in1=self.CM[:, r, :], op=ALU.mult),
                                  reads=[("pt", pi), "cm"], writes=[("pt", pi)])

                    def s3(t=t, st=st, j=j, n=n, nj=nj, mi=mi, vt=vt, vkey=vkey, po=po):
                        pi, bo, bl = t["pi"], st["bo"], st["bl"]
                        tr.op("pe", lambda e: e.matmul(ps[bo][po:po + dvm, :], lhsT=vt[:, j, :], rhs=PT[:, pi, :], start=(j == 0), stop=(j == nj - 1)),
                              reads=[vkey, ("pt", pi)], writes=[("ps", bo)])
                        tr.op("pe", lambda e: e.matmul(ps[bl][po:po + dvm, :], lhsT=self.ONESB[:, 0:dvm], rhs=PT[:, pi, :], start=(j == 0), stop=(j == nj - 1)),
                              reads=["onesb", ("pt", pi)], writes=[("ps", bl)])
                        if j == nj - 1 and mi == nm - 1:
                            self.attn_finish(bo, bl, dtot, out_ap_fn(n), out_key_fn(n))
                            if n == 1 and post is not None:
                                post()
                    tasks.append((s1, s2, s3))
        return tasks

    def mem_block(self, L, wout):
        MK, MV = self.mem_kv(L["memT"], L["norm_mem"], L["mem_w_kv"], L["mem_k_norm"])
        self.mem_attn(MK, MV, 0)
        self.barrier(self.htkeys() + self.hdkeys() + [("memn", k) for k in range(KC)] + [("mk", h) for h in range(4)] + [("mv", j) for j in range(2)])
        self.outproj_group(wout[2048:2560, :].rearrange("(kt p) d -> p kt d", p=128),
                           [(self.AT[0][:, m, :], ("cat", m)) for m in range(4)], 1.0)

    def mixer_mla(self, hf, L, S):
        tr = self.tr
        ps, WR, HT, GV2 = self.ps, self.WR, self.HT, self.GV2
        cc = self.cc
        c0 = 0 if cc else hf * T
        if cc:
            hf = 1
        w_in = L["w_in"].rearrange("(kc p) f -> p kc f", p=128)
        tr.dma("sp", GV2[:, 17:18], L["mem_q_norm"], writes=["gv2c"])
        tr.dma("sp", GV2[:, 20:24], L["q_a_norm"], writes=["gvm"])
        tr.dma("sp", GV2[:, 24:28], L["kv_a_norm"], writes=["gvm"])
        tr.dma("sp", GV2[:, 28:32], L["qk_cols"], writes=["gvm"])
        tr.dma("sp", self.SCR[0:64, 0:T], L["cos"][:, c0:c0 + T], writes=["cs"])
        tr.dma("sp", self.SCR[0:64, T:2 * T], L["sin"][:, c0:c0 + T], writes=["cs"])
        self.rmsnorm_x(L["norm_mix"])
        self.linear_fm(self.src_ht, KC, w_in, [(1088, 512, [(i * 128, 128) for i in range(4)])], self.mq_epi(1088))
        def krope_epi(c, w, n, b):
            self.stage_out(64, lambda o, k: self.rope_to(b, GV2[0:64, 31:32], "gvm", n * 512, o, k),
                           S["KR"][:, c0 + n * 512:c0 + (n + 1) * 512], ("krd", hf, n))
        self.linear_fm(self.src_ht, KC, w_in, [(1024, 64, [(0, 64)])], krope_epi)
        a32keys = [("a32", c, n) for c in range(4) for n in range(2)]
        self.barrier(self.atkeys() + a32keys)
        self.linear_fm(self.src_ht, KC, w_in, [(0, 512, [(i * 128, 128) for i in range(4)])], self.raw_epi(0))
        self.lat_norm(20, 512.0)
        wq = L["w_q_b"].rearrange("(kc p) f -> p kc f", p=128)
        def q_epi(c, w, n, b):
            h = c // 192
            if c % 192 == 0:
                self.stage_out(128, lambda o, k: self.norm_to(b, 128, 128.0, GV2[:, 28:29], "gvm", o, k),
                               S["QN"][h, :, n * 512:(n + 1) * 512], ("qnd", h, n))
            else:
                self.stage_out(64, lambda o, k: self.rope_to(b, GV2[0:64, 29:30], "gvm", n * 512, o, k),
                               S["QR"][h, :, n * 512:(n + 1) * 512], ("qrd", h, n))
        self.linear_fm(self.src_lat, 4, wq, [(384 * g, 384, [(0, 128), (128, 64), (192, 128), (320, 64)]) for g in range(8)], q_epi)
        self.linear_fm(self.src_ht, KC, w_in, [(512, 512, [(i * 128, 128) for i in range(4)])], self.raw_epi(512))
        self.lat_norm(24, 512.0)
        self.barrier(self.atkeys() + a32keys)
        wkv = L["w_kv_b"].rearrange("(kc p) f -> p kc f", p=128)
        pend_e = []

        def flush_e():
            b_, h_, n_ = pend_e.pop(0)
            self.stage_out(128, lambda o, k: self.norm_to(b_, 128, 128.0, GV2[:, 30:31], "gvm", o, k),
                           S["KN"][h_, :, c0 + n_ * 512:c0 + (n_ + 1) * 512], ("knd", h_, hf, n_))
        for g in range(8):
            s = self.wload(wkv[:, 0:4, 512 * g:512 * (g + 1)])
            for hh in range(2):
                h = 2 * g + hh
                for n in range(2):
                    b = self.next_ps(0, 4)
                    for kc in range(4):
                        tr.op("pe", lambda e, b=b, kc=kc, hh=hh, n=n, s=s: e.matmul(
                            ps[b][:, :], lhsT=WR[:, s, kc, hh * 256:hh * 256 + 128], rhs=self.LAT[:, kc, n * 512:(n + 1) * 512],
                            start=(kc == 0), stop=(kc == 3)), reads=[("w", s), ("lat", kc, n)], writes=[("ps", b)])
                    pend_e.append((b, h, n))
                    if len(pend_e) > 2:
                        flush_e()
            for tt in range(8):
                b = self.next_ps(4, 8)
                for kc in range(4):
                    tr.op("pe", lambda e, b=b, kc=kc, tt=tt, s=s: e.matmul(
                        ps[b][:, :], lhsT=self.LAT[:, kc, tt * 128:(tt + 1) * 128], rhs=WR[:, s, kc, :],
                        start=(kc == 0), stop=(kc == 3)), reads=[("w", s), ("lat", kc, tt // 4)], writes=[("ps", b)])
                pi = self.rot("pt", 4)
                tr.op("act", lambda e, b=b, pi=pi: e.copy(
                    out=self.PT[:, pi, 0:256].rearrange("p (h d) -> p h d", d=128),
                    in_=ps[b][:, :].rearrange("p (h two d) -> p h two d", two=2, d=128)[:, :, 1, :]),
                    reads=[("ps", b)], writes=[("pt", pi)])
                tr.dma("sp", S["V"][c0 + tt * 128:c0 + (tt + 1) * 128, 256 * g:256 * (g + 1)], self.PT[:, pi, 0:256],
                       reads=[("pt", pi)], writes=[("vd", g, hf, tt)])
        while pend_e:
            flush_e()
        wout = L["w_out"]
        if cc:
            self.allgather_rows(S["KNo"], S["KNr"], 1024, [("knd", h2, 1, n2) for h2 in range(16) for n2 in range(2)], "knr")
            tr.allgather(S["KR"], S["KRr"], reads=[("krd", 1, n2) for n2 in range(2)], writes=["krr"])
            self.allgather_rows(S["V"], S["Vr"], 512, [("vd", g2, 1, tt) for g2 in range(8) for tt in range(8)], "vr")
        self.mem_block(L, wout)
        hb = HT[:, :, :].rearrange("p a t -> p (a t)")
        Sk = (hf + 1) * T
        KRt = hb[0:64, 12288:12288 + SEQ]
        if cc:
            tr.dma("sp", KRt[:, 0:T], S["KRr"][0:64, :], reads=["krr"], writes=[("hd", "kr")])
            tr.dma("sp", KRt[:, T:2 * T], S["KR"], reads=[("krd", 1, n2) for n2 in range(2)], writes=[("hd", "kr")])
        else:
            krk = [("krd", h2, n2) for h2 in range(hf + 1) for n2 in range(2)]
            tr.dma("sp", KRt[:, 0:Sk], S["KR"][:, 0:Sk], reads=krk, writes=[("hd", "kr")])
        scale = 1.0 / math.sqrt(192.0)
        tasks = []
        for h in range(16):
            st_ = h % 2
            base = st_ * 6144
            QNt = hb[:, base:base + T]
            QRt = hb[0:64, base + T:base + 2 * T]
            KNt = hb[:, base + 2 * T:base + 2 * T + SEQ]
            Vt = hb[:, base + 4 * T:base + 4 * T + SEQ].rearrange("p (j c) -> p j c", c=128)
            ab = (h // 4) % 2
            m = h % 4

            def pre(h=h, st_=st_, QNt=QNt, QRt=QRt, KNt=KNt, Vt=Vt):
                tr.dma("sp", QNt, S["QN"][h], reads=[("qnd", h, n2) for n2 in range(2)], writes=[("hd", st_, "qn")])
                tr.dma("sp", QRt, S["QR"][h], reads=[("qrd", h, n2) for n2 in range(2)], writes=[("hd", st_, "qr")])
                if cc:
                    tr.dma("sp", KNt[:, 0:T], self.r0rows(S["KNr"], 1024, 128 * h, 128), reads=[("knr", h // 8)], writes=[("hd", st_, "kn")])
                    tr.dma("sp", KNt[:, T:2 * T], S["KN"][h], reads=[("knd", h, 1, n2) for n2 in range(2)], writes=[("hd", st_, "kn")])
                    for i2 in range(2):
                        tr.dma("sp", Vt[:, 4 * i2:4 * i2 + 4, :],
                               self.r0rows(S["Vr"], 512, 512 * i2, 512)[:, 128 * h:128 * (h + 1)].rearrange("(j p) c -> p j c", p=128),
                               reads=[("vr", i2)], writes=[("hd", st_, "v")])
                    tr.dma("sp", Vt[:, 8:16, :], S["V"][:, 128 * h:128 * (h + 1)].rearrange("(j p) c -> p j c", p=128),
                           reads=[("vd", h // 2, 1, tt) for tt in range(8)], writes=[("hd", st_, "v")])
                    return
                tr.dma("sp", KNt[:, 0:Sk], S["KN"][h, :, 0:Sk], reads=[("knd", h, h2, n2) for h2 in range(hf + 1) for n2 in range(2)],
                       writes=[("hd", st_, "kn")])
                tr.dma("sp", Vt[:, 0:Sk // 128, :], S["V"][0:Sk, 128 * h:128 * (h + 1)].rearrange("(j p) c -> p j c", p=128),
                       reads=[("vd", h // 2, h2, tt) for h2 in range(hf + 1) for tt in range(8)], writes=[("hd", st_, "v")])

            post = None
            if m == 3:
                def post(h=h, ab=ab):
                    g4 = h // 4
                    self.outproj_group(wout[512 * g4:512 * (g4 + 1), :].rearrange("(kt p) d -> p kt d", p=128),
                                       [(self.AT[ab][:, mm, :], ("cat", 4 * ab + mm)) for mm in range(4)], 1.0)
            mem_ = dict(q=[(QNt, ("hd", st_, "qn")), (QRt, ("hd", st_, "qr"))], k=[(KNt, ("hd", st_, "kn")), (KRt, ("hd", "kr"))],
                        v=Vt, vkey=("hd", st_, "v"), pre=pre, po=0,
                        bias_fn=(lambda j: (self.PCOL[:, 0:1], "pcol") if j < 8 else (self.NEGC[:, 0:1], "negc")) if cc else None)
            tasks += self.attn_tasks(hf, [mem_], 128, scale,
                                     lambda n, ab=ab, m=m: self.AT[ab][:, m, n * 512:(n + 1) * 512],
                                     lambda n, ab=ab, m=m: ("cat", 4 * ab + m, n), post=post)
        self.run_pipeline(tasks)
        self.barrier(self.htkeys() + self.hdkeys())

    def mixer_swa(self, hf, L, S):
        tr = self.tr
        ps, WR, HT, GV2, PT = self.ps, self.WR, self.HT, self.GV2, self.PT
        cc = self.cc
        c0 = 0 if cc else hf * T
        if cc:
            hf = 1
        W = 1152
        w_in = L["w_in"].rearrange("(kc p) f -> p kc f", p=128)
        tr.dma("sp", GV2[:, 17:18], L["mem_q_norm"], writes=["gv2c"])
        tr.dma("sp", GV2[:, 28:30], L["qk_cols"], writes=["gvm"])
        SK = GV2[:, 32:48]
        tr.dma("sp", SK, L["sinks_pair"], writes=["sk"])
        tr.op("act", lambda e: e.activation(out=SK, in_=SK, func=AF.Exp, bias=self.NEGC[:, 0:1]), reads=["sk", "negc"], writes=["sk"])
        if hf == 0 or cc:
            GROW = self.SCR[0:32, 0:W]
            tr.op("dve", lambda e: e.memset(GROW, NEG), writes=["grow"])
            tr.dma("sp", self.T1[0:32, 0, 0:32], L["rel_bias"], writes=[("t1", 0)])
            tr.dma("sp", self.T1[0:32, 1, 0:128], L["onehot"], writes=[("t1", 1)])
            b = self.next_ps(0, 4)
            tr.op("pe", lambda e: e.matmul(ps[b][0:32, 0:128], lhsT=self.T1[0:32, 0, 0:32], rhs=self.T1[0:32, 1, 0:128], start=True, stop=True),
                  reads=[("t1", 0), ("t1", 1)], writes=[("ps", b)])
            tr.op("act", lambda e: e.copy(out=GROW[:, 511:639], in_=ps[b][0:32, 0:128]), reads=[("ps", b), "grow"], writes=["grow"])
            for q4 in range(8):
                tr.dma("sp", S["GD"][:, 16 * q4:16 * (q4 + 1), :], GROW.unsqueeze(1).broadcast_to([32, 16, W]), reads=["grow"], writes=[("gd", q4)])
        self.rmsnorm_x(L["norm_mix"])
        self.linear_fm(self.src_ht, KC, w_in, [(2560, 512, [(i * 128, 128) for i in range(4)])], self.mq_epi(2560))
        def q_epi(c, w, n, b):
            cc = c // 128
            pi = self.rot("pt", 4)
            self.norm_to(b, 128, 64.0, GV2[:, 28:29], "gvm", PT[:, pi, :], ("pt", pi), blk64=True)
            for hh in range(2):
                tr.dma("sp", S["Q"][2 * cc + hh, :, n * 512:(n + 1) * 512], PT[64 * hh:64 * hh + 64, pi, :], reads=[("pt", pi)],
                       writes=[("qd", 2 * cc + hh, n)])
        self.linear_fm(self.src_ht, KC, w_in, [(512 * g, 512, [(i * 128, 128) for i in range(4)]) for g in range(4)], q_epi)
        def k_epi(c, w, n, b):
            cc = (c - 2048) // 128
            pi = self.rot("pt", 4)
            self.norm_to(b, 128, 64.0, GV2[:, 29:30], "gvm", PT[:, pi, :], ("pt", pi), blk64=True)
            for hh in range(2):
                tr.dma("sp", S["K"][2 * cc + hh, :, c0 + n * 512:c0 + (n + 1) * 512], PT[64 * hh:64 * hh + 64, pi, :], reads=[("pt", pi)],
                       writes=[("kd", 2 * cc + hh, hf, n)])
                if self.cc and n == 1:
                    tr.dma("sp", S["KHs"][64 * (2 * cc + hh):64 * (2 * cc + hh + 1), :], PT[64 * hh:64 * hh + 64, pi, 384:512], reads=[("pt", pi)],
                           writes=[("khs", 2 * cc + hh)])
        self.linear_fm(self.src_ht, KC, w_in, [(2048, 256, [(0, 128), (128, 128)])], k_epi)
        slots = [self.wload(w_in[:, 4 * s:4 * s + 4, 2304:2560]) for s in range(4)]
        for tt in range(8):
            b = self.next_ps(0, 4)
            for kc in range(KC):
                tr.op("pe", lambda e, b=b, kc=kc, tt=tt, s=slots[kc // 4]: e.matmul(
                    ps[b][:, 0:256], lhsT=HT[:, kc, tt * 128:(tt + 1) * 128], rhs=WR[:, s, kc % 4, 0:256],
                    start=(kc == 0), stop=(kc == KC - 1)), reads=[("w", slots[kc // 4]), ("ht", kc, tt // 4)], writes=[("ps", b)])
            pi = self.rot("pt", 4)
            tr.op("act", lambda e, b=b, pi=pi: e.copy(out=PT[:, pi, 0:256], in_=ps[b][:, 0:256]), reads=[("ps", b)], writes=[("pt", pi)])
            tr.dma("sp", S["V"][c0 + tt * 128:c0 + (tt + 1) * 128, :], PT[:, pi, 0:256], reads=[("pt", pi)], writes=[("vd", hf, tt)])
            if cc and tt == 7:
                tr.dma("sp", S["VHs"], PT[:, pi, 0:256], reads=[("pt", pi)], writes=["vhs"])
        if cc:
            tr.allgather(S["KHs"], S["KHr"], reads=[("khs", g2) for g2 in range(4)], writes=["khr"])
            tr.allgather(S["VHs"], S["VHr"], reads=["vhs"], writes=["vhr"])
        wout = L["w_out"]
        self.mem_block(L, wout)
        hb = HT[:, :, :].rearrange("p a t -> p (a t)")
        h32 = HT[:, :, :].bitcast(F32).rearrange("p a t -> p (a t)")
        Sk = (hf + 1) * T
        scale = 1.0 / 8.0
        jbase = hf * 8
        tasks = []
        for pr in range(16):
            g = (2 * pr) // 8
            pst = pr % 2
            gs_ = g % 2
            Kt = hb[0:64, 4096 + 2048 * gs_:4096 + 2048 * (gs_ + 1)]
            Vt = hb[:, 8192 + 1024 * gs_:8192 + 1024 * (gs_ + 1)].rearrange("p (j c) -> p j c", c=64)
            ab = (pr // 4) % 2
            m = pr % 4
            pres, Qts, BBs, sks = [], [], [], []
            for hh in range(2):
                h = 2 * pr + hh
                sk = 2 * pst + hh
                Qt = hb[0:64, 1024 * sk:1024 * (sk + 1)]
                BB = h32[:, 5120 + 256 * sk:5120 + 256 * (sk + 1)]

                def pre(h=h, g=g, sk=sk, gs_=gs_, Qt=Qt, Kt=Kt, Vt=Vt, BB=BB):
                    tr.dma("sp", Qt, S["Q"][h], reads=[("qd", h, n2) for n2 in range(2)], writes=[("hd", sk, "q")])
                    if h % 8 == 0 and cc:
                        tr.dma("sp", Kt[:, 0:128], S["KHr"][64 * g:64 * (g + 1), :], reads=["khr"], writes=[("hd", gs_, "k")])
                        tr.dma("sp", Kt[:, 128:128 + T], S["K"][g], reads=[("kd", g, 1, n2) for n2 in range(2)], writes=[("hd", gs_, "k")])
                        tr.dma("sp", Vt[:, 0, :], S["VHr"][0:128, 64 * g:64 * (g + 1)], reads=["vhr"], writes=[("hd", gs_, "v")])
                        tr.dma("sp", Vt[:, 1:9, :], S["V"][:, 64 * g:64 * (g + 1)].rearrange("(j p) c -> p j c", p=128),
                               reads=[("vd", 1, tt) for tt in range(8)], writes=[("hd", gs_, "v")])
                    elif h % 8 == 0:
                        tr.dma("sp", Kt[:, 0:Sk], S["K"][g, :, 0:Sk], reads=[("kd", g, h2, n2) for h2 in range(hf + 1) for n2 in range(2)],
                               writes=[("hd", gs_, "k")])
                        tr.dma("sp", Vt[:, 0:Sk // 128, :], S["V"][0:Sk, 64 * g:64 * (g + 1)].rearrange("(j p) c -> p j c", p=128),
                               reads=[("vd", h2, tt) for h2 in range(hf + 1) for tt in range(8)], writes=[("hd", gs_, "v")])
                    src = bass.AP(S["GDh"], h * 128 * W + 511, [[W - 1, 128], [1, 256]])
                    tr.dma("sp", BB, src, reads=[("gd", q4) for q4 in range(8)], writes=[("hd", "b", sk)])
                pres.append(pre); Qts.append(Qt); BBs.append(BB); sks.append(sk)

            for n in range(2):
                js = [((jbase + 4 * n + r) - (7 if cc else 0), r) for r in range(-1, 4) if jbase + 4 * n + r >= 0]
                has_prev = js[0][1] == -1
                st = {}
                for hh in range(2):
                    for idx, (j, r) in enumerate(js):
                        lo = max(0, 128 * r)
                        hi = min(512, 128 * r + 256)
                        w = hi - lo
                        boff = lo - 128 * r
                        t = {}
                        last = (idx == len(js) - 1) and hh == 1

                        def s1(t=t, st=st, idx=idx, j=j, n=n, lo=lo, hi=hi, w=w, pre=pres[hh], gs_=gs_, sk=sks[hh], Kt=Kt, Qt=Qts[hh], hh=hh):
                            if idx == 0 and n == 0:
                                pre()
                            if idx == 0 and hh == 0:
                                st["bo"] = self.next_ps(4, 6)
                                st["bl"] = self.next_ps(6, 8)
                            b = t["b"] = self.next_ps(0, 4)
                            tr.op("pe", lambda e: e.matmul(ps[b][:, 0:w], lhsT=Kt[:, j * 128:(j + 1) * 128], rhs=Qt[:, n * 512 + lo:n * 512 + hi],
                                                           start=True, stop=True),
                                  reads=[("hd", gs_, "k"), ("hd", sk, "q")], writes=[("ps", b)])

                        def s2(t=t, w=w, boff=boff, BB=BBs[hh], sk=sks[hh], jj=j):
                            b = t["b"]
                            ti = self.rot("t1")
                            tr.op("dve", lambda e: e.scalar_tensor_tensor(out=self.T1[:, ti, 0:w], in0=ps[b][:, 0:w], scalar=scale,
                                                                          in1=BB[:, boff:boff + w], op0=ALU.mult, op1=ALU.add),
                                  reads=[("ps", b), ("hd", "b", sk)], writes=[("t1", ti)])
                            pi = t["pi"] = self.rot("pt", 4)
                            ebias = self.PCOL[:, 0:1] if (cc and jj == 0) else self.NEGC[:, 0:1]
                            tr.op("act", lambda e: e.activation(out=PT[:, pi, 0:w], in_=self.T1[:, ti, 0:w], func=AF.Exp, bias=ebias),
                                  reads=[("t1", ti), "negc", "pcol"], writes=[("pt", pi)])

                        def s3(t=t, st=st, j=j, n=n, lo=lo, hi=hi, last=last, gs_=gs_, Vt=Vt, pr=pr, ab=ab, m=m, r=r, has_prev=has_prev, po=64 * hh):
                            pi, bo, bl = t["pi"], st["bo"], st["bl"]
                            for c in range(lo // 128, hi // 128):
                                first = (c == r + 1) or (c == 0 and not has_prev)
                                stp = (c == r)
                                pof = 128 * c - lo
                                tr.op("pe", lambda e, c=c, first=first, stp=stp, pof=pof: e.matmul(
                                    ps[bo][po:po + 64, 128 * c:128 * (c + 1)], lhsT=Vt[:, j, :], rhs=PT[:, pi, pof:pof + 128], start=first, stop=stp),
                                    reads=[("hd", gs_, "v"), ("pt", pi)], writes=[("ps", bo)])
                                tr.op("pe", lambda e, c=c, first=first, stp=stp, pof=pof: e.matmul(
                                    ps[bl][po:po + 64, 128 * c:128 * (c + 1)], lhsT=self.ONESB[:, 0:64], rhs=PT[:, pi, pof:pof + 128], start=first, stop=stp),
                                    reads=["onesb", ("pt", pi)], writes=[("ps", bl)])
                            if last:
                                sl = slice(n * 512, (n + 1) * 512)
                                self.attn_finish(bo, bl, 128, self.AT[ab][:, m, sl], ("cat", 4 * ab + m, n), add_col=SK[:, pr:pr + 1], add_key="sk")
                                if n == 1 and m == 3:
                                    g4 = pr // 4
                                    self.outproj_group(wout[512 * g4:512 * (g4 + 1), :].rearrange("(kt p) d -> p kt d", p=128),
                                                       [(self.AT[ab][:, mm, :], ("cat", 4 * ab + mm)) for mm in range(4)], 1.0)
                        tasks.append((s1, s2, s3))
        self.run_pipeline(tasks)
        self.barrier(self.htkeys() + self.hdkeys())

    def mixer_fox(self, hf, L, S):
        tr = self.tr
        ps, WR, HT, GV2, PT, T1 = self.ps, self.WR, self.HT, self.GV2, self.PT, self.T1
        cc = self.cc
        c0 = 0 if cc else hf * T
        if cc:
            hf = 1
        w_in = L["w_in"].rearrange("(kc p) f -> p kc f", p=128)
        tr.dma("sp", GV2[:, 17:18], L["mem_q_norm"], writes=["gv2c"])
        tr.dma("sp", GV2[:, 28:30], L["qk_cols"], writes=["gvm"])
        tr.op("dve", lambda e: e.tensor_scalar(out=GV2[:, 28:29], in0=GV2[:, 28:29], scalar1=1.0 / 8.0, scalar2=None, op0=ALU.mult),
              reads=["gvm"], writes=["gvm"])
        BF = GV2[:, 32:64]
        tr.dma("sp", BF, L["bf_bc"], writes=["bf"])
        IDN = self.SCR[:, 0:128]
        TRIU = self.SCR[:, 128:256]
        NLF = self.SCR[:, 256:512].rearrange("p (j c) -> p j c", c=32)
        CSN = self.SCR[:, 512:768].rearrange("p (j c) -> p j c", c=32)
        tr.dma("sp", IDN, L["ident"], writes=["idn"])
        tr.dma("sp", TRIU, L["triu"], writes=["triu"])
        self.rmsnorm_x(L["norm_mix"])
        self.linear_fm(self.src_ht, KC, w_in, [(6176, 512, [(i * 128, 128) for i in range(4)])], self.mq_epi(6176))
        slots = [self.wload(w_in[:, 4 * s:4 * s + 4, 6144:6176]) for s in range(4)]
        for tt in range(8):
            b = self.next_ps(0, 4)
            for kc in range(KC):
                tr.op("pe", lambda e, b=b, kc=kc, tt=tt, s=slots[kc // 4]: e.matmul(
                    ps[b][:, 0:32], lhsT=HT[:, kc, tt * 128:(tt + 1) * 128], rhs=WR[:, s, kc % 4, 0:32],
                    start=(kc == 0), stop=(kc == KC - 1)), reads=[("w", slots[kc // 4]), ("ht", kc, tt // 4)], writes=[("ps", b)])
            tr.op("dve", lambda e, b=b, tt=tt: e.tensor_tensor(out=NLF[:, tt, :], in0=ps[b][:, 0:32], in1=BF, op=ALU.add),
                  reads=[("ps", b), "bf"], writes=[("nlf", tt)])
            tr.op("act", lambda e, tt=tt: e.activation(out=NLF[:, tt, :], in_=NLF[:, tt, :], func=AF.Exp, scale=-1.0), reads=[("nlf", tt)], writes=[("nlf", tt)])
            tr.op("act", lambda e, tt=tt: e.activation(out=NLF[:, tt, :], in_=NLF[:, tt, :], func=AF.Ln, bias=self.ONES[:, 0:1]),
                  reads=[("nlf", tt), "ones"], writes=[("nlf", tt)])
        for tt in range(8):
            b = self.next_ps(0, 4)
            for ts in range(tt + 1):
                tri = TRIU if ts == tt else self.ONES[:, :]
                tr.op("pe", lambda e, b=b, ts=ts, tt=tt, tri=tri: e.matmul(ps[b][:, 0:32], lhsT=tri, rhs=NLF[:, ts, :], start=(ts == 0), stop=(ts == tt)),
                      reads=[("nlf", ts), "triu", "ones"], writes=[("ps", b)])
            tr.op("act", lambda e, b=b, tt=tt: e.copy(out=CSN[:, tt, :], in_=ps[b][:, 0:32]), reads=[("ps", b)], writes=[("csn", tt)])
            tr.dma("sp", S["CS"][c0 + tt * 128:c0 + (tt + 1) * 128, :], CSN[:, tt, :], reads=[("csn", tt)], writes=[("csd", hf, tt)])
        CQ = T1[0:32, :, :].rearrange("p a t -> p (a t)")
        for n in range(2):
            b = self.next_ps(0, 4)
            for t4 in range(4):
                tt = 4 * n + t4
                tr.op("pe", lambda e, b=b, tt=tt, t4=t4: e.matmul(ps[b][0:32, t4 * 128:(t4 + 1) * 128], lhsT=CSN[:, tt, :], rhs=IDN, start=True, stop=True),
                      reads=[("csn", tt), "idn"], writes=[("ps", b)])
            tr.op("act", lambda e, b=b, n=n: e.mul(out=CQ[:, n * 512:(n + 1) * 512], in_=ps[b][0:32, :], mul=-1.0), reads=[("ps", b)], writes=[("t1", n)])
        cqk = [("t1", 0), ("t1", 1)]
        SPL = self.LAT[0:32, 0:3, :]
        R1 = self.RS[0:32, :, :].rearrange("p a t -> p (a t)")
        tr.op("dve", lambda e: e.tensor_copy(out=SPL[:, 0, :], in_=CQ), reads=cqk, writes=[("lat", 0, 0)])
        tr.op("dve", lambda e: e.tensor_tensor(out=R1, in0=CQ, in1=SPL[:, 0, :], op=ALU.subtract), reads=cqk + [("lat", 0, 0)], writes=[("rs", 0), ("rs", 1)])
        tr.op("dve", lambda e: e.tensor_copy(out=SPL[:, 1, :], in_=R1), reads=[("rs", 0)], writes=[("lat", 1, 0)])
        tr.op("dve", lambda e: e.tensor_tensor(out=R1, in0=R1, in1=SPL[:, 1, :], op=ALU.subtract), reads=[("rs", 0), ("lat", 1, 0)], writes=[("rs", 0), ("rs", 1)])
        tr.op("dve", lambda e: e.tensor_copy(out=SPL[:, 2, :], in_=R1), reads=[("rs", 0)], writes=[("lat", 2, 0)])
        for i in range(3):
            tr.dma("sp", S["Q"][:, 64 + i, :], SPL[:, i, :], reads=[("lat", i, 0)], writes=[("qaug", i)])
        pi = self.rot("pt", 4)
        tr.op("dve", lambda e: e.memset(PT[0:32, pi, :], 1.0), writes=[("pt", pi)])
        for i in range(3):
            for n in range(2):
                tr.dma("sp", S["K"][:, 64 + i, c0 + n * 512:c0 + (n + 1) * 512], PT[0:32, pi, :], reads=[("pt", pi)], writes=[("kaug", hf, i, n)])
        def mk_epi(dst, col0, gc, keyname, cbase):
            def epi(c, w, n, b):
                cc = (c - col0) // 128
                pi = self.rot("pt", 4)
                self.norm_to(b, 128, 64.0, GV2[:, gc:gc + 1], "gvm", PT[:, pi, :], ("pt", pi), blk64=True)
                for hh in range(2):
                    tr.dma("sp", dst[2 * cc + hh, 0:64, cbase + n * 512:cbase + (n + 1) * 512], PT[64 * hh:64 * hh + 64, pi, :],
                           reads=[("pt", pi)], writes=[(keyname, 2 * cc + hh, hf if keyname == "kd" else 0, n)])
            return epi
        self.linear_fm(self.src_ht, KC, w_in, [(512 * g, 512, [(i * 128, 128) for i in range(4)]) for g in range(4)],
                       mk_epi(S["Q"], 0, 28, "qd", 0))
        self.linear_fm(self.src_ht, KC, w_in, [(2048 + 512 * g, 512, [(i * 128, 128) for i in range(4)]) for g in range(4)],
                       mk_epi(S["K"], 2048, 29, "kd", c0))
        for g in range(4):
            slots = [self.wload(w_in[:, 4 * s:4 * s + 4, 4096 + 512 * g:4096 + 512 * (g + 1)]) for s in range(4)]
            for tt in range(8):
                b = self.next_ps(0, 4)
                for kc in range(KC):
                    tr.op("pe", lambda e, b=b, kc=kc, tt=tt, s=slots[kc // 4]: e.matmul(
                        ps[b][:, :], lhsT=HT[:, kc, tt * 128:(tt + 1) * 128], rhs=WR[:, s, kc % 4, :],
                        start=(kc == 0), stop=(kc == KC - 1)), reads=[("w", slots[kc // 4]), ("ht", kc, tt // 4)], writes=[("ps", b)])
                pi = self.rot("pt", 4)
                tr.op("act", lambda e, b=b, pi=pi: e.copy(out=PT[:, pi, :], in_=ps[b][:, :]), reads=[("ps", b)], writes=[("pt", pi)])
                tr.dma("sp", S["V"][c0 + tt * 128:c0 + (tt + 1) * 128, 512 * g:512 * (g + 1)], PT[:, pi, :], reads=[("pt", pi)],
                       writes=[("vd", g, hf, tt)])
        if cc:
            own_k = [("kd", h2, 1, n2) for h2 in range(32) for n2 in range(2)] + [("kaug", 1, i, n2) for i in range(3) for n2 in range(2)]
            self.allgather_rows(S["Ko"], S["Kr"], 536, own_k, "kr")
            self.allgather_rows(S["V"], S["Vr"], 512, [("vd", g2, 1, tt) for g2 in range(4) for tt in range(8)], "vr")
            tr.allgather(S["CS"], S["CSr"], reads=[("csd", 1, tt) for tt in range(8)], writes=["csr"])
        wout = L["w_out"]
        self.mem_block(L, wout)
        hb = HT[:, :, :].rearrange("p a t -> p (a t)")
        h32 = HT[:, :, :].bitcast(F32).rearrange("p a t -> p (a t)")
        Sk = (hf + 1) * T
        l32 = self.LAT[:, :, :].bitcast(F32).rearrange("p a t -> p (a t)")
        BKt = l32[:, 0:512].rearrange("p (j c) -> p j c", c=32)
        TOT = l32[:, 512:544]
        self.barrier([("lat", c2, n2) for c2 in range(4) for n2 in range(2)] + [("hd", "bk"), ("hd", "tot")])
        csk = [("csd", h2, tt) for h2 in range(hf + 1) for tt in range(8)]
        if cc:
            tr.dma("sp", BKt[:, 0:8, :], S["CSr"][0:T, :].rearrange("(j p) c -> p j c", p=128), reads=["csr"], writes=[("hd", "bk")])
            tr.dma("sp", BKt[:, 8:16, :], S["CS"].rearrange("(j p) c -> p j c", p=128), reads=[("csd", 1, tt) for tt in range(8)], writes=[("hd", "bk")])
            tr.dma("sp", TOT, S["CSr"][T - 1:T, :].partition_broadcast(128), reads=["csr"], writes=[("hd", "tot")])
        else:
            tr.dma("sp", BKt[:, 0:Sk // 128, :], S["CS"][0:Sk, :].rearrange("(j p) c -> p j c", p=128), reads=csk, writes=[("hd", "bk")])
            if hf == 1:
                tr.dma("sp", TOT, S["CS"][T - 1:T, :].partition_broadcast(128), reads=csk, writes=[("hd", "tot")])
        if hf == 1:
            tr.op("dve", lambda e: e.tensor_tensor(out=BKt[:, 0:8, :], in0=BKt[:, 0:8, :], in1=TOT.unsqueeze(1).broadcast_to([128, 8, 32]), op=ALU.subtract),
                  reads=[("hd", "bk"), ("hd", "tot")], writes=[("hd", "bk")])
        if cc:
            tr.op("dve", lambda e: e.tensor_scalar(out=BKt[:, 0:8, :], in0=BKt[:, 0:8, :], scalar1=self.PCOL[:, 1:2], scalar2=None, op0=ALU.add),
                  reads=[("hd", "bk"), "pcol"], writes=[("hd", "bk")])
        tr.op("dve", lambda e: e.tensor_scalar(out=BKt[:, 0:Sk // 128, :], in0=BKt[:, 0:Sk // 128, :], scalar1=-CSHIFT, scalar2=None, op0=ALU.add),
              reads=[("hd", "bk")], writes=[("hd", "bk")])
        tasks = []
        for pr in range(16):
            pst = pr % 2
            members = []
            for hh in range(2):
                h = 2 * pr + hh
                base = (2 * pst + hh) * 4096
                Qt = hb[0:67, base:base + T]
                Kt = hb[0:67, base + T:base + T + SEQ]
                Vt = hb[:, base + 3 * T:base + 4 * T].rearrange("p (j c) -> p j c", c=64)
                sk = 2 * pst + hh

                def pre(h=h, sk=sk, Qt=Qt, Kt=Kt, Vt=Vt):
                    tr.dma("sp", Qt, S["Q"][h], reads=[("qd", h, 0, n2) for n2 in range(2)] + [("qaug", i) for i in range(3)], writes=[("hd", sk, "q")])
                    if cc:
                        tr.dma("sp", Kt[:, 0:T], self.r0rows(S["Kr"], 536, 67 * h, 67), reads=[("kr", h // 8)], writes=[("hd", sk, "k")])
                        tr.dma("sp", Kt[:, T:2 * T], S["K"][h], reads=[("kd", h, 1, n2) for n2 in range(2)] +
                               [("kaug", 1, i, n2) for i in range(3) for n2 in range(2)], writes=[("hd", sk, "k")])
                        for i2 in range(2):
                            tr.dma("sp", Vt[:, 4 * i2:4 * i2 + 4, :],
                                   self.r0rows(S["Vr"], 512, 512 * i2, 512)[:, 64 * h:64 * (h + 1)].rearrange("(j p) c -> p j c", p=128),
                                   reads=[("vr", i2)], writes=[("hd", sk, "v")])
                        tr.dma("sp", Vt[:, 8:16, :], S["V"][:, 64 * h:64 * (h + 1)].rearrange("(j p) c -> p j c", p=128),
                               reads=[("vd", h // 8, 1, tt) for tt in range(8)], writes=[("hd", sk, "v")])
                        return
                    tr.dma("sp", Kt[:, 0:Sk], S["K"][h, :, 0:Sk], reads=[("kd", h, h2, n2) for h2 in range(hf + 1) for n2 in range(2)] +
                           [("kaug", h2, i, n2) for h2 in range(hf + 1) for i in range(3) for n2 in range(2)], writes=[("hd", sk, "k")])
                    tr.dma("sp", Vt[:, 0:Sk // 128, :], S["V"][0:Sk, 64 * h:64 * (h + 1)].rearrange("(j p) c -> p j c", p=128),
                           reads=[("vd", h // 8, h2, tt) for h2 in range(hf + 1) for tt in range(8)], writes=[("hd", sk, "v")])

                members.append(dict(q=[(Qt, ("hd", sk, "q"))], k=[(Kt, ("hd", sk, "k"))], v=Vt, vkey=("hd", sk, "v"), pre=pre, po=64 * hh,
                                    bias_fn=lambda j, h=h: (BKt[:, j, h:h + 1], ("hd", "bk"))))
            ab = (pr // 4) % 2
            m = pr % 4
            post = None
            if m == 3:
                def post(pr=pr, ab=ab):
                    g4 = pr // 4
                    self.outproj_group(wout[512 * g4:512 * (g4 + 1), :].rearrange("(kt p) d -> p kt d", p=128),
                                       [(self.AT[ab][:, mm, :], ("cat", 4 * ab + mm)) for mm in range(4)], 1.0)
            tasks += self.attn_tasks(hf, members, 64, 1.0,
                                     lambda n, ab=ab, m=m: self.AT[ab][:, m, n * 512:(n + 1) * 512],
                                     lambda n, ab=ab, m=m: ("cat", 4 * ab + m, n), post=post, clamp=True)
        self.run_pipeline(tasks)
        self.barrier(self.htkeys() + self.hdkeys() + [("hd", "bk"), ("hd", "tot")] + [("lat", c2, n2) for c2 in range(4) for n2 in range(2)])

    def barrier(self, keys):
        self.tr.op("dve", lambda e: e.memset(self.DUMMY[:, 0:1], 0.0), writes=list(keys))

    def hdkeys(self):
        return ([("hd", s2, nm) for s2 in range(4) for nm in ("qn", "qr", "kn", "v", "q", "k")] + [("hd", "kr"), ("hd", "bk"), ("hd", "tot")]
                + [("hd", "b", r) for r in range(-1, 4)])

    def htkeys(self):
        return [("ht", kc, n) for kc in range(KC) for n in range(2)]

    def finish(self):
        self.tr.emit(final_waits=self.finals)
        self.st.close()
        return self.nc


def gain_layout(g):
    g = np.asarray(g, dtype=np.float32)
    return np.ascontiguousarray(g.reshape(-1, 128).T)


def col_layout(g, n=128):
    return np.ascontiguousarray(np.asarray(g, dtype=np.float32).reshape(n, 1))


def const_tables():
    k = np.arange(128)[:, None]
    q = np.arange(512)[None, :]
    cm = np.stack([(r * 128 + k <= q).astype(np.float32) for r in range(4)], axis=1)
    bd = np.zeros((128, 128), np.float32)
    bd[:64, :64] = 1.0
    bd[64:, 64:] = 1.0
    rmt = np.zeros((64, 64), np.float32)
    for m in range(32):
        rmt[m + 32, m] = -1.0
    for m in range(32, 64):
        rmt[m - 32, m] = 1.0
    return np.ascontiguousarray(cm), bd, rmt


def rope_tables():
    half = 32
    inv = 10000.0 ** (-np.arange(half, dtype=np.float32) / half)
    ang = np.arange(SEQ, dtype=np.float32)[None, :] * inv[:, None].astype(np.float32)
    cos = np.cos(ang).astype(np.float32)
    sin = np.sin(ang).astype(np.float32)
    return np.ascontiguousarray(np.concatenate([cos, cos], 0)), np.ascontiguousarray(np.concatenate([sin, sin], 0))


def t5_bucket_onehot():
    d = np.arange(128)
    exact = 16
    log_b = exact + (np.log(np.maximum(d, 1) / exact) / np.log(128 / exact) * (32 - exact)).astype(np.int32)
    log_b = np.minimum(log_b, 31)
    bk = np.where(d < exact, d, log_b).astype(np.int32)
    oh = np.zeros((32, 128), np.float32)
    oh[bk, d] = 1.0
    return oh


def bc128(v):
    v = np.asarray(v, dtype=np.float32).reshape(1, -1)
    return np.ascontiguousarray(np.repeat(v, 128, axis=0))


MIX_IN = {0: 6656, 1: 1600, 2: 3072, 3: 6688}


def build_full(n_layers=4, halves=(0, 1), use_ffn=True, cc=False):
    p = Prog(cc=cc)
    if cc:
        halves = (0,)
    xT = p.din("xT", [D, T if cc else SEQ])
    yT = p.dout("yT", [D, T if cc else SEQ])
    if cc:
        pcols = p.din("pcols", [128, 3])
    cm = p.din("cm", [128, 4, 512])
    bd = p.din("bd64", [128, 128])
    rmt = p.din("rmt", [64, 64])
    memT = p.din("memT", [D, MEM])
    Ls = []
    for i in range(n_layers):
        L = {"memT": memT}
        def di(nm, shp, i=i, L=L):
            L[nm] = p.din("l%d_%s" % (i, nm), shp)
        for nm in ("norm_ffn1", "norm_mix", "norm_ffn2", "norm_mem"):
            di(nm, [128, KC])
        for nm in ("w1g", "w1u", "w2g", "w2u"):
            di(nm, [D, FF])
        for nm in ("w1d", "w2d"):
            di(nm, [FF, D])
        di("w_in", [D, MIX_IN[i % 4]])
        di("w_out", [2560, D])
        di("mem_w_kv", [D, 1024])
        di("mem_q_norm", [128, 1])
        di("mem_k_norm", [128, 1])
        k = i % 4
        if k == 0:
            for nm in ("conv_w0", "conv_w1", "conv_w2"):
                di(nm, [128, 16])
            if cc:
                L["S"] = {"ZHs": p.dscr("c_ZHs", [128, 32], F32), "ZHr": p.dscr("c_ZHr", [256, 32], F32)}
        elif k == 1 and cc:
            di("q_a_norm", [128, 4]); di("kv_a_norm", [128, 4]); di("qk_cols", [128, 4])
            di("w_q_b", [512, 3072]); di("w_kv_b", [512, 4096]); di("cos", [64, T]); di("sin", [64, T])
            kno = p.dscr("m_KNo", [2048, T])
            L["S"] = {"QN": p.dscr("m_QN", [16, 128, T]), "QR": p.dscr("m_QR", [16, 64, T]),
                      "KNo": kno, "KN": kno.rearrange("(h p) t -> h p t", p=128), "KNr": p.dscr("m_KNr", [4096, T]),
                      "KR": p.dscr("m_KR", [64, T]), "KRr": p.dscr("m_KRr", [128, T]),
                      "V": p.dscr("m_V", [T, 2048]), "Vr": p.dscr("m_Vr", [2 * T, 2048])}
        elif k == 2 and cc:
            di("qk_cols", [128, 2]); di("sinks_pair", [128, 16]); di("rel_bias", [32, 32]); di("onehot", [32, 128])
            gdh = p.nc.dram_tensor("s_GD", [32, 128, 1152], F32)
            ko = p.dscr("s_Ko", [256, T])
            L["S"] = {"Q": p.dscr("s_Q", [32, 64, T]), "Ko": ko, "K": ko.rearrange("(h p) t -> h p t", p=64), "V": p.dscr("s_V", [T, 256]),
                      "KHs": p.dscr("s_KHs", [256, 128]), "KHr": p.dscr("s_KHr", [512, 128]),
                      "VHs": p.dscr("s_VHs", [128, 256]), "VHr": p.dscr("s_VHr", [256, 256]),
                      "GD": gdh.ap(), "GDh": gdh}
        elif k == 3 and cc:
            di("qk_cols", [128, 2]); di("bf_bc", [128, 32]); di("ident", [128, 128]); di("triu", [128, 128])
            ko = p.dscr("f_Ko", [32 * 67, T])
            L["S"] = {"Q": p.dscr("f_Q", [32, 67, T]), "Ko": ko, "K": ko.rearrange("(h p) t -> h p t", p=67), "Kr": p.dscr("f_Kr", [2 * 32 * 67, T]),
                      "V": p.dscr("f_V", [T, 2048]), "Vr": p.dscr("f_Vr", [2 * T, 2048]),
                      "CS": p.dscr("f_CS", [T, 32], F32), "CSr": p.dscr("f_CSr", [2 * T, 32], F32)}
        elif k == 1:
            di("q_a_norm", [128, 4]); di("kv_a_norm", [128, 4]); di("qk_cols", [128, 4])
            di("w_q_b", [512, 3072]); di("w_kv_b", [512, 4096]); di("cos", [64, SEQ]); di("sin", [64, SEQ])
            L["S"] = {"QN": p.dscr("m_QN", [16, 128, T]), "QR": p.dscr("m_QR", [16, 64, T]), "KN": p.dscr("m_KN", [16, 128, SEQ]),
                      "KR": p.dscr("m_KR", [64, SEQ]), "V": p.dscr("m_V", [SEQ, 2048])}
        elif k == 2:
            di("qk_cols", [128, 2]); di("sinks_pair", [128, 16]); di("rel_bias", [32, 32]); di("onehot", [32, 128])
            gdh = p.nc.dram_tensor("s_GD", [32, 128, 1152], F32)
            L["S"] = {"Q": p.dscr("s_Q", [32, 64, T]), "K": p.dscr("s_K", [4, 64, SEQ]), "V": p.dscr("s_V", [SEQ, 256]),
                      "GD": gdh.ap(), "GDh": gdh}
        else:
            di("qk_cols", [128, 2]); di("bf_bc", [128, 32]); di("ident", [128, 128]); di("triu", [128, 128])
            L["S"] = {"Q": p.dscr("f_Q", [32, 67, T]), "K": p.dscr("f_K", [32, 67, SEQ]), "V": p.dscr("f_V", [SEQ, 2048]),
                      "CS": p.dscr("f_CS", [SEQ, 32], F32)}
        Ls.append(L)
    p.load_consts(cm, bd, rmt)
    if cc:
        p.load_pcols(pcols)
    for hf in halves:
        p.load_x(xT, hf * T)
        for i in range(n_layers):
            L = Ls[i]
            if use_ffn:
                p.ffn(L["norm_ffn1"], L["w1g"], L["w1u"], L["w1d"])
            k = i % 4
            if k == 0:
                p.mixer_conv(hf, L, L.get("S"))
            elif k == 1:
                p.mixer_mla(hf, L, L["S"])
            elif k == 2:
                p.mixer_swa(hf, L, L["S"])
            else:
                p.mixer_fox(hf, L, L["S"])
            if use_ffn:
                p.ffn(L["norm_ffn2"], L["w2g"], L["w2u"], L["w2d"])
        p.store_x(yT, hf * T)
    return p.finish()


def host_inputs(inputs, n_layers=4, cc=False):
    f = lambda a: np.ascontiguousarray(np.asarray(a, dtype=np.float32))
    cmv, bdv, rmtv = const_tables()
    cosv, sinv = rope_tables()
    shared = {"cm": cmv, "bd64": bdv, "rmt": rmtv}
    for i in range(n_layers):
        pre = "l%d_" % i
        k, occ = i % 4, i // 4
        for nm in ("norm_ffn1", "norm_mix", "norm_ffn2", "norm_mem"):
            shared[pre + nm] = gain_layout(inputs[nm][i])
        shared[pre + "w1g"] = f(inputs["ffn1_w_gate"][i]); shared[pre + "w1u"] = f(inputs["ffn1_w_up"][i]); shared[pre + "w1d"] = f(inputs["ffn1_w_down"][i])
        shared[pre + "w2g"] = f(inputs["ffn2_w_gate"][i]); shared[pre + "w2u"] = f(inputs["ffn2_w_up"][i]); shared[pre + "w2d"] = f(inputs["ffn2_w_down"][i])
        shared[pre + "mem_w_kv"] = f(inputs["mem_w_kv"][i])
        shared[pre + "mem_q_norm"] = col_layout(inputs["mem_q_norm"][i])
        shared[pre + "mem_k_norm"] = col_layout(inputs["mem_k_norm"][i])
        if k == 0:
            shared[pre + "w_in"] = f(inputs["conv_w_in"][occ]); shared[pre + "w_out"] = f(inputs["conv_w_out"][occ])
            for t in range(3):
                shared[pre + "conv_w%d" % t] = gain_layout(np.asarray(inputs["conv_w"][occ])[t])
        elif k == 1:
            shared[pre + "w_in"] = f(inputs["mla_w_in"][occ]); shared[pre + "w_out"] = f(inputs["mla_w_out"][occ])
            shared[pre + "q_a_norm"] = gain_layout(inputs["mla_q_a_norm"][occ]); shared[pre + "kv_a_norm"] = gain_layout(inputs["mla_kv_a_norm"][occ])
            qn = np.asarray(inputs["mla_q_norm"][occ], dtype=np.float32); kn = np.asarray(inputs["mla_k_norm"][occ], dtype=np.float32)
            qk = np.zeros((128, 4), np.float32)
            qk[:, 0] = qn[:128]; qk[:64, 1] = qn[128:]; qk[:, 2] = kn[:128]; qk[:64, 3] = kn[128:]
            shared[pre + "qk_cols"] = qk
            shared[pre + "w_q_b"] = f(inputs["mla_w_q_b"][occ]); shared[pre + "w_kv_b"] = f(inputs["mla_w_kv_b"][occ])
            shared[pre + "cos"] = cosv; shared[pre + "sin"] = sinv
        elif k == 2:
            shared[pre + "w_in"] = f(inputs["swa_w_in"][occ]); shared[pre + "w_out"] = f(inputs["swa_w_out"][occ])
            qk = np.zeros((128, 2), np.float32)
            qk[:, 0] = np.tile(np.asarray(inputs["swa_q_norm"][occ], dtype=np.float32), 2)
            qk[:, 1] = np.tile(np.asarray(inputs["swa_k_norm"][occ], dtype=np.float32), 2)
            shared[pre + "qk_cols"] = qk
            sk_ = np.asarray(inputs["swa_sinks"][occ], dtype=np.float32)
            shared[pre + "sinks_pair"] = np.ascontiguousarray(np.repeat(sk_.reshape(16, 2).T, 64, axis=0))
            shared[pre + "rel_bias"] = f(inputs["rel_bias"]); shared[pre + "onehot"] = t5_bucket_onehot()
        else:
            shared[pre + "w_in"] = f(inputs["fox_w_in"][occ]); shared[pre + "w_out"] = f(inputs["fox_w_out"][occ])
            qk = np.zeros((128, 2), np.float32)
            qk[:, 0] = np.tile(np.asarray(inputs["fox_q_norm"][occ], dtype=np.float32), 2)
            qk[:, 1] = np.tile(np.asarray(inputs["fox_k_norm"][occ], dtype=np.float32), 2)
            shared[pre + "qk_cols"] = qk
            shared[pre + "bf_bc"] = bc128(inputs["fox_b_f"][occ])
            shared[pre + "ident"] = np.eye(128, dtype=np.float32); shared[pre + "triu"] = np.triu(np.ones((128, 128), np.float32))
    x = np.asarray(inputs["x"], dtype=np.float32)
    mem = np.asarray(inputs["mem"], dtype=np.float32)
    maps = []
    if not cc:
        for b in range(x.shape[0]):
            m = dict(shared)
            m["xT"] = np.ascontiguousarray(x[b].T)
            m["memT"] = np.ascontiguousarray(mem[b].T)
            maps.append(m)
        return maps
    for c in range(NCORES):
        b, hf = c // 2, c % 2
        m = dict(shared)
        m["xT"] = np.ascontiguousarray(x[b, hf * T:(hf + 1) * T].T)
        m["memT"] = np.ascontiguousarray(mem[b].T)
        pc = np.zeros((128, 3), np.float32)
        pc[:, 0] = -CSHIFT if hf == 1 else NEG
        pc[:, 1] = 0.0 if hf == 1 else NEG
        pc[:, 2] = 1.0 if hf == 1 else 0.0
        m["pcols"] = pc
        for i in range(n_layers):
            if i % 4 == 1:
                m["l%d_cos" % i] = np.ascontiguousarray(cosv[:, hf * T:(hf + 1) * T])
                m["l%d_sin" % i] = np.ascontiguousarray(sinv[:, hf * T:(hf + 1) * T])
        maps.append(m)
    return maps


def kernel(**inputs):
    nc = build_full(cc=True)
    maps = host_inputs(inputs, cc=True)
    res = run_bass_kernel_spmd(nc, maps, core_ids=list(range(NCORES)))
    out = np.zeros((4, SEQ, D), np.float32)
    for c, r in enumerate(res.results):
        out[c // 2, (c % 2) * T:(c % 2 + 1) * T] = r["yT"].T
    return out
```

```python
import contextlib
import math
import numpy as np
import concourse.bass as bass
import concourse.mybir as mybir
from concourse.bass_utils import run_bass_kernel_spmd

F32 = mybir.dt.float32
BF16 = mybir.dt.bfloat16
AF = mybir.ActivationFunctionType
ALU = mybir.AluOpType
AX = mybir.AxisListType

COMPUTE = ("pe", "act", "dve", "pool")

D = 2048
FF = 5632
T = 1024
SEQ = 2048
KC = D // 128
NG = FF // 512
EPS = 1e-6
MEM = 256
NCORES = 8


class Op:
    __slots__ = ("eng", "fn", "deps", "needs_inc", "is_dma", "sem", "count", "idx", "prev_same_sem", "inc")

    def __init__(self, eng, fn, is_dma=False):
        self.eng = eng
        self.fn = fn
        self.deps = []
        self.needs_inc = False
        self.is_dma = is_dma
        self.sem = None
        self.count = None
        self.idx = None
        self.prev_same_sem = None
        self.inc = 16


class Tracer:
    def __init__(self, nc, n_dma_sems=16):
        self.nc = nc
        self.ops = {e: [] for e in ("pe", "act", "dve", "pool", "sp")}
        self.last_writer = {}
        self.readers = {}
        self.n_dma_sems = n_dma_sems
        self.dma_rr = {"sp": 0, "pool": 0, "act": 0}
        self.dma_last = {}
        self.dma_cnt = {}

    def op(self, eng, fn, reads=(), writes=(), is_dma=False, cc=False, raw_dist=3):
        o = Op(eng, fn, is_dma)
        if cc:
            o.inc = 1
        o.idx = len(self.ops[eng])
        deps = []
        for r in reads:
            w = self.last_writer.get(r)
            if w is not None:
                deps.append((w, 0))
        for wkey in writes:
            w = self.last_writer.get(wkey)
            if w is not None:
                deps.append((w, 1))
            for rd in self.readers.get(wkey, ()):
                deps.append((rd, 2))
        seen = set()
        for d, kind in deps:
            if d is o or id(d) in seen:
                continue
            if (not d.is_dma) and d.eng == eng and not is_dma:
                if eng == "pe":
                    continue
                if kind != 0:
                    continue
                if o.idx - d.idx > raw_dist:
                    continue
            seen.add(id(d))
            o.deps.append(d)
            d.needs_inc = True
        if is_dma:
            if cc:
                key = ("cc", 0)
            else:
                slot = self.dma_rr[eng] % self.n_dma_sems
                self.dma_rr[eng] += 1
                key = (eng, slot)
            o.prev_same_sem = self.dma_last.get(key)
            self.dma_last[key] = o
            self.dma_cnt[key] = self.dma_cnt.get(key, 0) + o.inc
            o.sem = key
            o.count = self.dma_cnt[key]
            o.needs_inc = True
        for r in reads:
            self.readers.setdefault(r, []).append(o)
        for wkey in writes:
            self.last_writer[wkey] = o
            self.readers[wkey] = []
        self.ops[eng].append(o)
        return o

    def dma(self, queue, out, in_, reads=(), writes=(), **kw):
        return self.op(queue, lambda e: e.dma_start(out=out, in_=in_, **kw), reads, writes, is_dma=True)

    def allgather(self, src, dst, reads=(), writes=()):
        groups = [[0, 1], [2, 3], [4, 5], [6, 7]]
        return self.op("pool", lambda e: e.collective_compute("AllGather", ALU.bypass, replica_groups=groups,
                                                              ins=[src.opt()], outs=[dst.opt()]),
                       reads, writes, is_dma=True, cc=True)

    def emit(self, final_waits=()):
        nc = self.nc
        with contextlib.ExitStack() as st:
            esem = {e: st.enter_context(nc.semaphore("s_" + e)) for e in COMPUTE}
            dsem = {}
            for key in self.dma_cnt:
                dsem[key] = st.enter_context(nc.semaphore("d_%s_%d" % key))
            for e in COMPUTE:
                c = 0
                for o in self.ops[e]:
                    if o.is_dma:
                        continue
                    if o.needs_inc:
                        c += 1
                        o.count = c
                        o.sem = e
            ops = self.ops
            final_waits = list(final_waits)

            def semof(d):
                return dsem[d.sem] if d.is_dma else esem[d.sem]

            def run(ename, eng):
                known = {}
                for o in ops[ename]:
                    waits = {}
                    dl = list(o.deps)
                    if o.is_dma and o.prev_same_sem is not None:
                        dl.append(o.prev_same_sem)
                    for d in dl:
                        k = d.sem
                        if waits.get(k, (None, 0))[1] < d.count:
                            waits[k] = (semof(d), d.count)
                    for k, (s, v) in waits.items():
                        if known.get(k, 0) >= v:
                            continue
                        known[k] = v
                        eng.wait_ge(s, v)
                    ins = o.fn(eng)
                    if o.needs_inc:
                        if o.is_dma:
                            ins.then_inc(dsem[o.sem], o.inc)
                        else:
                            ins.then_inc(esem[o.sem], 1)
                if ename == "sp":
                    for d in final_waits:
                        if known.get(d.sem, 0) < d.count:
                            known[d.sem] = d.count
                            eng.wait_ge(semof(d), d.count)

            with nc.Block() as block:
                @block.sync
                def _(e):
                    run("sp", e)

                @block.tensor
                def _(e):
                    run("pe", e)

                @block.scalar
                def _(e):
                    run("act", e)

                @block.vector
                def _(e):
                    run("dve", e)

                @block.gpsimd
                def _(e):
                    run("pool", e)


CSHIFT = 12.0
NEG = -30000.0


class Prog:
    def __init__(self, cc=False):
        self.cc = cc
        self.nc = bass.Bass("TRN2", target_bir_lowering=False)
        self.tr = Tracer(self.nc)
        self.st = contextlib.ExitStack()
        self.finals = []
        nc, st = self.nc, self.st
        sb = lambda n, shp, dt: st.enter_context(nc.sbuf_tensor(n, shp, dt))
        self.XT = sb("XT", [128, KC, T], F32)
        self.HT = sb("HT", [128, KC, T], BF16)
        self.AT2 = sb("AT2", [128, 8, T], BF16)
        self.NWS = 12
        self.WR = sb("WR", [128, self.NWS, 4, 512], BF16)
        self.SG = sb("SG", [128, 2, 512], F32)
        self.SCR = sb("SCR", [128, 2056], F32)
        self.MQ = sb("MQ", [128, 4, T], BF16)
        self.LAT = sb("LAT", [128, 4, T], BF16)
        self.PT = sb("PT", [128, 4, 512], BF16)
        self.RS = sb("RS", [128, 2, 512], F32)
        self.T1 = sb("T1", [128, 2, 512], F32)
        self.ONES = sb("ONES", [128, 128], F32)
        self.ONESB = sb("ONESB", [128, 128], BF16)
        self.BD64 = sb("BD64", [128, 128], F32)
        self.RMT = sb("RMT", [64, 64], F32)
        self.CM = sb("CM", [128, 4, 512], BF16)
        self.EPSB = sb("EPSB", [128, 1], F32)
        self.NEGC = sb("NEGC", [128, 1], F32)
        self.GV = sb("GV", [128, 64], F32)
        self.GV2 = sb("GV2", [128, 96], F32)
        self.DUMMY = sb("DUMMY", [128, 4], F32)
        self.ZB = sb("ZB", [128, 64], BF16)
        self.PCOL = sb("PCOL", [128, 4], F32)
        self.GB01 = sb("GB01", [128, 16, 2], F32)
        self.ZHP = sb("ZHP", [128, 16, 2], F32)
        self.DY = sb("DY", [128, 16, 2], F32)
        self.DYT = sb("DYT", [128, 16], F32)
        self.DYB = sb("DYB", [128, 16, 2], BF16)
        self.ZH = sb("ZH", [128, 16, 2], F32)
        self.ps = [st.enter_context(nc.psum_tensor("ps%d" % i, [128, 512], F32)) for i in range(8)]
        self.wslot = 0
        self.psrr = {}
        self.rr = {}
        self.AT = [self.AT2[:, 4 * b:4 * b + 4, :] for b in range(2)]
        f32v = self.AT2[:, :, :].bitcast(F32).rearrange("p a t -> p (a t)")
        self.ACC = f32v[:, 0:1024]
        self.SQ = [f32v[:, 1024:2048], f32v[:, 2048:3072]]
        self.RSTD = f32v[:, 3072:4096]
        self.ACC1 = self.T1[:, :, :].rearrange("p a t -> p (a t)")
        self.AT32 = self.AT2[:, :, :].bitcast(F32)
        tr = self.tr
        tr.op("dve", lambda e: e.memset(self.ONES[:, :], 1.0), writes=["ones"])
        tr.op("dve", lambda e: e.memset(self.ONESB[:, :], 1.0), writes=["onesb"])
        tr.op("dve", lambda e: e.memset(self.EPSB[:, :], EPS), writes=["eps"])
        tr.op("dve", lambda e: e.memset(self.NEGC[:, :], -CSHIFT), writes=["negc"])
        tr.op("dve", lambda e: e.memset(self.ZB[:, :], 0.0), writes=["zb"])

    def rot(self, name, n=2):
        i = self.rr.get(name, 0)
        self.rr[name] = i + 1
        return i % n

    def din(self, name, shape, dt=F32):
        return self.nc.dram_tensor(name, list(shape), dt, kind="ExternalInput").ap()

    def dout(self, name, shape, dt=F32):
        return self.nc.dram_tensor(name, list(shape), dt, kind="ExternalOutput").ap()

    def dscr(self, name, shape, dt=BF16):
        return self.nc.dram_tensor(name, list(shape), dt).ap()

    def atkeys(self):
        return [("cat", c, n) for c in range(8) for n in range(2)]

    def allgather_rows(self, src, dst, rc, reads, wkey):
        R = src.shape[0]
        assert R % rc == 0
        for i in range(R // rc):
            self.tr.allgather(src[i * rc:(i + 1) * rc, :], dst[2 * rc * i:2 * rc * (i + 1), :], reads=reads, writes=[(wkey, i)])

    @staticmethod
    def r0rows(dst, rc, r_lo, n):
        i, loc = r_lo // rc, r_lo % rc
        assert loc + n <= rc
        return dst[2 * rc * i + loc:2 * rc * i + loc + n, :]

    def load_pcols(self, pcols):
        self.tr.dma("sp", self.PCOL[:, 0:3], pcols, writes=["pcol"])

    def load_consts(self, cm, bd64, rmt):
        self.tr.dma("pool", self.CM[:, :, :], cm, writes=["cm"])
        self.tr.dma("sp", self.BD64[:, :], bd64, writes=["bd64"])
        self.tr.dma("sp", self.RMT[:, :], rmt, writes=["rmt"])

    def load_x(self, xT, c0):
        v = xT.rearrange("(c p) t -> p c t", p=128)
        for q in range(4):
            self.tr.dma("sp", self.XT[:, 4 * q:4 * q + 4, :], v[:, 4 * q:4 * q + 4, c0:c0 + T],
                        writes=[("xt", c, n) for c in range(4 * q, 4 * q + 4) for n in range(2)])

    def store_x(self, yT, c0):
        v = yT.rearrange("(c p) t -> p c t", p=128)
        for q in range(4):
            o = self.tr.dma("sp", v[:, 4 * q:4 * q + 4, c0:c0 + T], self.XT[:, 4 * q:4 * q + 4, :],
                            reads=[("xt", c, n) for c in range(4 * q, 4 * q + 4) for n in range(2)],
                            writes=[("yT", q, c0)])
            self.finals.append(o)

    def next_ps(self, lo=0, hi=8):
        k = self.psrr.get((lo, hi), 0)
        self.psrr[(lo, hi)] = k + 1
        return lo + (k % (hi - lo))

    def rmsnorm_x(self, gain):
        tr = self.tr
        XT, HT, ACC, SQ, RSTD, GV = self.XT, self.HT, self.ACC, self.SQ, self.RSTD, self.GV
        scr = self.atkeys()
        tr.dma("sp", GV[:, 0:KC], gain, writes=["gv"])
        for kc in range(KC):
            sq = SQ[kc % 2]
            tr.op("act", lambda e, kc=kc, sq=sq: e.activation(out=sq, in_=XT[:, kc, :], func=AF.Square),
                  reads=[("xt", kc, 0), ("xt", kc, 1)], writes=[("sq", kc % 2)] + (scr if kc == 0 else []))
            acc = ACC if kc % 2 == 0 else self.ACC1
            ak = ["acc"] if kc % 2 == 0 else [("t1", 0), ("t1", 1)]
            if kc < 2:
                tr.op("dve", lambda e, sq=sq, acc=acc: e.tensor_copy(out=acc, in_=sq), reads=[("sq", kc % 2)], writes=ak)
            else:
                tr.op("dve", lambda e, sq=sq, acc=acc: e.tensor_tensor(out=acc, in0=acc, in1=sq, op=ALU.add),
                      reads=[("sq", kc % 2)] + ak, writes=ak, raw_dist=1)
        for n in range(2):
            b = self.next_ps()
            sl = slice(n * 512, (n + 1) * 512)
            tr.op("pe", lambda e, b=b, sl=sl: e.matmul(self.ps[b][:, :], lhsT=self.ONES[:, :], rhs=ACC[:, sl], start=True, stop=False),
                  reads=["acc", "ones"], writes=[("ps", b)])
            tr.op("pe", lambda e, b=b, sl=sl: e.matmul(self.ps[b][:, :], lhsT=self.ONES[:, :], rhs=self.ACC1[:, sl], start=False, stop=True),
                  reads=[("t1", 0), ("t1", 1), "ones"], writes=[("ps", b)])
            tr.op("act", lambda e, b=b, sl=sl: e.activation(out=RSTD[:, sl], in_=self.ps[b][:, :], func=AF.Ln,
                                                             scale=1.0 / D, bias=self.EPSB[:, 0:1]),
                  reads=[("ps", b), "eps"], writes=[("rstd", n)])
            tr.op("act", lambda e, sl=sl: e.activation(out=RSTD[:, sl], in_=RSTD[:, sl], func=AF.Exp, scale=-0.5),
                  reads=[("rstd", n)], writes=[("rstd", n)])
        for kc in range(KC):
            for n in range(2):
                sl = slice(n * 512, (n + 1) * 512)
                tr.op("dve", lambda e, kc=kc, sl=sl: e.scalar_tensor_tensor(
                    out=HT[:, kc, sl], in0=XT[:, kc, sl], scalar=GV[:, kc:kc + 1], in1=RSTD[:, sl],
                    op0=ALU.mult, op1=ALU.mult),
                    reads=[("xt", kc, n), ("rstd", n), "gv"] + scr, writes=[("ht", kc, n)])

    def wload(self, src):
        s = self.wslot % self.NWS
        self.wslot += 1
        kp, kcn, ncol = src.shape[0], src.shape[1], src.shape[2]
        self.tr.dma("pool", self.WR[0:kp, s, 0:kcn, 0:ncol], src, writes=[("w", s)])
        return s

    def ffn(self, gain, wg, wu, wd):
        tr = self.tr
        self.rmsnorm_x(gain)
        wgv = wg.rearrange("(kc p) f -> p kc f", p=128)
        wuv = wu.rearrange("(kc p) f -> p kc f", p=128)
        wdv = wd.rearrange("(fc p) d -> p fc d", p=128)
        XT, HT, WR, SG, ps = self.XT, self.HT, self.WR, self.SG, self.ps

        def gu(g):
            at = self.AT[g % 2]
            gs = [self.wload(wgv[:, 4 * s:4 * s + 4, g * 512:(g + 1) * 512]) for s in range(4)]
            us = [self.wload(wuv[:, 4 * s:4 * s + 4, g * 512:(g + 1) * 512]) for s in range(4)]
            for m in range(4):
                for n in range(2):
                    bg = self.next_ps(0, 4)
                    bu = self.next_ps(0, 4)
                    for (bb, ss) in ((bg, gs), (bu, us)):
                        for kc in range(KC):
                            tr.op("pe", lambda e, bb=bb, s=ss[kc // 4], kc=kc, m=m, n=n: e.matmul(
                                ps[bb][:, :], lhsT=WR[:, s, kc % 4, m * 128:(m + 1) * 128],
                                rhs=HT[:, kc, n * 512:(n + 1) * 512], start=(kc == 0), stop=(kc == KC - 1)),
                                reads=[("w", ss[kc // 4]), ("ht", kc, n)], writes=[("ps", bb)])
                    si = self.rot("sg")
                    tr.op("act", lambda e, bg=bg, si=si: e.activation(out=SG[:, si, :], in_=ps[bg][:, :], func=AF.Silu),
                          reads=[("ps", bg)], writes=[("sg", si)])
                    tr.op("dve", lambda e, bu=bu, si=si, at=at, m=m, n=n: e.tensor_tensor(
                        out=at[:, m, n * 512:(n + 1) * 512], in0=SG[:, si, :], in1=ps[bu][:, :], op=ALU.mult),
                        reads=[("sg", si), ("ps", bu)], writes=[("cat", 4 * (g % 2) + m, n)])

        def dn(g):
            ab = g % 2
            self.outproj_group(wdv[:, 4 * g:4 * g + 4, :], [(self.AT[ab][:, kc, :], ("cat", 4 * ab + kc)) for kc in range(4)], 0.5)

        gu(0)
        for g in range(1, NG):
            gu(g)
            dn(g - 1)
        dn(NG - 1)

    def outproj_group(self, wv, tiles, scale, small=None):
        tr = self.tr
        XT, WR, ps = self.XT, self.WR, self.ps
        nkt = len(tiles)
        kp = wv.shape[0]
        ds = [self.wload(wv[:, :, j * 512:(j + 1) * 512]) for j in range(4)]
        for mo in range(KC):
            for n in range(2 if small is None else 1):
                b = self.next_ps(4, 8)
                c_lo, c_hi = (n * 512, (n + 1) * 512) if small is None else (0, small)
                wd_ = c_hi - c_lo
                for kt in range(nkt):
                    a, key = tiles[kt]
                    rk = key + (n,) if small is None else key
                    tr.op("pe", lambda e, b=b, s=ds[mo // 4], kt=kt, mo=mo, a=a, c_lo=c_lo, c_hi=c_hi, wd_=wd_: e.matmul(
                        ps[b][:, 0:wd_], lhsT=WR[0:kp, s, kt, (mo % 4) * 128:(mo % 4 + 1) * 128],
                        rhs=a[:, c_lo:c_hi], start=(kt == 0), stop=(kt == nkt - 1)),
                        reads=[("w", ds[mo // 4]), rk], writes=[("ps", b)])
                tr.op("dve", lambda e, b=b, mo=mo, c_lo=c_lo, c_hi=c_hi, wd_=wd_: e.scalar_tensor_tensor(
                    out=XT[:, mo, c_lo:c_hi], in0=ps[b][:, 0:wd_], scalar=scale,
                    in1=XT[:, mo, c_lo:c_hi], op0=ALU.mult, op1=ALU.add),
                    reads=[("ps", b), ("xt", mo, n)], writes=[("xt", mo, n)])

    def linear_fm(self, src, nk, wv, groups, epi, kp=128):
        tr = self.tr
        WR, ps = self.WR, self.ps
        pending = []
        for (g0, gw, mcs) in groups:
            slots = [self.wload(wv[:, 4 * s:min(4 * s + 4, nk), g0:g0 + gw]) for s in range((nk + 3) // 4)]
            for (off, w) in mcs:
                for n in range(2):
                    b = self.next_ps(0, 4)
                    for kc in range(nk):
                        a, key = src(kc, n)
                        s = slots[kc // 4]
                        tr.op("pe", lambda e, b=b, s=s, kc=kc, off=off, w=w, a=a: e.matmul(
                            ps[b][0:w, :], lhsT=WR[0:kp, s, kc % 4, off:off + w], rhs=a,
                            start=(kc == 0), stop=(kc == nk - 1)),
                            reads=[("w", s), key], writes=[("ps", b)])
                    pending.append((g0 + off, w, n, b))
                    if len(pending) > (2 if nk <= 4 else 1):
                        epi(*pending.pop(0))
        while pending:
            epi(*pending.pop(0))

    def src_ht(self, kc, n):
        return self.HT[:, kc, n * 512:(n + 1) * 512], ("ht", kc, n)

    def src_lat(self, kc, n):
        return self.LAT[:, kc, n * 512:(n + 1) * 512], ("lat", kc, n)

    def colnorm(self, b, w, dsz, out_op, blk64=False):
        tr = self.tr
        ps = self.ps
        si = self.rot("sg")
        ri = self.rot("rs")
        SG, RS = self.SG, self.RS
        tr.op("act", lambda e: e.activation(out=SG[0:w, si, :], in_=ps[b][0:w, :], func=AF.Square),
              reads=[("ps", b)], writes=[("sg", si)])
        b2 = self.next_ps(4, 8)
        ones = self.BD64 if blk64 else self.ONES
        tr.op("pe", lambda e: e.matmul(ps[b2][0:w, :], lhsT=ones[0:w, 0:w], rhs=SG[0:w, si, :], start=True, stop=True),
              reads=[("sg", si), "ones", "bd64"], writes=[("ps", b2)])
        tr.op("act", lambda e: e.activation(out=RS[0:w, ri, :], in_=ps[b2][0:w, :], func=AF.Ln, scale=1.0 / dsz,
                                            bias=self.EPSB[0:w, 0:1]),
              reads=[("ps", b2), "eps"], writes=[("rs", ri)])
        tr.op("act", lambda e: e.activation(out=RS[0:w, ri, :], in_=RS[0:w, ri, :], func=AF.Exp, scale=-0.5), reads=[("rs", ri)], writes=[("rs", ri)])
        out_op(RS[0:w, ri, :], ("rs", ri))

    def norm_to(self, b, w, dsz, gcol, gkey, out_ap, out_key, blk64=False):
        def fin(rs, rskey):
            self.tr.op("dve", lambda e: e.scalar_tensor_tensor(out=out_ap, in0=self.ps[b][0:w, :], scalar=gcol, in1=rs,
                                                               op0=ALU.mult, op1=ALU.mult),
                       reads=[("ps", b), rskey, gkey], writes=[out_key])
        self.colnorm(b, w, dsz, fin, blk64)

    def rope_to(self, b, gcol, gkey, cs_lo, out_ap, out_key):
        tr = self.tr
        ti = self.rot("t1")
        T1 = self.T1
        cosv = self.SCR[0:64, 0:T][:, cs_lo:cs_lo + 512]
        sinv = self.SCR[0:64, T:2 * T][:, cs_lo:cs_lo + 512]
        self.norm_to(b, 64, 64.0, gcol, gkey, T1[0:64, ti, :], ("t1", ti))
        b3 = self.next_ps(4, 8)
        tr.op("pe", lambda e: e.matmul(self.ps[b3][0:64, :], lhsT=self.RMT[:, :], rhs=T1[0:64, ti, :], start=True, stop=True),
              reads=[("t1", ti), "rmt"], writes=[("ps", b3)])
        si = self.rot("sg")
        tr.op("dve", lambda e: e.tensor_tensor(out=self.SG[0:64, si, :], in0=self.ps[b3][0:64, :], in1=sinv, op=ALU.mult),
              reads=[("ps", b3), "cs"], writes=[("sg", si)])
        tr.op("dve", lambda e: e.tensor_tensor(out=T1[0:64, ti, :], in0=T1[0:64, ti, :], in1=cosv, op=ALU.mult),
              reads=[("t1", ti), "cs"], writes=[("t1", ti)])
        tr.op("dve", lambda e: e.tensor_tensor(out=out_ap, in0=T1[0:64, ti, :], in1=self.SG[0:64, si, :], op=ALU.add),
              reads=[("t1", ti), ("sg", si)], writes=[out_key])

    def mem_kv(self, memT, gmem, wkv, gk):
        tr = self.tr
        HT = self.HT
        h32 = HT[:, :, :].bitcast(F32).rearrange("p a t -> p (a t)")
        MEMF = h32[:, 0:4096].rearrange("p (c m) -> p c m", m=MEM)
        hb = HT[:, :, :].rearrange("p a t -> p (a t)")
        MEMN = hb[:, 8192:12288].rearrange("p (c m) -> p c m", m=MEM)
        MK = hb[:, 12288:13312].rearrange("p (h m) -> p h m", m=MEM)
        MV = hb[:, 13312:14336].rearrange("p (j c) -> p j c", c=512)
        allht = [("ht", kc, n) for kc in range(KC) for n in range(2)]
        tr.dma("sp", MEMF, memT.rearrange("(c p) m -> p c m", p=128), writes=allht)
        tr.dma("sp", self.GV2[:, 0:KC], gmem, writes=["gv2"])
        tr.dma("sp", self.GV2[:, 16:17], gk, writes=["gv2b"])
        b = self.next_ps(4, 8)
        for kc in range(KC):
            si = self.rot("sg")
            tr.op("act", lambda e, kc=kc, si=si: e.activation(out=self.SG[:, si, 0:MEM], in_=MEMF[:, kc, :], func=AF.Square),
                  reads=allht[0:1], writes=[("sg", si)])
            tr.op("pe", lambda e, kc=kc, si=si: e.matmul(self.ps[b][:, 0:MEM], lhsT=self.ONES[:, :], rhs=self.SG[:, si, 0:MEM],
                                                         start=(kc == 0), stop=(kc == KC - 1)),
                  reads=[("sg", si), "ones"], writes=[("ps", b)])
        ri = self.rot("rs")
        RS = self.RS
        tr.op("act", lambda e: e.activation(out=RS[:, ri, 0:MEM], in_=self.ps[b][:, 0:MEM], func=AF.Ln, scale=1.0 / D,
                                            bias=self.EPSB[:, 0:1]), reads=[("ps", b), "eps"], writes=[("rs", ri)])
        tr.op("act", lambda e: e.activation(out=RS[:, ri, 0:MEM], in_=RS[:, ri, 0:MEM], func=AF.Exp, scale=-0.5), reads=[("rs", ri)], writes=[("rs", ri)])
        for kc in range(KC):
            tr.op("dve", lambda e, kc=kc: e.scalar_tensor_tensor(out=MEMN[:, kc, :], in0=MEMF[:, kc, :], scalar=self.GV2[:, kc:kc + 1],
                                                                in1=RS[:, ri, 0:MEM], op0=ALU.mult, op1=ALU.mult),
                  reads=[("rs", ri), "gv2"] + allht[0:1], writes=[("memn", kc)])
        wv = wkv.rearrange("(kc p) f -> p kc f", p=128)
        slots = [self.wload(wv[:, 4 * s:4 * s + 4, 0:512]) for s in range(4)]
        for h in range(4):
            bb = self.next_ps(0, 4)
            for kc in range(KC):
                tr.op("pe", lambda e, kc=kc, h=h, bb=bb, s=slots[kc // 4]: e.matmul(
                    self.ps[bb][:, 0:MEM], lhsT=self.WR[:, s, kc % 4, h * 128:(h + 1) * 128], rhs=MEMN[:, kc, :],
                    start=(kc == 0), stop=(kc == KC - 1)), reads=[("w", slots[kc // 4]), ("memn", kc)], writes=[("ps", bb)])
            si = self.rot("sg")
            tr.op("act", lambda e, bb=bb, si=si: e.activation(out=self.SG[:, si, 0:MEM], in_=self.ps[bb][:, 0:MEM], func=AF.Square),
                  reads=[("ps", bb)], writes=[("sg", si)])
            b2 = self.next_ps(4, 8)
            tr.op("pe", lambda e, b2=b2, si=si: e.matmul(self.ps[b2][:, 0:MEM], lhsT=self.ONES[:, :], rhs=self.SG[:, si, 0:MEM],
                                                         start=True, stop=True), reads=[("sg", si), "ones"], writes=[("ps", b2)])
            r2 = self.rot("rs")
            tr.op("act", lambda e, b2=b2, r2=r2: e.activation(out=RS[:, r2, 0:MEM], in_=self.ps[b2][:, 0:MEM], func=AF.Ln,
                                                              scale=1.0 / 128, bias=self.EPSB[:, 0:1]),
                  reads=[("ps", b2), "eps"], writes=[("rs", r2)])
            tr.op("act", lambda e, r2=r2: e.activation(out=RS[:, r2, 0:MEM], in_=RS[:, r2, 0:MEM], func=AF.Exp, scale=-0.5), reads=[("rs", r2)], writes=[("rs", r2)])
            tr.op("dve", lambda e, bb=bb, r2=r2, h=h: e.scalar_tensor_tensor(out=MK[:, h, :], in0=self.ps[bb][:, 0:MEM],
                                                                            scalar=self.GV2[:, 16:17], in1=RS[:, r2, 0:MEM],
                                                                            op0=ALU.mult, op1=ALU.mult),
                  reads=[("ps", bb), ("rs", r2), "gv2b"], writes=[("mk", h)])
        slots = [self.wload(wv[:, 4 * s:4 * s + 4, 512:1024]) for s in range(4)]
        for j in range(2):
            bb = self.next_ps(0, 4)
            for kc in range(KC):
                tr.op("pe", lambda e, kc=kc, j=j, bb=bb, s=slots[kc // 4]: e.matmul(
                    self.ps[bb][:, :], lhsT=MEMN[:, kc, j * 128:(j + 1) * 128], rhs=self.WR[:, s, kc % 4, :],
                    start=(kc == 0), stop=(kc == KC - 1)), reads=[("w", slots[kc // 4]), ("memn", kc)], writes=[("ps", bb)])
            tr.op("act", lambda e, bb=bb, j=j: e.copy(out=MV[:, j, :], in_=self.ps[bb][:, :]), reads=[("ps", bb)], writes=[("mv", j)])
        return MK, MV

    def mem_attn(self, MK, MV, ab):
        tr = self.tr
        ps, PT = self.ps, self.PT
        scale = 1.0 / math.sqrt(128.0)
        for h in range(4):
            for n in range(2):
                sl = slice(n * 512, (n + 1) * 512)
                bo = self.next_ps(4, 6)
                bl = self.next_ps(6, 8)
                for j in range(2):
                    b = self.next_ps(0, 4)
                    tr.op("pe", lambda e, b=b, h=h, j=j, sl=sl: e.matmul(ps[b][:, :], lhsT=MK[:, h, j * 128:(j + 1) * 128],
                                                                        rhs=self.MQ[:, h, sl], start=True, stop=True),
                          reads=[("mk", h), ("mq", h, n)], writes=[("ps", b)])
                    pi = self.rot("pt", 4)
                    tr.op("act", lambda e, b=b, pi=pi: e.activation(out=PT[:, pi, :], in_=ps[b][:, :], func=AF.Exp, scale=scale,
                                                                    bias=self.NEGC[:, 0:1]),
                          reads=[("ps", b), "negc"], writes=[("pt", pi)])
                    tr.op("pe", lambda e, bo=bo, h=h, j=j, pi=pi: e.matmul(ps[bo][:, :], lhsT=MV[:, j, h * 128:(h + 1) * 128],
                                                                          rhs=PT[:, pi, :], start=(j == 0), stop=(j == 1)),
                          reads=[("mv", j), ("pt", pi)], writes=[("ps", bo)])
                    tr.op("pe", lambda e, bl=bl, j=j, pi=pi: e.matmul(ps[bl][:, :], lhsT=self.ONESB[:, :], rhs=PT[:, pi, :],
                                                                     start=(j == 0), stop=(j == 1)),
                          reads=["onesb", ("pt", pi)], writes=[("ps", bl)])
                self.attn_finish(bo, bl, 128, self.AT[ab][:, h, sl], ("cat", 4 * ab + h, n))

    def attn_finish(self, bo, bl, dv, out_ap, out_key, add_col=None, add_key=None):
        tr = self.tr
        ri = self.rot("rs")
        RS = self.RS
        if add_col is None:
            tr.op("act", lambda e: e.activation(out=RS[0:dv, ri, :], in_=self.ps[bl][0:dv, :], func=AF.Ln), reads=[("ps", bl)], writes=[("rs", ri)])
        else:
            tr.op("act", lambda e: e.activation(out=RS[0:dv, ri, :], in_=self.ps[bl][0:dv, :], func=AF.Ln, bias=add_col),
                  reads=[("ps", bl), add_key], writes=[("rs", ri)])
        tr.op("act", lambda e: e.activation(out=RS[0:dv, ri, :], in_=RS[0:dv, ri, :], func=AF.Exp, scale=-1.0), reads=[("rs", ri)], writes=[("rs", ri)])
        tr.op("dve", lambda e: e.tensor_tensor(out=out_ap, in0=self.ps[bo][0:dv, :], in1=RS[0:dv, ri, :], op=ALU.mult),
              reads=[("ps", bo), ("rs", ri)], writes=[out_key])

    def mq_epi(self, col0):
        def epi(c, w, n, b):
            h = (c - col0) // 128
            self.norm_to(b, 128, 128.0, self.GV2[:, 17:18], "gv2c", self.MQ[:, h, n * 512:(n + 1) * 512], ("mq", h, n))
        return epi

    def mixer_conv(self, hf, L, S=None):
        tr = self.tr
        cc = self.cc
        if cc:
            hf = 0
        w_in = L["w_in"].rearrange("(kc p) f -> p kc f", p=128)
        tr.dma("sp", self.GV2[:, 17:18], L["mem_q_norm"], writes=["gv2c"])
        tr.dma("sp", self.GV2[:, 20:36], L["conv_w0"], writes=["cw"])
        tr.dma("sp", self.GV2[:, 36:52], L["conv_w1"], writes=["cw"])
        tr.dma("sp", self.GV2[:, 52:68], L["conv_w2"], writes=["cw"])
        self.rmsnorm_x(L["norm_mix"])
        self.linear_fm(self.src_ht, KC, w_in, [(6144, 512, [(i * 128, 128) for i in range(4)])], self.mq_epi(6144))
        Z = self.SCR[:, 0:T + 2]
        CV = self.SCR[:, T + 8:2 * T + 8]
        S1 = self.SG[:, :, :].rearrange("p a t -> p (a t)")
        ps, WR, HT = self.ps, self.WR, self.HT
        wout = L["w_out"]
        for G in range(4):
            ab = G % 2
            sl_gc = [self.wload(w_in[:, 4 * s:4 * s + 4, 2048 + 512 * G:2048 + 512 * (G + 1)]) for s in range(4)]
            sl_xt = [self.wload(w_in[:, 4 * s:4 * s + 4, 4096 + 512 * G:4096 + 512 * (G + 1)]) for s in range(4)]
            sl_gb = [self.wload(w_in[:, 4 * s:4 * s + 4, 512 * G:512 * (G + 1)]) for s in range(4)]
            for m in range(4):
                c = 4 * G + m
                w0 = self.GV2[:, 20 + c:21 + c]
                w1 = self.GV2[:, 36 + c:37 + c]
                w2 = self.GV2[:, 52 + c:53 + c]
                banks = {}
                for (nm, ss) in (("xt", sl_xt), ("gc", sl_gc), ("gb", sl_gb)):
                    for n in range(2):
                        b = self.next_ps(0, 8)
                        banks[(nm, n)] = b
                        for kc in range(KC):
                            tr.op("pe", lambda e, b=b, s=ss[kc // 4], kc=kc, m=m, n=n: e.matmul(
                                ps[b][:, :], lhsT=WR[:, s, kc % 4, m * 128:(m + 1) * 128], rhs=HT[:, kc, n * 512:(n + 1) * 512],
                                start=(kc == 0), stop=(kc == KC - 1)), reads=[("w", ss[kc // 4]), ("ht", kc, n)], writes=[("ps", b)])
                        if nm == "xt":
                            tr.op("act", lambda e, b=b, n=n: e.copy(out=S1[:, n * 512:(n + 1) * 512], in_=ps[b][:, :]),
                                  reads=[("ps", b)], writes=[("sg", n)])
                        if nm == "gc":
                            tr.op("dve", lambda e, b=b, n=n: e.tensor_tensor(out=Z[:, 2 + n * 512:2 + (n + 1) * 512], in0=S1[:, n * 512:(n + 1) * 512],
                                                                            in1=ps[b][:, :], op=ALU.mult),
                                  reads=[("ps", b), ("sg", n)], writes=[("z", n)])
                if hf == 0:
                    tr.op("dve", lambda e: e.memset(Z[:, 0:2], 0.0), writes=[("z", 2)])
                else:
                    tr.op("dve", lambda e, c=c: e.tensor_copy(out=Z[:, 0:2], in_=self.ZH[:, c, :]), reads=[("zh", c)], writes=[("z", 2)])
                zk = [("z", 0), ("z", 1), ("z", 2)]
                tr.op("act", lambda e, w2=w2: e.mul(out=CV, in_=Z[:, 2:T + 2], mul=w2), reads=zk + ["cw"], writes=["cv"])
                tr.op("dve", lambda e, w1=w1: e.scalar_tensor_tensor(out=CV, in0=Z[:, 1:T + 1], scalar=w1, in1=CV, op0=ALU.mult, op1=ALU.add),
                      reads=zk + ["cv", "cw"], writes=["cv"])
                tr.op("dve", lambda e, w0=w0: e.scalar_tensor_tensor(out=CV, in0=Z[:, 0:T], scalar=w0, in1=CV, op0=ALU.mult, op1=ALU.add),
                      reads=zk + ["cv", "cw"], writes=["cv"])
                if hf == 0:
                    tr.op("act", lambda e, c=c: e.copy(out=self.ZH[:, c, :], in_=Z[:, T:T + 2]), reads=zk, writes=[("zh", c)])
                if cc:
                    b0 = banks[("gb", 0)]
                    tr.op("act", lambda e, c=c, b0=b0: e.copy(out=self.GB01[:, c, :], in_=ps[b0][:, 0:2]), reads=[("ps", b0)], writes=[("gb01", c)])
                for n in range(2):
                    b = banks[("gb", n)]
                    tr.op("dve", lambda e, b=b, n=n, m=m, ab=ab: e.tensor_tensor(out=self.AT[ab][:, m, n * 512:(n + 1) * 512], in0=CV[:, n * 512:(n + 1) * 512],
                                                                                  in1=ps[b][:, :], op=ALU.mult),
                          reads=[("ps", b), "cv"], writes=[("cat", 4 * ab + m, n)])
            self.outproj_group(wout[512 * G:512 * (G + 1), :].rearrange("(kt p) d -> p kt d", p=128),
                               [(self.AT[ab][:, m, :], ("cat", 4 * ab + m)) for m in range(4)], 1.0)
        if cc:
            zhk = [("zh", c) for c in range(16)]
            tr.dma("sp", S["ZHs"], self.ZH[:, :, :].rearrange("p c t -> p (c t)"), reads=zhk, writes=["zhs"])
            tr.allgather(S["ZHs"], S["ZHr"], reads=["zhs"], writes=["zhr"])
            tr.dma("sp", self.ZHP[:, :, :].rearrange("p c t -> p (c t)"), S["ZHr"][0:128, :], reads=["zhr"], writes=["zhp"])
            W0, W1 = self.GV2[:, 20:36], self.GV2[:, 36:52]
            DY, DYT, GB01, ZHP = self.DY, self.DYT, self.GB01, self.ZHP
            gk = [("gb01", c) for c in range(16)]
            tr.op("dve", lambda e: e.tensor_tensor(out=DY[:, :, 0], in0=W0, in1=ZHP[:, :, 0], op=ALU.mult), reads=["zhp", "cw"], writes=["dy0"])
            tr.op("dve", lambda e: e.tensor_tensor(out=DYT[:, :], in0=W1, in1=ZHP[:, :, 1], op=ALU.mult), reads=["zhp", "cw"], writes=["dyt"])
            tr.op("dve", lambda e: e.tensor_tensor(out=DY[:, :, 0], in0=DY[:, :, 0], in1=DYT[:, :], op=ALU.add), reads=["dy0", "dyt"], writes=["dy0"])
            tr.op("dve", lambda e: e.tensor_tensor(out=DY[:, :, 0], in0=DY[:, :, 0], in1=GB01[:, :, 0], op=ALU.mult), reads=["dy0"] + gk, writes=["dy0"])
            tr.op("dve", lambda e: e.tensor_tensor(out=DY[:, :, 1], in0=W0, in1=ZHP[:, :, 1], op=ALU.mult), reads=["zhp", "cw"], writes=["dy1"])
            tr.op("dve", lambda e: e.tensor_tensor(out=DY[:, :, 1], in0=DY[:, :, 1], in1=GB01[:, :, 1], op=ALU.mult), reads=["dy1"] + gk, writes=["dy1"])
            tr.op("dve", lambda e: e.tensor_scalar(out=self.DYB[:, :, :], in0=DY[:, :, :], scalar1=self.PCOL[:, 2:3], scalar2=None, op0=ALU.mult),
                  reads=["dy0", "dy1", "pcol"], writes=[("dyb",)])
            for G in range(4):
                self.outproj_group(wout[512 * G:512 * (G + 1), :].rearrange("(kt p) d -> p kt d", p=128),
                                   [(self.DYB[:, 4 * G + m, :], ("dyb",)) for m in range(4)], 1.0, small=2)
        MK, MV = self.mem_kv(L["memT"], L["norm_mem"], L["mem_w_kv"], L["mem_k_norm"])
        self.mem_attn(MK, MV, 0)
        self.barrier(self.htkeys() + self.hdkeys() + [("memn", k) for k in range(KC)] + [("mk", h) for h in range(4)] + [("mv", j) for j in range(2)])
        self.outproj_group(wout[2048:2560, :].rearrange("(kt p) d -> p kt d", p=128),
                           [(self.AT[0][:, m, :], ("cat", m)) for m in range(4)], 1.0)

    def lat_norm(self, gcol0, nsz):
        tr = self.tr
        A32 = self.AT32
        for n in range(2):
            b = self.next_ps(4, 8)
            for c in range(4):
                si = self.rot("sg")
                tr.op("act", lambda e, c=c, n=n, si=si: e.activation(out=self.SG[:, si, :], in_=A32[:, 2 * c + n, :], func=AF.Square),
                      reads=[("a32", c, n)], writes=[("sg", si)])
                tr.op("pe", lambda e, c=c, si=si, b=b: e.matmul(self.ps[b][:, :], lhsT=self.ONES[:, :], rhs=self.SG[:, si, :],
                                                                start=(c == 0), stop=(c == 3)),
                      reads=[("sg", si), "ones"], writes=[("ps", b)])
            ri = self.rot("rs")
            RS = self.RS
            tr.op("act", lambda e, b=b, ri=ri: e.activation(out=RS[:, ri, :], in_=self.ps[b][:, :], func=AF.Ln, scale=1.0 / nsz,
                                                            bias=self.EPSB[:, 0:1]), reads=[("ps", b), "eps"], writes=[("rs", ri)])
            tr.op("act", lambda e, ri=ri: e.activation(out=RS[:, ri, :], in_=RS[:, ri, :], func=AF.Exp, scale=-0.5), reads=[("rs", ri)], writes=[("rs", ri)])
            for c in range(4):
                tr.op("dve", lambda e, c=c, n=n, ri=ri: e.scalar_tensor_tensor(
                    out=self.LAT[:, c, n * 512:(n + 1) * 512], in0=A32[:, 2 * c + n, :], scalar=self.GV2[:, gcol0 + c:gcol0 + c + 1],
                    in1=RS[:, ri, :], op0=ALU.mult, op1=ALU.mult),
                    reads=[("a32", c, n), ("rs", ri), "gvm"], writes=[("lat", c, n)])

    def raw_epi(self, col0):
        def epi(c, w, n, b):
            cc = (c - col0) // 128
            self.tr.op("act", lambda e: e.copy(out=self.AT32[:, 2 * cc + n, :], in_=self.ps[b][:, :]),
                       reads=[("ps", b)], writes=[("a32", cc, n)])
        return epi

    def stage_out(self, w, make, dst, dkey):
        pi = self.rot("pt", 4)
        make(self.PT[0:w, pi, :], ("pt", pi))
        self.tr.dma("sp", dst, self.PT[0:w, pi, :], reads=[("pt", pi)], writes=[dkey])

    def run_pipeline(self, tasks, la=3):
        n = len(tasks)
        for i in range(n + la):
            if i < n:
                tasks[i][0]()
                tasks[i][1]()
            if i - la >= 0:
                tasks[i - la][2]()

    def attn_tasks(self, hf, members, dvm, scale, out_ap_fn, out_key_fn, post=None, clamp=False):
        tr = self.tr
        ps, PT = self.ps, self.PT
        tasks = []
        nm = len(members)
        dtot = members[-1]["po"] + dvm
        for n in range(2):
            sl = slice(n * 512, (n + 1) * 512)
            nj = hf * 8 + 4 * (n + 1)
            st = {}
            for mi, mem in enumerate(members):
                qparts, kparts, vt, vkey, bias_fn, pre, po = mem["q"], mem["k"], mem["v"], mem["vkey"], mem["bias_fn"], mem["pre"], mem["po"]
                np_ = len(qparts)
                for j in range(nj):
                    t = {}
                    r = j - (hf * 8 + 4 * n)

                    def s1(t=t, st=st, j=j, n=n, sl=sl, mi=mi, pre=pre, qparts=qparts, kparts=kparts, np_=np_):
                        if j == 0 and n == 0 and pre is not None:
                            pre()
                        if j == 0 and mi == 0:
                            st["bo"] = self.next_ps(4, 6)
                            st["bl"] = self.next_ps(6, 8)
                        b = t["b"] = self.next_ps(0, 4)
                        for i in range(np_):
                            qa, qk = qparts[i]
                            ka, kk = kparts[i]
                            tr.op("pe", lambda e, b=b, i=i, qa=qa, ka=ka: e.matmul(
                                ps[b][:, :], lhsT=ka[:, j * 128:(j + 1) * 128], rhs=qa[:, sl], start=(i == 0), stop=(i == np_ - 1)),
                                reads=[qk, kk], writes=[("ps", b)])

                    def s2(t=t, j=j, r=r, bias_fn=bias_fn):
                        b = t["b"]
                        pi = t["pi"] = self.rot("pt", 4)
                        if bias_fn is None:
                            ba, bk = self.NEGC[:, 0:1], "negc"
                        else:
                            ba, bk = bias_fn(j)
                        if clamp and r >= 0:
                            ti = self.rot("t1")
                            tr.op("dve", lambda e: e.tensor_scalar(out=self.T1[:, ti, :], in0=ps[b][:, :], scalar1=ba, scalar2=60.0,
                                                                   op0=ALU.add, op1=ALU.min),
                                  reads=[("ps", b), bk], writes=[("t1", ti)])
                            tr.op("act", lambda e: e.activation(out=PT[:, pi, :], in_=self.T1[:, ti, :], func=AF.Exp),
                                  reads=[("t1", ti)], writes=[("pt", pi)])
                        else:
                            tr.op("act", lambda e: e.activation(out=PT[:, pi, :], in_=ps[b][:, :], func=AF.Exp, scale=scale, bias=ba),
                                  reads=[("ps", b), bk], writes=[("pt", pi)])
                        if r >= 0:
                            tr.op("dve", lambda e: e.tensor_tensor(out=PT[:, pi, :], in0=PT[:, pi, :], in1=self.CM[:, r, :], op=ALU.mult),
                                  reads=[("pt", pi), "cm"], writes=[("pt", pi)])

                    def s3(t=t, st=st, j=j, n=n, nj=nj, mi=mi, vt=vt, vkey=vkey, po=po):
                        pi, bo, bl = t["pi"], st["bo"], st["bl"]
                        tr.op("pe", lambda e: e.matmul(ps[bo][po:po + dvm, :], lhsT=vt[:, j, :], rhs=PT[:, pi, :], start=(j == 0), stop=(j == nj - 1)),
                              reads=[vkey, ("pt", pi)], writes=[("ps", bo)])
                        tr.op("pe", lambda e: e.matmul(ps[bl][po:po + dvm, :], lhsT=self.ONESB[:, 0:dvm], rhs=PT[:, pi, :], start=(j == 0), stop=(j == nj - 1)),
                              reads=["onesb", ("pt", pi)], writes=[("ps", bl)])
                        if j == nj - 1 and mi == nm - 1:
                            self.attn_finish(bo, bl, dtot, out_ap_fn(n), out_key_fn(n))
                            if n == 1 and post is not None:
                                post()
                    tasks.append((s1, s2, s3))
        return tasks

    def mem_block(self, L, wout):
        MK, MV = self.mem_kv(L["memT"], L["norm_mem"], L["mem_w_kv"], L["mem_k_norm"])
        self.mem_attn(MK, MV, 0)
        self.barrier(self.htkeys() + self.hdkeys() + [("memn", k) for k in range(KC)] + [("mk", h) for h in range(4)] + [("mv", j) for j in range(2)])
        self.outproj_group(wout[2048:2560, :].rearrange("(kt p) d -> p kt d", p=128),
                           [(self.AT[0][:, m, :], ("cat", m)) for m in range(4)], 1.0)

    def mixer_mla(self, hf, L, S):
        tr = self.tr
        ps, WR, HT, GV2 = self.ps, self.WR, self.HT, self.GV2
        cc = self.cc
        c0 = 0 if cc else hf * T
        if cc:
            hf = 1
        w_in = L["w_in"].rearrange("(kc p) f -> p kc f", p=128)
        tr.dma("sp", GV2[:, 17:18], L["mem_q_norm"], writes=["gv2c"])
        tr.dma("sp", GV2[:, 20:24], L["q_a_norm"], writes=["gvm"])
        tr.dma("sp", GV2[:, 24:28], L["kv_a_norm"], writes=["gvm"])
        tr.dma("sp", GV2[:, 28:32], L["qk_cols"], writes=["gvm"])
        tr.dma("sp", self.SCR[0:64, 0:T], L["cos"][:, c0:c0 + T], writes=["cs"])
        tr.dma("sp", self.SCR[0:64, T:2 * T], L["sin"][:, c0:c0 + T], writes=["cs"])
        self.rmsnorm_x(L["norm_mix"])
        self.linear_fm(self.src_ht, KC, w_in, [(1088, 512, [(i * 128, 128) for i in range(4)])], self.mq_epi(1088))
        def krope_epi(c, w, n, b):
            self.stage_out(64, lambda o, k: self.rope_to(b, GV2[0:64, 31:32], "gvm", n * 512, o, k),
                           S["KR"][:, c0 + n * 512:c0 + (n + 1) * 512], ("krd", hf, n))
        self.linear_fm(self.src_ht, KC, w_in, [(1024, 64, [(0, 64)])], krope_epi)
        a32keys = [("a32", c, n) for c in range(4) for n in range(2)]
        self.barrier(self.atkeys() + a32keys)
        self.linear_fm(self.src_ht, KC, w_in, [(0, 512, [(i * 128, 128) for i in range(4)])], self.raw_epi(0))
        self.lat_norm(20, 512.0)
        wq = L["w_q_b"].rearrange("(kc p) f -> p kc f", p=128)
        def q_epi(c, w, n, b):
            h = c // 192
            if c % 192 == 0:
                self.stage_out(128, lambda o, k: self.norm_to(b, 128, 128.0, GV2[:, 28:29], "gvm", o, k),
                               S["QN"][h, :, n * 512:(n + 1) * 512], ("qnd", h, n))
            else:
                self.stage_out(64, lambda o, k: self.rope_to(b, GV2[0:64, 29:30], "gvm", n * 512, o, k),
                               S["QR"][h, :, n * 512:(n + 1) * 512], ("qrd", h, n))
        self.linear_fm(self.src_lat, 4, wq, [(384 * g, 384, [(0, 128), (128, 64), (192, 128), (320, 64)]) for g in range(8)], q_epi)
        self.linear_fm(self.src_ht, KC, w_in, [(512, 512, [(i * 128, 128) for i in range(4)])], self.raw_epi(512))
        self.lat_norm(24, 512.0)
        self.barrier(self.atkeys() + a32keys)
        wkv = L["w_kv_b"].rearrange("(kc p) f -> p kc f", p=128)
        pend_e = []

        def flush_e():
            b_, h_, n_ = pend_e.pop(0)
            self.stage_out(128, lambda o, k: self.norm_to(b_, 128, 128.0, GV2[:, 30:31], "gvm", o, k),
                           S["KN"][h_, :, c0 + n_ * 512:c0 + (n_ + 1) * 512], ("knd", h_, hf, n_))
        for g in range(8):
            s = self.wload(wkv[:, 0:4, 512 * g:512 * (g + 1)])
            for hh in range(2):
                h = 2 * g + hh
                for n in range(2):
                    b = self.next_ps(0, 4)
                    for kc in range(4):
                        tr.op("pe", lambda e, b=b, kc=kc, hh=hh, n=n, s=s: e.matmul(
                            ps[b][:, :], lhsT=WR[:, s, kc, hh * 256:hh * 256 + 128], rhs=self.LAT[:, kc, n * 512:(n + 1) * 512],
                            start=(kc == 0), stop=(kc == 3)), reads=[("w", s), ("lat", kc, n)], writes=[("ps", b)])
                    pend_e.append((b, h, n))
                    if len(pend_e) > 2:
                        flush_e()
            for tt in range(8):
                b = self.next_ps(4, 8)
                for kc in range(4):
                    tr.op("pe", lambda e, b=b, kc=kc, tt=tt, s=s: e.matmul(
                        ps[b][:, :], lhsT=self.LAT[:, kc, tt * 128:(tt + 1) * 128], rhs=WR[:, s, kc, :],
                        start=(kc == 0), stop=(kc == 3)), reads=[("w", s), ("lat", kc, tt // 4)], writes=[("ps", b)])
                pi = self.rot("pt", 4)
                tr.op("act", lambda e, b=b, pi=pi: e.copy(
                    out=self.PT[:, pi, 0:256].rearrange("p (h d) -> p h d", d=128),
                    in_=ps[b][:, :].rearrange("p (h two d) -> p h two d", two=2, d=128)[:, :, 1, :]),
                    reads=[("ps", b)], writes=[("pt", pi)])
                tr.dma("sp", S["V"][c0 + tt * 128:c0 + (tt + 1) * 128, 256 * g:256 * (g + 1)], self.PT[:, pi, 0:256],
                       reads=[("pt", pi)], writes=[("vd", g, hf, tt)])
        while pend_e:
            flush_e()
        wout = L["w_out"]
        if cc:
            self.allgather_rows(S["KNo"], S["KNr"], 1024, [("knd", h2, 1, n2) for h2 in range(16) for n2 in range(2)], "knr")
            tr.allgather(S["KR"], S["KRr"], reads=[("krd", 1, n2) for n2 in range(2)], writes=["krr"])
            self.allgather_rows(S["V"], S["Vr"], 512, [("vd", g2, 1, tt) for g2 in range(8) for tt in range(8)], "vr")
        self.mem_block(L, wout)
        hb = HT[:, :, :].rearrange("p a t -> p (a t)")
        Sk = (hf + 1) * T
        KRt = hb[0:64, 12288:12288 + SEQ]
        if cc:
            tr.dma("sp", KRt[:, 0:T], S["KRr"][0:64, :], reads=["krr"], writes=[("hd", "kr")])
            tr.dma("sp", KRt[:, T:2 * T], S["KR"], reads=[("krd", 1, n2) for n2 in range(2)], writes=[("hd", "kr")])
        else:
            krk = [("krd", h2, n2) for h2 in range(hf + 1) for n2 in range(2)]
            tr.dma("sp", KRt[:, 0:Sk], S["KR"][:, 0:Sk], reads=krk, writes=[("hd", "kr")])
        scale = 1.0 / math.sqrt(192.0)
        tasks = []
        for h in range(16):
            st_ = h % 2
            base = st_ * 6144
            QNt = hb[:, base:base + T]
            QRt = hb[0:64, base + T:base + 2 * T]
            KNt = hb[:, base + 2 * T:base + 2 * T + SEQ]
            Vt = hb[:, base + 4 * T:base + 4 * T + SEQ].rearrange("p (j c) -> p j c", c=128)
            ab = (h // 4) % 2
            m = h % 4

            def pre(h=h, st_=st_, QNt=QNt, QRt=QRt, KNt=KNt, Vt=Vt):
                tr.dma("sp", QNt, S["QN"][h], reads=[("qnd", h, n2) for n2 in range(2)], writes=[("hd", st_, "qn")])
                tr.dma("sp", QRt, S["QR"][h], reads=[("qrd", h, n2) for n2 in range(2)], writes=[("hd", st_, "qr")])
                if cc:
                    tr.dma("sp", KNt[:, 0:T], self.r0rows(S["KNr"], 1024, 128 * h, 128), reads=[("knr", h // 8)], writes=[("hd", st_, "kn")])
                    tr.dma("sp", KNt[:, T:2 * T], S["KN"][h], reads=[("knd", h, 1, n2) for n2 in range(2)], writes=[("hd", st_, "kn")])
                    for i2 in range(2):
                        tr.dma("sp", Vt[:, 4 * i2:4 * i2 + 4, :],
                               self.r0rows(S["Vr"], 512, 512 * i2, 512)[:, 128 * h:128 * (h + 1)].rearrange("(j p) c -> p j c", p=128),
                               reads=[("vr", i2)], writes=[("hd", st_, "v")])
                    tr.dma("sp", Vt[:, 8:16, :], S["V"][:, 128 * h:128 * (h + 1)].rearrange("(j p) c -> p j c", p=128),
                           reads=[("vd", h // 2, 1, tt) for tt in range(8)], writes=[("hd", st_, "v")])
                    return
                tr.dma("sp", KNt[:, 0:Sk], S["KN"][h, :, 0:Sk], reads=[("knd", h, h2, n2) for h2 in range(hf + 1) for n2 in range(2)],
                       writes=[("hd", st_, "kn")])
                tr.dma("sp", Vt[:, 0:Sk // 128, :], S["V"][0:Sk, 128 * h:128 * (h + 1)].rearrange("(j p) c -> p j c", p=128),
                       reads=[("vd", h // 2, h2, tt) for h2 in range(hf + 1) for tt in range(8)], writes=[("hd", st_, "v")])

            post = None
            if m == 3:
                def post(h=h, ab=ab):
                    g4 = h // 4
                    self.outproj_group(wout[512 * g4:512 * (g4 + 1), :].rearrange("(kt p) d -> p kt d", p=128),
                                       [(self.AT[ab][:, mm, :], ("cat", 4 * ab + mm)) for mm in range(4)], 1.0)
            mem_ = dict(q=[(QNt, ("hd", st_, "qn")), (QRt, ("hd", st_, "qr"))], k=[(KNt, ("hd", st_, "kn")), (KRt, ("hd", "kr"))],
                        v=Vt, vkey=("hd", st_, "v"), pre=pre, po=0,
                        bias_fn=(lambda j: (self.PCOL[:, 0:1], "pcol") if j < 8 else (self.NEGC[:, 0:1], "negc")) if cc else None)
            tasks += self.attn_tasks(hf, [mem_], 128, scale,
                                     lambda n, ab=ab, m=m: self.AT[ab][:, m, n * 512:(n + 1) * 512],
                                     lambda n, ab=ab, m=m: ("cat", 4 * ab + m, n), post=post)
        self.run_pipeline(tasks)
        self.barrier(self.htkeys() + self.hdkeys())

    def mixer_swa(self, hf, L, S):
        tr = self.tr
        ps, WR, HT, GV2, PT = self.ps, self.WR, self.HT, self.GV2, self.PT
        cc = self.cc
        c0 = 0 if cc else hf * T
        if cc:
            hf = 1
        W = 1152
        w_in = L["w_in"].rearrange("(kc p) f -> p kc f", p=128)
        tr.dma("sp", GV2[:, 17:18], L["mem_q_norm"], writes=["gv2c"])
        tr.dma("sp", GV2[:, 28:30], L["qk_cols"], writes=["gvm"])
        SK = GV2[:, 32:48]
        tr.dma("sp", SK, L["sinks_pair"], writes=["sk"])
        tr.op("act", lambda e: e.activation(out=SK, in_=SK, func=AF.Exp, bias=self.NEGC[:, 0:1]), reads=["sk", "negc"], writes=["sk"])
        if hf == 0 or cc:
            GROW = self.SCR[0:32, 0:W]
            tr.op("dve", lambda e: e.memset(GROW, NEG), writes=["grow"])
            tr.dma("sp", self.T1[0:32, 0, 0:32], L["rel_bias"], writes=[("t1", 0)])
            tr.dma("sp", self.T1[0:32, 1, 0:128], L["onehot"], writes=[("t1", 1)])
            b = self.next_ps(0, 4)
            tr.op("pe", lambda e: e.matmul(ps[b][0:32, 0:128], lhsT=self.T1[0:32, 0, 0:32], rhs=self.T1[0:32, 1, 0:128], start=True, stop=True),
                  reads=[("t1", 0), ("t1", 1)], writes=[("ps", b)])
            tr.op("act", lambda e: e.copy(out=GROW[:, 511:639], in_=ps[b][0:32, 0:128]), reads=[("ps", b), "grow"], writes=["grow"])
            for q4 in range(8):
                tr.dma("sp", S["GD"][:, 16 * q4:16 * (q4 + 1), :], GROW.unsqueeze(1).broadcast_to([32, 16, W]), reads=["grow"], writes=[("gd", q4)])
        self.rmsnorm_x(L["norm_mix"])
        self.linear_fm(self.src_ht, KC, w_in, [(2560, 512, [(i * 128, 128) for i in range(4)])], self.mq_epi(2560))
        def q_epi(c, w, n, b):
            cc = c // 128
            pi = self.rot("pt", 4)
            self.norm_to(b, 128, 64.0, GV2[:, 28:29], "gvm", PT[:, pi, :], ("pt", pi), blk64=True)
            for hh in range(2):
                tr.dma("sp", S["Q"][2 * cc + hh, :, n * 512:(n + 1) * 512], PT[64 * hh:64 * hh + 64, pi, :], reads=[("pt", pi)],
                       writes=[("qd", 2 * cc + hh, n)])
        self.linear_fm(self.src_ht, KC, w_in, [(512 * g, 512, [(i * 128, 128) for i in range(4)]) for g in range(4)], q_epi)
        def k_epi(c, w, n, b):
            cc = (c - 2048) // 128
            pi = self.rot("pt", 4)
            self.norm_to(b, 128, 64.0, GV2[:, 29:30], "gvm", PT[:, pi, :], ("pt", pi), blk64=True)
            for hh in range(2):
                tr.dma("sp", S["K"][2 * cc + hh, :, c0 + n * 512:c0 + (n + 1) * 512], PT[64 * hh:64 * hh + 64, pi, :], reads=[("pt", pi)],
                       writes=[("kd", 2 * cc + hh, hf, n)])
                if self.cc and n == 1:
                    tr.dma("sp", S["KHs"][64 * (2 * cc + hh):64 * (2 * cc + hh + 1), :], PT[64 * hh:64 * hh + 64, pi, 384:512], reads=[("pt", pi)],
                           writes=[("khs", 2 * cc + hh)])
        self.linear_fm(self.src_ht, KC, w_in, [(2048, 256, [(0, 128), (128, 128)])], k_epi)
        slots = [self.wload(w_in[:, 4 * s:4 * s + 4, 2304:2560]) for s in range(4)]
        for tt in range(8):
            b = self.next_ps(0, 4)
            for kc in range(KC):
                tr.op("pe", lambda e, b=b, kc=kc, tt=tt, s=slots[kc // 4]: e.matmul(
                    ps[b][:, 0:256], lhsT=HT[:, kc, tt * 128:(tt + 1) * 128], rhs=WR[:, s, kc % 4, 0:256],
                    start=(kc == 0), stop=(kc == KC - 1)), reads=[("w", slots[kc // 4]), ("ht", kc, tt // 4)], writes=[("ps", b)])
            pi = self.rot("pt", 4)
            tr.op("act", lambda e, b=b, pi=pi: e.copy(out=PT[:, pi, 0:256], in_=ps[b][:, 0:256]), reads=[("ps", b)], writes=[("pt", pi)])
            tr.dma("sp", S["V"][c0 + tt * 128:c0 + (tt + 1) * 128, :], PT[:, pi, 0:256], reads=[("pt", pi)], writes=[("vd", hf, tt)])
            if cc and tt == 7:
                tr.dma("sp", S["VHs"], PT[:, pi, 0:256], reads=[("pt", pi)], writes=["vhs"])
        if cc:
            tr.allgather(S["KHs"], S["KHr"], reads=[("khs", g2) for g2 in range(4)], writes=["khr"])
            tr.allgather(S["VHs"], S["VHr"], reads=["vhs"], writes=["vhr"])
        wout = L["w_out"]
        self.mem_block(L, wout)
        hb = HT[:, :, :].rearrange("p a t -> p (a t)")
        h32 = HT[:, :, :].bitcast(F32).rearrange("p a t -> p (a t)")
        Sk = (hf + 1) * T
        scale = 1.0 / 8.0
        jbase = hf * 8
        tasks = []
        for pr in range(16):
            g = (2 * pr) // 8
            pst = pr % 2
            gs_ = g % 2
            Kt = hb[0:64, 4096 + 2048 * gs_:4096 + 2048 * (gs_ + 1)]
            Vt = hb[:, 8192 + 1024 * gs_:8192 + 1024 * (gs_ + 1)].rearrange("p (j c) -> p j c", c=64)
            ab = (pr // 4) % 2
            m = pr % 4
            pres, Qts, BBs, sks = [], [], [], []
            for hh in range(2):
                h = 2 * pr + hh
                sk = 2 * pst + hh
                Qt = hb[0:64, 1024 * sk:1024 * (sk + 1)]
                BB = h32[:, 5120 + 256 * sk:5120 + 256 * (sk + 1)]

                def pre(h=h, g=g, sk=sk, gs_=gs_, Qt=Qt, Kt=Kt, Vt=Vt, BB=BB):
                    tr.dma("sp", Qt, S["Q"][h], reads=[("qd", h, n2) for n2 in range(2)], writes=[("hd", sk, "q")])
                    if h % 8 == 0 and cc:
                        tr.dma("sp", Kt[:, 0:128], S["KHr"][64 * g:64 * (g + 1), :], reads=["khr"], writes=[("hd", gs_, "k")])
                        tr.dma("sp", Kt[:, 128:128 + T], S["K"][g], reads=[("kd", g, 1, n2) for n2 in range(2)], writes=[("hd", gs_, "k")])
                        tr.dma("sp", Vt[:, 0, :], S["VHr"][0:128, 64 * g:64 * (g + 1)], reads=["vhr"], writes=[("hd", gs_, "v")])
                        tr.dma("sp", Vt[:, 1:9, :], S["V"][:, 64 * g:64 * (g + 1)].rearrange("(j p) c -> p j c", p=128),
                               reads=[("vd", 1, tt) for tt in range(8)], writes=[("hd", gs_, "v")])
                    elif h % 8 == 0:
                        tr.dma("sp", Kt[:, 0:Sk], S["K"][g, :, 0:Sk], reads=[("kd", g, h2, n2) for h2 in range(hf + 1) for n2 in range(2)],
                               writes=[("hd", gs_, "k")])
                        tr.dma("sp", Vt[:, 0:Sk // 128, :], S["V"][0:Sk, 64 * g:64 * (g + 1)].rearrange("(j p) c -> p j c", p=128),
                               reads=[("vd", h2, tt) for h2 in range(hf + 1) for tt in range(8)], writes=[("hd", gs_, "v")])
                    src = bass.AP(S["GDh"], h * 128 * W + 511, [[W - 1, 128], [1, 256]])
                    tr.dma("sp", BB, src, reads=[("gd", q4) for q4 in range(8)], writes=[("hd", "b", sk)])
                pres.append(pre); Qts.append(Qt); BBs.append(BB); sks.append(sk)

            for n in range(2):
                js = [((jbase + 4 * n + r) - (7 if cc else 0), r) for r in range(-1, 4) if jbase + 4 * n + r >= 0]
                has_prev = js[0][1] == -1
                st = {}
                for hh in range(2):
                    for idx, (j, r) in enumerate(js):
                        lo = max(0, 128 * r)
                        hi = min(512, 128 * r + 256)
                        w = hi - lo
                        boff = lo - 128 * r
                        t = {}
                        last = (idx == len(js) - 1) and hh == 1

                        def s1(t=t, st=st, idx=idx, j=j, n=n, lo=lo, hi=hi, w=w, pre=pres[hh], gs_=gs_, sk=sks[hh], Kt=Kt, Qt=Qts[hh], hh=hh):
                            if idx == 0 and n == 0:
                                pre()
                            if idx == 0 and hh == 0:
                                st["bo"] = self.next_ps(4, 6)
                                st["bl"] = self.next_ps(6, 8)
                            b = t["b"] = self.next_ps(0, 4)
                            tr.op("pe", lambda e: e.matmul(ps[b][:, 0:w], lhsT=Kt[:, j * 128:(j + 1) * 128], rhs=Qt[:, n * 512 + lo:n * 512 + hi],
                                                           start=True, stop=True),
                                  reads=[("hd", gs_, "k"), ("hd", sk, "q")], writes=[("ps", b)])

                        def s2(t=t, w=w, boff=boff, BB=BBs[hh], sk=sks[hh], jj=j):
                            b = t["b"]
                            ti = self.rot("t1")
                            tr.op("dve", lambda e: e.scalar_tensor_tensor(out=self.T1[:, ti, 0:w], in0=ps[b][:, 0:w], scalar=scale,
                                                                          in1=BB[:, boff:boff + w], op0=ALU.mult, op1=ALU.add),
                                  reads=[("ps", b), ("hd", "b", sk)], writes=[("t1", ti)])
                            pi = t["pi"] = self.rot("pt", 4)
                            ebias = self.PCOL[:, 0:1] if (cc and jj == 0) else self.NEGC[:, 0:1]
                            tr.op("act", lambda e: e.activation(out=PT[:, pi, 0:w], in_=self.T1[:, ti, 0:w], func=AF.Exp, bias=ebias),
                                  reads=[("t1", ti), "negc", "pcol"], writes=[("pt", pi)])

                        def s3(t=t, st=st, j=j, n=n, lo=lo, hi=hi, last=last, gs_=gs_, Vt=Vt, pr=pr, ab=ab, m=m, r=r, has_prev=has_prev, po=64 * hh):
                            pi, bo, bl = t["pi"], st["bo"], st["bl"]
                            for c in range(lo // 128, hi // 128):
                                first = (c == r + 1) or (c == 0 and not has_prev)
                                stp = (c == r)
                                pof = 128 * c - lo
                                tr.op("pe", lambda e, c=c, first=first, stp=stp, pof=pof: e.matmul(
                                    ps[bo][po:po + 64, 128 * c:128 * (c + 1)], lhsT=Vt[:, j, :], rhs=PT[:, pi, pof:pof + 128], start=first, stop=stp),
                                    reads=[("hd", gs_, "v"), ("pt", pi)], writes=[("ps", bo)])
                                tr.op("pe", lambda e, c=c, first=first, stp=stp, pof=pof: e.matmul(
                                    ps[bl][po:po + 64, 128 * c:128 * (c + 1)], lhsT=self.ONESB[:, 0:64], rhs=PT[:, pi, pof:pof + 128], start=first, stop=stp),
                                    reads=["onesb", ("pt", pi)], writes=[("ps", bl)])
                            if last:
                                sl = slice(n * 512, (n + 1) * 512)
                                self.attn_finish(bo, bl, 128, self.AT[ab][:, m, sl], ("cat", 4 * ab + m, n), add_col=SK[:, pr:pr + 1], add_key="sk")
                                if n == 1 and m == 3:
                                    g4 = pr // 4
                                    self.outproj_group(wout[512 * g4:512 * (g4 + 1), :].rearrange("(kt p) d -> p kt d", p=128),
                                                       [(self.AT[ab][:, mm, :], ("cat", 4 * ab + mm)) for mm in range(4)], 1.0)
                        tasks.append((s1, s2, s3))
        self.run_pipeline(tasks)
        self.barrier(self.htkeys() + self.hdkeys())

    def mixer_fox(self, hf, L, S):
        tr = self.tr
        ps, WR, HT, GV2, PT, T1 = self.ps, self.WR, self.HT, self.GV2, self.PT, self.T1
        cc = self.cc
        c0 = 0 if cc else hf * T
        if cc:
            hf = 1
        w_in = L["w_in"].rearrange("(kc p) f -> p kc f", p=128)
        tr.dma("sp", GV2[:, 17:18], L["mem_q_norm"], writes=["gv2c"])
        tr.dma("sp", GV2[:, 28:30], L["qk_cols"], writes=["gvm"])
        tr.op("dve", lambda e: e.tensor_scalar(out=GV2[:, 28:29], in0=GV2[:, 28:29], scalar1=1.0 / 8.0, scalar2=None, op0=ALU.mult),
              reads=["gvm"], writes=["gvm"])
        BF = GV2[:, 32:64]
        tr.dma("sp", BF, L["bf_bc"], writes=["bf"])
        IDN = self.SCR[:, 0:128]
        TRIU = self.SCR[:, 128:256]
        NLF = self.SCR[:, 256:512].rearrange("p (j c) -> p j c", c=32)
        CSN = self.SCR[:, 512:768].rearrange("p (j c) -> p j c", c=32)
        tr.dma("sp", IDN, L["ident"], writes=["idn"])
        tr.dma("sp", TRIU, L["triu"], writes=["triu"])
        self.rmsnorm_x(L["norm_mix"])
        self.linear_fm(self.src_ht, KC, w_in, [(6176, 512, [(i * 128, 128) for i in range(4)])], self.mq_epi(6176))
        slots = [self.wload(w_in[:, 4 * s:4 * s + 4, 6144:6176]) for s in range(4)]
        for tt in range(8):
            b = self.next_ps(0, 4)
            for kc in range(KC):
                tr.op("pe", lambda e, b=b, kc=kc, tt=tt, s=slots[kc // 4]: e.matmul(
                    ps[b][:, 0:32], lhsT=HT[:, kc, tt * 128:(tt + 1) * 128], rhs=WR[:, s, kc % 4, 0:32],
                    start=(kc == 0), stop=(kc == KC - 1)), reads=[("w", slots[kc // 4]), ("ht", kc, tt // 4)], writes=[("ps", b)])
            tr.op("dve", lambda e, b=b, tt=tt: e.tensor_tensor(out=NLF[:, tt, :], in0=ps[b][:, 0:32], in1=BF, op=ALU.add),
                  reads=[("ps", b), "bf"], writes=[("nlf", tt)])
            tr.op("act", lambda e, tt=tt: e.activation(out=NLF[:, tt, :], in_=NLF[:, tt, :], func=AF.Exp, scale=-1.0), reads=[("nlf", tt)], writes=[("nlf", tt)])
            tr.op("act", lambda e, tt=tt: e.activation(out=NLF[:, tt, :], in_=NLF[:, tt, :], func=AF.Ln, bias=self.ONES[:, 0:1]),
                  reads=[("nlf", tt), "ones"], writes=[("nlf", tt)])
        for tt in range(8):
            b = self.next_ps(0, 4)
            for ts in range(tt + 1):
                tri = TRIU if ts == tt else self.ONES[:, :]
                tr.op("pe", lambda e, b=b, ts=ts, tt=tt, tri=tri: e.matmul(ps[b][:, 0:32], lhsT=tri, rhs=NLF[:, ts, :], start=(ts == 0), stop=(ts == tt)),
                      reads=[("nlf", ts), "triu", "ones"], writes=[("ps", b)])
            tr.op("act", lambda e, b=b, tt=tt: e.copy(out=CSN[:, tt, :], in_=ps[b][:, 0:32]), reads=[("ps", b)], writes=[("csn", tt)])
            tr.dma("sp", S["CS"][c0 + tt * 128:c0 + (tt + 1) * 128, :], CSN[:, tt, :], reads=[("csn", tt)], writes=[("csd", hf, tt)])
        CQ = T1[0:32, :, :].rearrange("p a t -> p (a t)")
        for n in range(2):
            b = self.next_ps(0, 4)
            for t4 in range(4):
                tt = 4 * n + t4
                tr.op("pe", lambda e, b=b, tt=tt, t4=t4: e.matmul(ps[b][0:32, t4 * 128:(t4 + 1) * 128], lhsT=CSN[:, tt, :], rhs=IDN, start=True, stop=True),
                      reads=[("csn", tt), "idn"], writes=[("ps", b)])
            tr.op("act", lambda e, b=b, n=n: e.mul(out=CQ[:, n * 512:(n + 1) * 512], in_=ps[b][0:32, :], mul=-1.0), reads=[("ps", b)], writes=[("t1", n)])
        cqk = [("t1", 0), ("t1", 1)]
        SPL = self.LAT[0:32, 0:3, :]
        R1 = self.RS[0:32, :, :].rearrange("p a t -> p (a t)")
        tr.op("dve", lambda e: e.tensor_copy(out=SPL[:, 0, :], in_=CQ), reads=cqk, writes=[("lat", 0, 0)])
        tr.op("dve", lambda e: e.tensor_tensor(out=R1, in0=CQ, in1=SPL[:, 0, :], op=ALU.subtract), reads=cqk + [("lat", 0, 0)], writes=[("rs", 0), ("rs", 1)])
        tr.op("dve", lambda e: e.tensor_copy(out=SPL[:, 1, :], in_=R1), reads=[("rs", 0)], writes=[("lat", 1, 0)])
        tr.op("dve", lambda e: e.tensor_tensor(out=R1, in0=R1, in1=SPL[:, 1, :], op=ALU.subtract), reads=[("rs", 0), ("lat", 1, 0)], writes=[("rs", 0), ("rs", 1)])
        tr.op("dve", lambda e: e.tensor_copy(out=SPL[:, 2, :], in_=R1), reads=[("rs", 0)], writes=[("lat", 2, 0)])
        for i in range(3):
            tr.dma("sp", S["Q"][:, 64 + i, :], SPL[:, i, :], reads=[("lat", i, 0)], writes=[("qaug", i)])
        pi = self.rot("pt", 4)
        tr.op("dve", lambda e: e.memset(PT[0:32, pi, :], 1.0), writes=[("pt", pi)])
        for i in range(3):
            for n in range(2):
                tr.dma("sp", S["K"][:, 64 + i, c0 + n * 512:c0 + (n + 1) * 512], PT[0:32, pi, :], reads=[("pt", pi)], writes=[("kaug", hf, i, n)])
        def mk_epi(dst, col0, gc, keyname, cbase):
            def epi(c, w, n, b):
                cc = (c - col0) // 128
                pi = self.rot("pt", 4)
                self.norm_to(b, 128, 64.0, GV2[:, gc:gc + 1], "gvm", PT[:, pi, :], ("pt", pi), blk64=True)
                for hh in range(2):
                    tr.dma("sp", dst[2 * cc + hh, 0:64, cbase + n * 512:cbase + (n + 1) * 512], PT[64 * hh:64 * hh + 64, pi, :],
                           reads=[("pt", pi)], writes=[(keyname, 2 * cc + hh, hf if keyname == "kd" else 0, n)])
            return epi
        self.linear_fm(self.src_ht, KC, w_in, [(512 * g, 512, [(i * 128, 128) for i in range(4)]) for g in range(4)],
                       mk_epi(S["Q"], 0, 28, "qd", 0))
        self.linear_fm(self.src_ht, KC, w_in, [(2048 + 512 * g, 512, [(i * 128, 128) for i in range(4)]) for g in range(4)],
                       mk_epi(S["K"], 2048, 29, "kd", c0))
        for g in range(4):
            slots = [self.wload(w_in[:, 4 * s:4 * s + 4, 4096 + 512 * g:4096 + 512 * (g + 1)]) for s in range(4)]
            for tt in range(8):
                b = self.next_ps(0, 4)
                for kc in range(KC):
                    tr.op("pe", lambda e, b=b, kc=kc, tt=tt, s=slots[kc // 4]: e.matmul(
                        ps[b][:, :], lhsT=HT[:, kc, tt * 128:(tt + 1) * 128], rhs=WR[:, s, kc % 4, :],
                        start=(kc == 0), stop=(kc == KC - 1)), reads=[("w", slots[kc // 4]), ("ht", kc, tt // 4)], writes=[("ps", b)])
                pi = self.rot("pt", 4)
                tr.op("act", lambda e, b=b, pi=pi: e.copy(out=PT[:, pi, :], in_=ps[b][:, :]), reads=[("ps", b)], writes=[("pt", pi)])
                tr.dma("sp", S["V"][c0 + tt * 128:c0 + (tt + 1) * 128, 512 * g:512 * (g + 1)], PT[:, pi, :], reads=[("pt", pi)],
                       writes=[("vd", g, hf, tt)])
        if cc:
            own_k = [("kd", h2, 1, n2) for h2 in range(32) for n2 in range(2)] + [("kaug", 1, i, n2) for i in range(3) for n2 in range(2)]
            self.allgather_rows(S["Ko"], S["Kr"], 536, own_k, "kr")
            self.allgather_rows(S["V"], S["Vr"], 512, [("vd", g2, 1, tt) for g2 in range(4) for tt in range(8)], "vr")
            tr.allgather(S["CS"], S["CSr"], reads=[("csd", 1, tt) for tt in range(8)], writes=["csr"])
        wout = L["w_out"]
        self.mem_block(L, wout)
        hb = HT[:, :, :].rearrange("p a t -> p (a t)")
        h32 = HT[:, :, :].bitcast(F32).rearrange("p a t -> p (a t)")
        Sk = (hf + 1) * T
        l32 = self.LAT[:, :, :].bitcast(F32).rearrange("p a t -> p (a t)")
        BKt = l32[:, 0:512].rearrange("p (j c) -> p j c", c=32)
        TOT = l32[:, 512:544]
        self.barrier([("lat", c2, n2) for c2 in range(4) for n2 in range(2)] + [("hd", "bk"), ("hd", "tot")])
        csk = [("csd", h2, tt) for h2 in range(hf + 1) for tt in range(8)]
        if cc:
            tr.dma("sp", BKt[:, 0:8, :], S["CSr"][0:T, :].rearrange("(j p) c -> p j c", p=128), reads=["csr"], writes=[("hd", "bk")])
            tr.dma("sp", BKt[:, 8:16, :], S["CS"].rearrange("(j p) c -> p j c", p=128), reads=[("csd", 1, tt) for tt in range(8)], writes=[("hd", "bk")])
            tr.dma("sp", TOT, S["CSr"][T - 1:T, :].partition_broadcast(128), reads=["csr"], writes=[("hd", "tot")])
        else:
            tr.dma("sp", BKt[:, 0:Sk // 128, :], S["CS"][0:Sk, :].rearrange("(j p) c -> p j c", p=128), reads=csk, writes=[("hd", "bk")])
            if hf == 1:
                tr.dma("sp", TOT, S["CS"][T - 1:T, :].partition_broadcast(128), reads=csk, writes=[("hd", "tot")])
        if hf == 1:
            tr.op("dve", lambda e: e.tensor_tensor(out=BKt[:, 0:8, :], in0=BKt[:, 0:8, :], in1=TOT.unsqueeze(1).broadcast_to([128, 8, 32]), op=ALU.subtract),
                  reads=[("hd", "bk"), ("hd", "tot")], writes=[("hd", "bk")])
        if cc:
            tr.op("dve", lambda e: e.tensor_scalar(out=BKt[:, 0:8, :], in0=BKt[:, 0:8, :], scalar1=self.PCOL[:, 1:2], scalar2=None, op0=ALU.add),
                  reads=[("hd", "bk"), "pcol"], writes=[("hd", "bk")])
        tr.op("dve", lambda e: e.tensor_scalar(out=BKt[:, 0:Sk // 128, :], in0=BKt[:, 0:Sk // 128, :], scalar1=-CSHIFT, scalar2=None, op0=ALU.add),
              reads=[("hd", "bk")], writes=[("hd", "bk")])
        tasks = []
        for pr in range(16):
            pst = pr % 2
            members = []
            for hh in range(2):
                h = 2 * pr + hh
                base = (2 * pst + hh) * 4096
                Qt = hb[0:67, base:base + T]
                Kt = hb[0:67, base + T:base + T + SEQ]
                Vt = hb[:, base + 3 * T:base + 4 * T].rearrange("p (j c) -> p j c", c=64)
                sk = 2 * pst + hh

                def pre(h=h, sk=sk, Qt=Qt, Kt=Kt, Vt=Vt):
                    tr.dma("sp", Qt, S["Q"][h], reads=[("qd", h, 0, n2) for n2 in range(2)] + [("qaug", i) for i in range(3)], writes=[("hd", sk, "q")])
                    if cc:
                        tr.dma("sp", Kt[:, 0:T], self.r0rows(S["Kr"], 536, 67 * h, 67), reads=[("kr", h // 8)], writes=[("hd", sk, "k")])
                        tr.dma("sp", Kt[:, T:2 * T], S["K"][h], reads=[("kd", h, 1, n2) for n2 in range(2)] +
                               [("kaug", 1, i, n2) for i in range(3) for n2 in range(2)], writes=[("hd", sk, "k")])
                        for i2 in range(2):
                            tr.dma("sp", Vt[:, 4 * i2:4 * i2 + 4, :],
                                   self.r0rows(S["Vr"], 512, 512 * i2, 512)[:, 64 * h:64 * (h + 1)].rearrange("(j p) c -> p j c", p=128),
                                   reads=[("vr", i2)], writes=[("hd", sk, "v")])
                        tr.dma("sp", Vt[:, 8:16, :], S["V"][:, 64 * h:64 * (h + 1)].rearrange("(j p) c -> p j c", p=128),
                               reads=[("vd", h // 8, 1, tt) for tt in range(8)], writes=[("hd", sk, "v")])
                        return
                    tr.dma("sp", Kt[:, 0:Sk], S["K"][h, :, 0:Sk], reads=[("kd", h, h2, n2) for h2 in range(hf + 1) for n2 in range(2)] +
                           [("kaug", h2, i, n2) for h2 in range(hf + 1) for i in range(3) for n2 in range(2)], writes=[("hd", sk, "k")])
                    tr.dma("sp", Vt[:, 0:Sk // 128, :], S["V"][0:Sk, 64 * h:64 * (h + 1)].rearrange("(j p) c -> p j c", p=128),
                           reads=[("vd", h // 8, h2, tt) for h2 in range(hf + 1) for tt in range(8)], writes=[("hd", sk, "v")])

                members.append(dict(q=[(Qt, ("hd", sk, "q"))], k=[(Kt, ("hd", sk, "k"))], v=Vt, vkey=("hd", sk, "v"), pre=pre, po=64 * hh,
                                    bias_fn=lambda j, h=h: (BKt[:, j, h:h + 1], ("hd", "bk"))))
            ab = (pr // 4) % 2
            m = pr % 4
            post = None
            if m == 3:
                def post(pr=pr, ab=ab):
                    g4 = pr // 4
                    self.outproj_group(wout[512 * g4:512 * (g4 + 1), :].rearrange("(kt p) d -> p kt d", p=128),
                                       [(self.AT[ab][:, mm, :], ("cat", 4 * ab + mm)) for mm in range(4)], 1.0)
            tasks += self.attn_tasks(hf, members, 64, 1.0,
                                     lambda n, ab=ab, m=m: self.AT[ab][:, m, n * 512:(n + 1) * 512],
                                     lambda n, ab=ab, m=m: ("cat", 4 * ab + m, n), post=post, clamp=True)
        self.run_pipeline(tasks)
        self.barrier(self.htkeys() + self.hdkeys() + [("hd", "bk"), ("hd", "tot")] + [("lat", c2, n2) for c2 in range(4) for n2 in range(2)])

    def barrier(self, keys):
        self.tr.op("dve", lambda e: e.memset(self.DUMMY[:, 0:1], 0.0), writes=list(keys))

    def hdkeys(self):
        return ([("hd", s2, nm) for s2 in range(4) for nm in ("qn", "qr", "kn", "v", "q", "k")] + [("hd", "kr"), ("hd", "bk"), ("hd", "tot")]
                + [("hd", "b", r) for r in range(-1, 4)])

    def htkeys(self):
        return [("ht", kc, n) for kc in range(KC) for n in range(2)]

    def finish(self):
        self.tr.emit(final_waits=self.finals)
        self.st.close()
        return self.nc


def gain_layout(g):
    g = np.asarray(g, dtype=np.float32)
    return np.ascontiguousarray(g.reshape(-1, 128).T)


def col_layout(g, n=128):
    return np.ascontiguousarray(np.asarray(g, dtype=np.float32).reshape(n, 1))


def const_tables():
    k = np.arange(128)[:, None]
    q = np.arange(512)[None, :]
    cm = np.stack([(r * 128 + k <= q).astype(np.float32) for r in range(4)], axis=1)
    bd = np.zeros((128, 128), np.float32)
    bd[:64, :64] = 1.0
    bd[64:, 64:] = 1.0
    rmt = np.zeros((64, 64), np.float32)
    for m in range(32):
        rmt[m + 32, m] = -1.0
    for m in range(32, 64):
        rmt[m - 32, m] = 1.0
    return np.ascontiguousarray(cm), bd, rmt


def rope_tables():
    half = 32
    inv = 10000.0 ** (-np.arange(half, dtype=np.float32) / half)
    ang = np.arange(SEQ, dtype=np.float32)[None, :] * inv[:, None].astype(np.float32)
    cos = np.cos(ang).astype(np.float32)
    sin = np.sin(ang).astype(np.float32)
    return np.ascontiguousarray(np.concatenate([cos, cos], 0)), np.ascontiguousarray(np.concatenate([sin, sin], 0))


def t5_bucket_onehot():
    d = np.arange(128)
    exact = 16
    log_b = exact + (np.log(np.maximum(d, 1) / exact) / np.log(128 / exact) * (32 - exact)).astype(np.int32)
    log_b = np.minimum(log_b, 31)
    bk = np.where(d < exact, d, log_b).astype(np.int32)
    oh = np.zeros((32, 128), np.float32)
    oh[bk, d] = 1.0
    return oh


def bc128(v):
    v = np.asarray(v, dtype=np.float32).reshape(1, -1)
    return np.ascontiguousarray(np.repeat(v, 128, axis=0))


MIX_IN = {0: 6656, 1: 1600, 2: 3072, 3: 6688}


def build_full(n_layers=4, halves=(0, 1), use_ffn=True, cc=False):
    p = Prog(cc=cc)
    if cc:
        halves = (0,)
    xT = p.din("xT", [D, T if cc else SEQ])
    yT = p.dout("yT", [D, T if cc else SEQ])
    if cc:
        pcols = p.din("pcols", [128, 3])
    cm = p.din("cm", [128, 4, 512])
    bd = p.din("bd64", [128, 128])
    rmt = p.din("rmt", [64, 64])
    memT = p.din("memT", [D, MEM])
    Ls = []
    for i in range(n_layers):
        L = {"memT": memT}
        def di(nm, shp, i=i, L=L):
            L[nm] = p.din("l%d_%s" % (i, nm), shp)
        for nm in ("norm_ffn1", "norm_mix", "norm_ffn2", "norm_mem"):
            di(nm, [128, KC])
        for nm in ("w1g", "w1u", "w2g", "w2u"):
            di(nm, [D, FF])
        for nm in ("w1d", "w2d"):
            di(nm, [FF, D])
        di("w_in", [D, MIX_IN[i % 4]])
        di("w_out", [2560, D])
        di("mem_w_kv", [D, 1024])
        di("mem_q_norm", [128, 1])
        di("mem_k_norm", [128, 1])
        k = i % 4
        if k == 0:
            for nm in ("conv_w0", "conv_w1", "conv_w2"):
                di(nm, [128, 16])
            if cc:
                L["S"] = {"ZHs": p.dscr("c_ZHs", [128, 32], F32), "ZHr": p.dscr("c_ZHr", [256, 32], F32)}
        elif k == 1 and cc:
            di("q_a_norm", [128, 4]); di("kv_a_norm", [128, 4]); di("qk_cols", [128, 4])
            di("w_q_b", [512, 3072]); di("w_kv_b", [512, 4096]); di("cos", [64, T]); di("sin", [64, T])
            kno = p.dscr("m_KNo", [2048, T])
            L["S"] = {"QN": p.dscr("m_QN", [16, 128, T]), "QR": p.dscr("m_QR", [16, 64, T]),
                      "KNo": kno, "KN": kno.rearrange("(h p) t -> h p t", p=128), "KNr": p.dscr("m_KNr", [4096, T]),
                      "KR": p.dscr("m_KR", [64, T]), "KRr": p.dscr("m_KRr", [128, T]),
                      "V": p.dscr("m_V", [T, 2048]), "Vr": p.dscr("m_Vr", [2 * T, 2048])}
        elif k == 2 and cc:
            di("qk_cols", [128, 2]); di("sinks_pair", [128, 16]); di("rel_bias", [32, 32]); di("onehot", [32, 128])
            gdh = p.nc.dram_tensor("s_GD", [32, 128, 1152], F32)
            ko = p.dscr("s_Ko", [256, T])
            L["S"] = {"Q": p.dscr("s_Q", [32, 64, T]), "Ko": ko, "K": ko.rearrange("(h p) t -> h p t", p=64), "V": p.dscr("s_V", [T, 256]),
                      "KHs": p.dscr("s_KHs", [256, 128]), "KHr": p.dscr("s_KHr", [512, 128]),
                      "VHs": p.dscr("s_VHs", [128, 256]), "VHr": p.dscr("s_VHr", [256, 256]),
                      "GD": gdh.ap(), "GDh": gdh}
        elif k == 3 and cc:
            di("qk_cols", [128, 2]); di("bf_bc", [128, 32]); di("ident", [128, 128]); di("triu", [128, 128])
            ko = p.dscr("f_Ko", [32 * 67, T])
            L["S"] = {"Q": p.dscr("f_Q", [32, 67, T]), "Ko": ko, "K": ko.rearrange("(h p) t -> h p t", p=67), "Kr": p.dscr("f_Kr", [2 * 32 * 67, T]),
                      "V": p.dscr("f_V", [T, 2048]), "Vr": p.dscr("f_Vr", [2 * T, 2048]),
                      "CS": p.dscr("f_CS", [T, 32], F32), "CSr": p.dscr("f_CSr", [2 * T, 32], F32)}
        elif k == 1:
            di("q_a_norm", [128, 4]); di("kv_a_norm", [128, 4]); di("qk_cols", [128, 4])
            di("w_q_b", [512, 3072]); di("w_kv_b", [512, 4096]); di("cos", [64, SEQ]); di("sin", [64, SEQ])
            L["S"] = {"QN": p.dscr("m_QN", [16, 128, T]), "QR": p.dscr("m_QR", [16, 64, T]), "KN": p.dscr("m_KN", [16, 128, SEQ]),
                      "KR": p.dscr("m_KR", [64, SEQ]), "V": p.dscr("m_V", [SEQ, 2048])}
        elif k == 2:
            di("qk_cols", [128, 2]); di("sinks_pair", [128, 16]); di("rel_bias", [32, 32]); di("onehot", [32, 128])
            gdh = p.nc.dram_tensor("s_GD", [32, 128, 1152], F32)
            L["S"] = {"Q": p.dscr("s_Q", [32, 64, T]), "K": p.dscr("s_K", [4, 64, SEQ]), "V": p.dscr("s_V", [SEQ, 256]),
                      "GD": gdh.ap(), "GDh": gdh}
        else:
            di("qk_cols", [128, 2]); di("bf_bc", [128, 32]); di("ident", [128, 128]); di("triu", [128, 128])
            L["S"] = {"Q": p.dscr("f_Q", [32, 67, T]), "K": p.dscr("f_K", [32, 67, SEQ]), "V": p.dscr("f_V", [SEQ, 2048]),
                      "CS": p.dscr("f_CS", [SEQ, 32], F32)}
        Ls.append(L)
    p.load_consts(cm, bd, rmt)
    if cc:
        p.load_pcols(pcols)
    for hf in halves:
        p.load_x(xT, hf * T)
        for i in range(n_layers):
            L = Ls[i]
            if use_ffn:
                p.ffn(L["norm_ffn1"], L["w1g"], L["w1u"], L["w1d"])
            k = i % 4
            if k == 0:
                p.mixer_conv(hf, L, L.get("S"))
            elif k == 1:
                p.mixer_mla(hf, L, L["S"])
            elif k == 2:
                p.mixer_swa(hf, L, L["S"])
            else:
                p.mixer_fox(hf, L, L["S"])
            if use_ffn:
                p.ffn(L["norm_ffn2"], L["w2g"], L["w2u"], L["w2d"])
        p.store_x(yT, hf * T)
    return p.finish()


def host_inputs(inputs, n_layers=4, cc=False):
    f = lambda a: np.ascontiguousarray(np.asarray(a, dtype=np.float32))
    cmv, bdv, rmtv = const_tables()
    cosv, sinv = rope_tables()
    shared = {"cm": cmv, "bd64": bdv, "rmt": rmtv}
    for i in range(n_layers):
        pre = "l%d_" % i
        k, occ = i % 4, i // 4
        for nm in ("norm_ffn1", "norm_mix", "norm_ffn2", "norm_mem"):
            shared[pre + nm] = gain_layout(inputs[nm][i])
        shared[pre + "w1g"] = f(inputs["ffn1_w_gate"][i]); shared[pre + "w1u"] = f(inputs["ffn1_w_up"][i]); shared[pre + "w1d"] = f(inputs["ffn1_w_down"][i])
        shared[pre + "w2g"] = f(inputs["ffn2_w_gate"][i]); shared[pre + "w2u"] = f(inputs["ffn2_w_up"][i]); shared[pre + "w2d"] = f(inputs["ffn2_w_down"][i])
        shared[pre + "mem_w_kv"] = f(inputs["mem_w_kv"][i])
        shared[pre + "mem_q_norm"] = col_layout(inputs["mem_q_norm"][i])
        shared[pre + "mem_k_norm"] = col_layout(inputs["mem_k_norm"][i])
        if k == 0:
            shared[pre + "w_in"] = f(inputs["conv_w_in"][occ]); shared[pre + "w_out"] = f(inputs["conv_w_out"][occ])
            for t in range(3):
                shared[pre + "conv_w%d" % t] = gain_layout(np.asarray(inputs["conv_w"][occ])[t])
        elif k == 1:
            shared[pre + "w_in"] = f(inputs["mla_w_in"][occ]); shared[pre + "w_out"] = f(inputs["mla_w_out"][occ])
            shared[pre + "q_a_norm"] = gain_layout(inputs["mla_q_a_norm"][occ]); shared[pre + "kv_a_norm"] = gain_layout(inputs["mla_kv_a_norm"][occ])
            qn = np.asarray(inputs["mla_q_norm"][occ], dtype=np.float32); kn = np.asarray(inputs["mla_k_norm"][occ], dtype=np.float32)
            qk = np.zeros((128, 4), np.float32)
            qk[:, 0] = qn[:128]; qk[:64, 1] = qn[128:]; qk[:, 2] = kn[:128]; qk[:64, 3] = kn[128:]
            shared[pre + "qk_cols"] = qk
            shared[pre + "w_q_b"] = f(inputs["mla_w_q_b"][occ]); shared[pre + "w_kv_b"] = f(inputs["mla_w_kv_b"][occ])
            shared[pre + "cos"] = cosv; shared[pre + "sin"] = sinv
        elif k == 2:
            shared[pre + "w_in"] = f(inputs["swa_w_in"][occ]); shared[pre + "w_out"] = f(inputs["swa_w_out"][occ])
            qk = np.zeros((128, 2), np.float32)
            qk[:, 0] = np.tile(np.asarray(inputs["swa_q_norm"][occ], dtype=np.float32), 2)
            qk[:, 1] = np.tile(np.asarray(inputs["swa_k_norm"][occ], dtype=np.float32), 2)
            shared[pre + "qk_cols"] = qk
            sk_ = np.asarray(inputs["swa_sinks"][occ], dtype=np.float32)
            shared[pre + "sinks_pair"] = np.ascontiguousarray(np.repeat(sk_.reshape(16, 2).T, 64, axis=0))
            shared[pre + "rel_bias"] = f(inputs["rel_bias"]); shared[pre + "onehot"] = t5_bucket_onehot()
        else:
            shared[pre + "w_in"] = f(inputs["fox_w_in"][occ]); shared[pre + "w_out"] = f(inputs["fox_w_out"][occ])
            qk = np.zeros((128, 2), np.float32)
            qk[:, 0] = np.tile(np.asarray(inputs["fox_q_norm"][occ], dtype=np.float32), 2)
            qk[:, 1] = np.tile(np.asarray(inputs["fox_k_norm"][occ], dtype=np.float32), 2)
            shared[pre + "qk_cols"] = qk
            shared[pre + "bf_bc"] = bc128(inputs["fox_b_f"][occ])
            shared[pre + "ident"] = np.eye(128, dtype=np.float32); shared[pre + "triu"] = np.triu(np.ones((128, 128), np.float32))
    x = np.asarray(inputs["x"], dtype=np.float32)
    mem = np.asarray(inputs["mem"], dtype=np.float32)
    maps = []
    if not cc:
        for b in range(x.shape[0]):
            m = dict(shared)
            m["xT"] = np.ascontiguousarray(x[b].T)
            m["memT"] = np.ascontiguousarray(mem[b].T)
            maps.append(m)
        return maps
    for c in range(NCORES):
        b, hf = c // 2, c % 2
        m = dict(shared)
        m["xT"] = np.ascontiguousarray(x[b, hf * T:(hf + 1) * T].T)
        m["memT"] = np.ascontiguousarray(mem[b].T)
        pc = np.zeros((128, 3), np.float32)
        pc[:, 0] = -CSHIFT if hf == 1 else NEG
        pc[:, 1] = 0.0 if hf == 1 else NEG
        pc[:, 2] = 1.0 if hf == 1 else 0.0
        m["pcols"] = pc
        for i in range(n_layers):
            if i % 4 == 1:
                m["l%d_cos" % i] = np.ascontiguousarray(cosv[:, hf * T:(hf + 1) * T])
                m["l%d_sin" % i] = np.ascontiguousarray(sinv[:, hf * T:(hf + 1) * T])
        maps.append(m)
    return maps


def kernel(**inputs):
    nc = build_full(cc=True)
    maps = host_inputs(inputs, cc=True)
    res = run_bass_kernel_spmd(nc, maps, core_ids=list(range(NCORES)))
    out = np.zeros((4, SEQ, D), np.float32)
    for c, r in enumerate(res.results):
        out[c // 2, (c % 2) * T:(c % 2 + 1) * T] = r["yT"].T
    return out
```
